# Optimizing a Trainium2 kernel written in Bass

```python
import jax
import jax.numpy as jnp
from jax import lax
import numpy as np

D_MODEL = 1024
BATCH = 2
SEQ = 8192
DEPTH = 2

HEAD_DIM = 64
H_NSA = 4
H_DIL = 6
H_SB = 6
DIL_PAIRS = ((128, 1), (512, 4), (2048, 16))
N_DIL_GROUPS = len(DIL_PAIRS)
H_PER_DIL = H_DIL // N_DIL_GROUPS
CMP_LEN = 32
CMP_STRIDE = 16
CMP_HIDDEN = 128
SEL_BLOCK = 64
SEL_TOPK = 16
WIN_NSA = 512
Q_BLOCK = 128
D_FF = 4 * D_MODEL
N_SOFTMAX_HEADS = H_NSA + H_DIL
RMS_EPS = 1e-6
NEG_INF = -1e30
FORCE_BONUS = 1e4

NSA_Q_W = H_NSA * HEAD_DIM
NSA_KV_W = 6 * HEAD_DIM
NSA_GATE_W = 3 * H_NSA
DIL_W = 3 * H_DIL * HEAD_DIM
SB_W = 3 * H_SB * HEAD_DIM
D_PROJ = NSA_Q_W + NSA_KV_W + NSA_GATE_W + DIL_W + SB_W
PROJ_SPLITS = (NSA_Q_W, NSA_Q_W + NSA_KV_W, NSA_Q_W + NSA_KV_W + NSA_GATE_W,
               NSA_Q_W + NSA_KV_W + NSA_GATE_W + DIL_W)
D_CAT = (H_NSA + H_PER_DIL + H_SB) * HEAD_DIM

kernel_name = 'hybrid_nsa_dilated_stickbreak_block'


def rms_norm(x, g):
    xf = x.astype(jnp.float32)
    y = xf * lax.rsqrt(jnp.mean(xf * xf, axis=-1, keepdims=True) + RMS_EPS)
    return (y * g.astype(jnp.float32)).astype(x.dtype)


def alibi_slopes():
    i = jnp.arange(1, N_SOFTMAX_HEADS + 1, dtype=jnp.float32)
    return jnp.exp2(-8.0 * i / N_SOFTMAX_HEADS)


def masked_softmax(s, mask):
    s = jnp.where(mask, s, NEG_INF)
    m = jnp.max(s, axis=-1, keepdims=True)
    p = jnp.where(mask, jnp.exp(s - m), 0.0)
    l = jnp.maximum(jnp.sum(p, axis=-1, keepdims=True), 1e-30)
    return p / l, (m + jnp.log(l))[..., 0]


def unblock(y):
    y = jnp.moveaxis(y, 0, 1)
    return y.reshape((y.shape[0], y.shape[1] * y.shape[2]) + y.shape[3:])


def nsa_mixer(q, kv, gate_logits, qk_gain, cmp_pe, cmp_w1, cmp_w2, slopes):
    B, S = q.shape[0], q.shape[1]
    dt = q.dtype
    scale = HEAD_DIM ** -0.5
    q = rms_norm(q, qk_gain[0])
    k_sel = rms_norm(kv[:, :, 2], qk_gain[2])
    v_sel = kv[:, :, 3]
    k_win = rms_norm(kv[:, :, 4], qk_gain[3])
    v_win = kv[:, :, 5]
    n_cmp = (S - CMP_LEN) // CMP_STRIDE + 1
    cmp_start = CMP_STRIDE * jnp.arange(n_cmp)
    cmp_end = cmp_start + CMP_LEN - 1
    cmp_idx = cmp_start[:, None] + jnp.arange(CMP_LEN)[None]
    raw = kv[:, :, 0:2].transpose(0, 2, 1, 3)[:, :, cmp_idx]
    raw = (raw + cmp_pe[None, :, None]).reshape(B, 2, n_cmp, CMP_LEN * HEAD_DIM)
    hid = jax.nn.gelu(jnp.einsum('bcnf,cfh->bcnh', raw, cmp_w1))
    kv_c = jnp.einsum('bcnh,chd->bcnd', hid, cmp_w2)
    k_c = rms_norm(kv_c[:, 0], qk_gain[1])
    v_c = kv_c[:, 1]
    n_sel = S // SEL_BLOCK
    sel_start = SEL_BLOCK * jnp.arange(n_sel)
    overlap = ((cmp_start[:, None] <= sel_start[None] + SEL_BLOCK - 1)
               & (cmp_end[:, None] >= sel_start[None])).astype(jnp.float32)
    k_top = min(SEL_TOPK, n_sel)
    sel_offsets = jnp.arange(SEL_BLOCK)
    sel_ids = jnp.arange(n_sel)
    k_win_p = jnp.pad(k_win, ((0, 0), (WIN_NSA, 0), (0, 0)))
    v_win_p = jnp.pad(v_win, ((0, 0), (WIN_NSA, 0), (0, 0)))
    win_offsets = jnp.arange(WIN_NSA + Q_BLOCK)
    gates = jax.nn.sigmoid(gate_logits.astype(jnp.float32)).astype(dt)
    gather_rows = jax.vmap(lambda a, i: a[i])

    def block(n):
        t0 = n * Q_BLOCK
        t = t0 + jnp.arange(Q_BLOCK)
        qb = lax.dynamic_slice_in_dim(q, t0, Q_BLOCK, axis=1)
        d_c = (t[:, None] - cmp_end[None]).astype(jnp.float32)
        s = jnp.einsum('bqhd,bnd->bhqn', qb, k_c).astype(jnp.float32) * scale - slopes[:, None, None] * d_c
        p_c, _ = masked_softmax(s, d_c >= 0)
        o_c = jnp.einsum('bhqn,bnd->bqhd', p_c.astype(dt), v_c)
        imp = jnp.einsum('bhqn,nj->bqj', p_c, overlap)
        cur = t // SEL_BLOCK
        forced = (sel_ids[None] == 0) | (sel_ids[None] == cur[:, None]) | (sel_ids[None] == cur[:, None] - 1)
        imp = jnp.where(forced, imp + FORCE_BONUS, imp)
        imp = jnp.where(sel_start[None] <= t[:, None], imp, NEG_INF)
        _, top = lax.top_k(imp, k_top)
        tok = (top[..., None] * SEL_BLOCK + sel_offsets).reshape(B, Q_BLOCK, k_top * SEL_BLOCK)
        k_g = gather_rows(k_sel, tok)
        v_g = gather_rows(v_sel, tok)
        d_s = (t[None, :, None] - tok).astype(jnp.float32)
        s = jnp.einsum('bqhd,bqkd->bhqk', qb, k_g).astype(jnp.float32) * scale - slopes[None, :, None, None] * d_s[:, None]
        p_s, _ = masked_softmax(s, (d_s >= 0)[:, None])
        o_s = jnp.einsum('bhqk,bqkd->bqhd', p_s.astype(dt), v_g)
        k_w = lax.dynamic_slice_in_dim(k_win_p, t0, WIN_NSA + Q_BLOCK, axis=1)
        v_w = lax.dynamic_slice_in_dim(v_win_p, t0, WIN_NSA + Q_BLOCK, axis=1)
        kpos = t0 - WIN_NSA + win_offsets
        d_w = (t[:, None] - kpos[None]).astype(jnp.float32)
        mask_w = (d_w >= 0) & (d_w < WIN_NSA) & (kpos[None] >= 0)
        s = jnp.einsum('bqhd,bkd->bhqk', qb, k_w).astype(jnp.float32) * scale - slopes[:, None, None] * d_w
        p_w, _ = masked_softmax(s, mask_w)
        o_w = jnp.einsum('bhqk,bkd->bqhd', p_w.astype(dt), v_w)
        g = lax.dynamic_slice_in_dim(gates, t0, Q_BLOCK, axis=1)
        return g[..., 0:1] * o_c + g[..., 1:2] * o_s + g[..., 2:3] * o_w

    return unblock(lax.map(block, jnp.arange(S // Q_BLOCK)))


def dilated_mixer(qkv, qk_gain, slopes):
    S = qkv.shape[1]
    dt = qkv.dtype
    scale = HEAD_DIM ** -0.5
    q = rms_norm(qkv[:, :, 0], qk_gain[0])
    k = rms_norm(qkv[:, :, 1], qk_gain[1])
    v = qkv[:, :, 2]
    k_groups = [k[:, :, g] for g in range(N_DIL_GROUPS)]
    v_groups = [v[:, :, g] for g in range(N_DIL_GROUPS)]
    slopes_g = slopes.reshape(N_DIL_GROUPS, H_PER_DIL)

    def block(n):
        t0 = n * Q_BLOCK
        t = t0 + jnp.arange(Q_BLOCK)
        qb = lax.dynamic_slice_in_dim(q, t0, Q_BLOCK, axis=1)
        outs, lses = [], []
        for g, (w, r) in enumerate(DIL_PAIRS):
            dist = r * jnp.arange(w // r + 1)
            kidx = t[:, None] - dist[None]
            valid = kidx >= 0
            kidx = jnp.maximum(kidx, 0)
            k_g = k_groups[g][:, kidx]
            v_g = v_groups[g][:, kidx]
            s = (jnp.einsum('bqhd,bqjhd->bhqj', qb[:, :, g], k_g).astype(jnp.float32) * scale
                 - slopes_g[g][:, None, None] * dist.astype(jnp.float32))
            p, lse = masked_softmax(s, valid)
            outs.append(jnp.einsum('bhqj,bqjhd->bqhd', p.astype(dt), v_g))
            lses.append(lse)
        alpha = jax.nn.softmax(jnp.stack(lses), axis=0)
        alpha = jnp.transpose(alpha, (0, 1, 3, 2))[..., None].astype(dt)
        return jnp.sum(alpha * jnp.stack(outs), axis=0)

    return unblock(lax.map(block, jnp.arange(S // Q_BLOCK)))


def stick_breaking_mixer(qkv):
    S = qkv.shape[1]
    dt = qkv.dtype
    scale = HEAD_DIM ** -0.5
    q, k, v = qkv[:, :, 0], qkv[:, :, 1], qkv[:, :, 2]
    kpos = jnp.arange(S)

    def block(n):
        t0 = n * Q_BLOCK
        t = t0 + jnp.arange(Q_BLOCK)
        qb = lax.dynamic_slice_in_dim(q, t0, Q_BLOCK, axis=1)
        z = jnp.einsum('bqhd,bshd->bhqs', qb, k).astype(jnp.float32) * scale
        causal = kpos[None] < t[:, None]
        log_beta = jax.nn.log_sigmoid(z)
        log_fail = jnp.where(causal, jax.nn.log_sigmoid(-z), 0.0)
        after = lax.cumsum(log_fail, axis=3, reverse=True) - log_fail
        a = jnp.where(causal, jnp.exp(log_beta + after), 0.0)
        return jnp.einsum('bhqs,bshd->bqhd', a.astype(dt), v)

    return unblock(lax.map(block, jnp.arange(S // Q_BLOCK)))


def setup_inputs(seed: int = 0) -> dict:
    key = jax.random.key(seed)
    ks = jax.random.split(key, 12)
    f32 = jnp.float32

    def nrm(k, shape, fan_in):
        return jax.random.normal(k, shape, f32) * (fan_in ** -0.5)

    def gain(k, shape):
        return 1.0 + 0.02 * jax.random.normal(k, shape, f32)

    return {
        'x': jax.random.normal(ks[0], (BATCH, SEQ, D_MODEL), f32),
        'norm_mix': gain(ks[1], (DEPTH, D_MODEL)),
        'norm_mlp': gain(ks[2], (DEPTH, D_MODEL)),
        'w_in': nrm(ks[3], (DEPTH, D_MODEL, D_PROJ), D_MODEL),
        'qk_gain_nsa': gain(ks[4], (DEPTH, 4, HEAD_DIM)),
        'qk_gain_dil': gain(ks[5], (DEPTH, 2, HEAD_DIM)),
        'cmp_pe': 0.1 * jax.random.normal(ks[6], (DEPTH, 2, CMP_LEN, HEAD_DIM), f32),
        'cmp_w1': nrm(ks[7], (DEPTH, 2, CMP_LEN * HEAD_DIM, CMP_HIDDEN), CMP_LEN * HEAD_DIM),
        'cmp_w2': nrm(ks[8], (DEPTH, 2, CMP_HIDDEN, HEAD_DIM), CMP_HIDDEN),
        'w_out': nrm(ks[9], (DEPTH, D_CAT, D_MODEL), D_CAT),
        'w_up': nrm(ks[10], (DEPTH, D_MODEL, D_FF), D_MODEL),
        'w_down': nrm(ks[11], (DEPTH, D_FF, D_MODEL), D_FF),
    }


def reference(x, norm_mix, norm_mlp, w_in, qk_gain_nsa, qk_gain_dil, cmp_pe, cmp_w1, cmp_w2,
              w_out, w_up, w_down):
    B, S, _ = x.shape
    slopes = alibi_slopes()
    slopes_dil = slopes[:H_DIL]
    slopes_nsa = slopes[H_DIL:]
    for l in range(DEPTH):
        h = rms_norm(x, norm_mix[l])
        proj = h @ w_in[l]
        a_q, a_kv, a_g, b_qkv, c_qkv = jnp.split(proj, PROJ_SPLITS, axis=-1)
        o_a = nsa_mixer(a_q.reshape(B, S, H_NSA, HEAD_DIM), a_kv.reshape(B, S, 6, HEAD_DIM),
                        a_g.reshape(B, S, H_NSA, 3), qk_gain_nsa[l], cmp_pe[l], cmp_w1[l],
                        cmp_w2[l], slopes_nsa)
        o_b = dilated_mixer(b_qkv.reshape(B, S, 3, N_DIL_GROUPS, H_PER_DIL, HEAD_DIM),
                            qk_gain_dil[l], slopes_dil)
        o_c = stick_breaking_mixer(c_qkv.reshape(B, S, 3, H_SB, HEAD_DIM))
        cat = jnp.concatenate([o_a.reshape(B, S, -1), o_b.reshape(B, S, -1),
                               o_c.reshape(B, S, -1)], axis=-1)
        x = x + cat @ w_out[l]
        h = rms_norm(x, norm_mlp[l])
        x = x + jnp.square(jax.nn.relu(h @ w_up[l])) @ w_down[l]
    return x
```

```python
from contextlib import ExitStack
import numpy as np
import ml_dtypes
import concourse.bass as bass
import concourse.mybir as mybir
from concourse.bass_utils import run_bass_kernel_spmd

F32 = mybir.dt.float32
BF16 = mybir.dt.bfloat16
AF = mybir.ActivationFunctionType
ALU = mybir.AluOpType
NPBF = ml_dtypes.bfloat16

SAME_ENGINE_SYNC = True
DBG = set()


class Buf:
    __slots__ = ("name", "w", "r", "semkey", "semv")

    def __init__(self, name):
        self.name = name
        self.w = None
        self.r = {}
        self.semkey = None
        self.semv = 0


class KB:
    def __init__(self):
        self.nc = bass.Bass("TRN2", target_bir_lowering=False)
        nc = self.nc
        self.es = ExitStack()
        self.engs = {"pe": nc.tensor, "act": nc.scalar, "dve": nc.vector,
                     "pool": nc.gpsimd, "sp": nc.sync}
        self.semobj = {}
        self.cnt = {}
        self.seen = {}
        for n in self.engs:
            self.semobj[n] = self.es.enter_context(nc.semaphore("s_" + n))
            self.cnt[n] = 0
            self.seen[n] = {}
        self.nbuf = 0
        self.ninstr = 0
        self.nwait = 0
        self.root_es = self.es
        self.dmasems = {}
        self.bank = [self.es.enter_context(nc.psum_tensor("bank%d" % i, [128, 512], F32)) for i in range(7)]
        self.bank_b = [Buf("bank%d" % i) for i in range(7)]
        self.bankh = self.es.enter_context(nc.psum_tensor("bankh", [128, 1024], BF16))
        self.bankh_b = Buf("bankh")

    def push_scope(self):
        self._outer = getattr(self, "_outer", [])
        self._outer.append(self.es)
        self.es = ExitStack()

    def pop_scope(self):
        self.barrier()
        self.es.close()
        self.es = self._outer.pop()

    def barrier(self):
        deps = {n: self.cnt[n] for n in self.engs if self.cnt[n] > 0}
        deps.update(self.dmasems)
        for n in self.engs:
            self._wait(n, dict(deps))

    def sbt(self, name, shape, dt):
        return self.sb(name, shape, dt), self.buf(name)

    def sb(self, name, shape, dt):
        self.nsb = getattr(self, "nsb", 0) + 1
        return self.es.enter_context(self.nc.sbuf_tensor("%s_%d" % (name, self.nsb), list(shape), dt))

    def ps(self, name, shape, dt=F32):
        return self.es.enter_context(self.nc.psum_tensor(name, list(shape), dt))

    def dram(self, name, shape, dt, kind):
        return self.nc.dram_tensor(name, list(shape), dt, kind=kind).ap()

    def dram_bf16(self, name, shape, kind):
        shp = list(shape)
        assert shp[-1] % 2 == 0
        shp[-1] //= 2
        return self.nc.dram_tensor(name, shp, F32, kind=kind).ap().bitcast(BF16)

    def buf(self, name=None):
        self.nbuf += 1
        return Buf(name or f"b{self.nbuf}")

    def _deps(self, reads, writes):
        deps = {}
        for b in reads:
            if b.w is not None:
                k, v = b.w
                if deps.get(k, 0) < v:
                    deps[k] = v
        for b in writes:
            if b.w is not None:
                k, v = b.w
                if deps.get(k, 0) < v:
                    deps[k] = v
            for k, v in b.r.items():
                if deps.get(k, 0) < v:
                    deps[k] = v
        return deps

    def _wait(self, eng, deps):
        seen = self.seen[eng]
        e = self.engs[eng]
        for k, v in deps.items():
            if k == eng and (eng in ("pe", "sp") or not SAME_ENGINE_SYNC):
                continue
            if seen.get(k, 0) >= v:
                continue
            e.wait_ge(self.semobj[k], v)
            self.nwait += 1
            seen[k] = v

    def op(self, eng, fn, reads=(), writes=()):
        self._wait(eng, self._deps(reads, writes))
        ins = fn(self.engs[eng])
        self.cnt[eng] += 1
        tok = (eng, self.cnt[eng])
        ins.then_inc(self.semobj[eng], 1)
        self.ninstr += 1
        for b in reads:
            if b.r.get(eng, 0) < tok[1]:
                b.r[eng] = tok[1]
        for b in writes:
            b.w = tok
            b.r = {}
        return ins

    def dma(self, q, out, in_, reads=(), writes=(), fn=None, **kw):
        self._wait(q, self._deps(reads, writes))
        wb = writes[0]
        if wb.semkey is None:
            wb.semkey = "d%d_%s" % (self.nbuf, wb.name)
            self.nbuf += 1
            self.semobj[wb.semkey] = self.root_es.enter_context(self.nc.semaphore(wb.semkey))
        if fn is not None:
            ins = fn(self.engs[q])
            ins.then_inc(self.semobj[wb.semkey])
            wb.semv += 1
        else:
            ins = self.engs[q].dma_start(out=out, in_=in_, **kw)
            ins.then_inc(self.semobj[wb.semkey], 16)
            wb.semv += 16
        tok = (wb.semkey, wb.semv)
        self.dmasems[wb.semkey] = wb.semv
        self.ninstr += 1
        for b in reads:
            if b.r.get(tok[0], 0) < tok[1]:
                b.r[tok[0]] = tok[1]
        for b in writes:
            b.w = tok
            b.r = {}
        return ins

    def finish(self, outs):
        deps = {}
        for b in outs:
            if b.w is not None:
                deps[b.w[0]] = max(deps.get(b.w[0], 0), b.w[1])
        self._wait("sp", deps)

    def close(self):
        self.es.close()


D_MODEL = 1024
BATCH = 2
SEQ = 8192
DEPTH = 2
HD = 64
NCORE = 8
RANKS = 4
NT = 16
TOK = NT * 128
NG = 4
DC = D_MODEL // 128
D_PROJ = 2956
NFM = 17
NTM = 908
WP = NFM * 128 + NTM
D_FF = 4096
D_CAT = 768
RMS_EPS = 1e-6
NORM_BLOCKS = (0, 1, 2, 3, 5, 6, 7, 8, 9, 10)
Q_BLOCKS = (0, 1, 5, 6, 7, 11, 12, 13)
K_BLOCKS = (2, 3, 4, 8, 9, 10, 14, 15, 16)
NQB = len(Q_BLOCKS)
NKB = len(K_BLOCKS)
NVH = 14


def win_column_perm():
    cols = []
    A_KV = 256
    B0 = 652
    C0 = 1804
    cols += list(range(0, 128))
    cols += list(range(128, 256))
    ksel = list(range(A_KV + 128, A_KV + 192))
    kwin = list(range(A_KV + 256, A_KV + 320))
    cols += ksel + ksel
    cols += kwin + kwin
    cols += list(range(A_KV, A_KV + 128))
    for g in range(3):
        cols += list(range(B0 + g * 128, B0 + g * 128 + 128))
    for g in range(3):
        cols += list(range(B0 + (3 + g) * 128, B0 + (3 + g) * 128 + 128))
    for p in range(3):
        cols += list(range(C0 + p * 128, C0 + p * 128 + 128))
    for p in range(3):
        cols += list(range(C0 + 384 + p * 128, C0 + 384 + p * 128 + 128))
    assert len(cols) == NFM * 128
    cols += list(range(A_KV + 192, A_KV + 256))
    cols += list(range(A_KV + 320, A_KV + 384))
    cols += list(range(B0 + 768, B0 + 1152))
    cols += list(range(C0 + 768, C0 + 1152))
    cols += list(range(640, 652))
    assert len(cols) == WP
    return np.array(cols, dtype=np.int64)


def emit_phase_A(kb, x_src, xb_fn, io):
    nc = kb.nc
    kb.push_scope()
    W = kb.sb("A_W", [128, DC, WP], BF16)
    W_b = kb.buf("A_W")
    hT = kb.sb("A_hT", [128, DC, TOK], BF16)
    hT_b = [kb.buf("A_hT%d" % g) for g in range(NG)]
    gmix = kb.sb("A_gmix", [128, DC], F32)
    gfm = kb.sb("A_gfm", [128, NFM], F32)
    ident = kb.sb("A_ident", [128, 128], F32)
    blk = kb.sb("A_blk", [128, 128], BF16)
    cst_b = kb.buf("A_cst")
    blk_b = kb.buf("A_blk")
    xt = [kb.sb("A_xt%d" % i, [128, D_MODEL], F32) for i in range(2)]
    xt_b = [kb.buf("A_xt%d" % i) for i in range(2)]
    junk = kb.sb("A_junk", [128, D_MODEL], F32)
    junk_b = kb.buf("A_junk")
    xn = [kb.sb("A_xn%d" % i, [128, D_MODEL], F32) for i in range(2)]
    xn_b = [kb.buf("A_xn%d" % i) for i in range(2)]
    st = [kb.sb("A_st%d" % i, [128, 4], F32) for i in range(2)]
    st_b = [kb.buf("A_st%d" % i) for i in range(2)]
    NPS = 6
    ps = kb.bank[:NPS]
    ps_b = kb.bank_b[:NPS]
    sq = [kb.sb("A_sq%d" % i, [128, 512], BF16) for i in range(2)]
    sq_b = [kb.buf("A_sq%d" % i) for i in range(2)]
    lnb = [kb.sb("A_ln%d" % i, [128, 512], F32) for i in range(2)]
    lnb_b = [kb.buf("A_ln%d" % i) for i in range(2)]
    fo = [kb.sb("A_fo%d" % i, [128, 512], BF16) for i in range(3)]
    fo_b = [kb.buf("A_fo%d" % i) for i in range(3)]
    vo = [kb.sb("A_vo%d" % i, [128, 1024], BF16) for i in range(2)]
    vo_b = [kb.buf("A_vo%d" % i) for i in range(2)]
    go = kb.sb("A_go", [128, NT * 12], F32)
    go_b = kb.buf("A_go")
    epsb = kb.sb("A_eps", [128, 1], F32)
    eps_ap = epsb[:, 0:1]

    kb.op("pool", lambda e: e.memset(epsb[:], RMS_EPS), writes=[cst_b])
    for i in range(2):
        kb.op("pool", lambda e: e.memset(vo[i][:, 896:1024], 0.0), writes=[vo_b[i]])
    if io.get("zero_fill"):
        zt = kb.sb("A_zero", [128, 2048], BF16)
        zt_b = kb.buf("A_zero")
        kb.op("pool", lambda e: e.memset(zt[:], 0.0), writes=[zt_b])
        for (zap, zb) in io["zero_fill"]:
            kb.dma("sp", zap, zt[:], reads=[zt_b], writes=[zb])
    kb.dma("sp", gmix[:], io["gmix"], writes=[cst_b])
    kb.dma("sp", gfm[:], io["gfm"], writes=[cst_b])
    kb.dma("sp", ident[:], io["ident"], writes=[cst_b])
    kb.op("pool", lambda e: e.memset(blk[:], 0.0), writes=[blk_b])
    kb.op("pool", lambda e: e.memset(blk[0:64, 0:64], 1.0), writes=[blk_b])
    kb.op("pool", lambda e: e.memset(blk[64:128, 64:128], 1.0), writes=[blk_b])
    half = WP // 2
    for c in range(DC):
        for h0 in (0, half):
            kb.dma("pool", W[:, c, h0:h0 + half], io["winp"][c * 128:(c + 1) * 128, h0:h0 + half],
                   writes=[W_b])

    psi = [0]

    def next_ps():
        i = psi[0] % NPS
        psi[0] += 1
        return ps[i], ps_b[i]

    cnt = {"sq": 0, "fo": 0, "vo": 0}
    for j in range(NT):
        s = j % 2
        src_ap, src_b = x_src(j)
        kb.dma("pool", xt[s][:], src_ap, reads=[src_b] if src_b else [], writes=[xt_b[s]])
        kb.op("act", lambda e: e.activation(out=junk[:], in_=xt[s][:], func=AF.Square,
                                            accum_out=st[s][:, 0:1]),
              reads=[xt_b[s]], writes=[junk_b, st_b[s]])
        kb.op("dve", lambda e: e.tensor_scalar(out=st[s][:, 1:2], in0=st[s][:, 0:1],
                                               scalar1=1.0 / D_MODEL, scalar2=RMS_EPS,
                                               op0=ALU.mult, op1=ALU.add),
              reads=[st_b[s]], writes=[st_b[s]])
        kb.op("act", lambda e: e.activation(out=st[s][:, 2:3], in_=st[s][:, 1:2], func=AF.Sqrt),
              reads=[st_b[s]], writes=[st_b[s]])
        kb.op("dve", lambda e: e.reciprocal(out=st[s][:, 3:4], in_=st[s][:, 2:3]),
              reads=[st_b[s]], writes=[st_b[s]])
        kb.op("dve", lambda e: e.tensor_scalar(out=xn[s][:], in0=xt[s][:], scalar1=st[s][:, 3:4],
                                               scalar2=None, op0=ALU.mult),
              reads=[xt_b[s], st_b[s]], writes=[xn_b[s]])
        g = j // 4
        for hf in range(2):
            p, pb = next_ps()
            for cc in range(4):
                c = hf * 4 + cc
                kb.op("pe", lambda e: e.transpose(out=p[:, cc * 128:(cc + 1) * 128],
                                                  in_=xn[s][:, c * 128:(c + 1) * 128],
                                                  identity=ident[:]),
                      reads=[xn_b[s], cst_b], writes=[pb])
            for cc in range(4):
                c = hf * 4 + cc
                eng = "act" if cc % 2 == 0 else "dve"
                if eng == "act":
                    kb.op("act", lambda e: e.activation(out=hT[:, c, j * 128:(j + 1) * 128],
                                                        in_=p[:, cc * 128:(cc + 1) * 128],
                                                        func=AF.Copy, scale=gmix[:, c:c + 1]),
                          reads=[pb, cst_b], writes=[hT_b[g]])
                else:
                    kb.op("dve", lambda e: e.tensor_scalar(out=hT[:, c, j * 128:(j + 1) * 128],
                                                           in0=p[:, cc * 128:(cc + 1) * 128],
                                                           scalar1=gmix[:, c:c + 1], scalar2=None,
                                                           op0=ALU.mult),
                          reads=[pb, cst_b], writes=[hT_b[g]])
        if j % 4 != 3 or "noproj" in DBG:
            continue
        t0 = g * 512
        for blk_i in range(NFM):
            if "nofm" in DBG:
                break
            p, pb = next_ps()
            for c in range(DC):
                kb.op("pe", lambda e: e.matmul(p[:, :], lhsT=W[:, c, blk_i * 128:(blk_i + 1) * 128],
                                               rhs=hT[:, c, t0:t0 + 512],
                                               start=(c == 0), stop=(c == DC - 1)),
                      reads=[W_b, hT_b[g]], writes=[pb])
            fi = cnt["fo"] % 3
            cnt["fo"] += 1
            if blk_i in NORM_BLOCKS:
                si = cnt["sq"] % 2
                cnt["sq"] += 1
                kb.op("act", lambda e: e.activation(out=sq[si][:], in_=p[:, :], func=AF.Square),
                      reads=[pb], writes=[sq_b[si]])
                p2, p2b = next_ps()
                kb.op("pe", lambda e: e.matmul(p2[:, :], lhsT=blk[:], rhs=sq[si][:],
                                               start=True, stop=True),
                      reads=[blk_b, sq_b[si]], writes=[p2b])
                kb.op("act", lambda e: e.activation(out=lnb[si][:], in_=p2[:, :], func=AF.Ln,
                                                    scale=1.0 / HD, bias=eps_ap),
                      reads=[p2b, cst_b], writes=[lnb_b[si]])
                kb.op("act", lambda e: e.activation(out=lnb[si][:], in_=lnb[si][:], func=AF.Exp,
                                                    scale=-0.5),
                      reads=[lnb_b[si]], writes=[lnb_b[si]])
                kb.op("dve", lambda e: e.scalar_tensor_tensor(out=fo[fi][:], in0=p[:, :],
                                                              scalar=gfm[:, blk_i:blk_i + 1],
                                                              in1=lnb[si][:], op0=ALU.mult,
                                                              op1=ALU.mult),
                      reads=[pb, cst_b, lnb_b[si]], writes=[fo_b[fi]])
            else:
                kb.op("dve", lambda e: e.tensor_copy(out=fo[fi][:], in_=p[:, :]),
                      reads=[pb], writes=[fo_b[fi]])
            if blk_i in Q_BLOCKS:
                qi = Q_BLOCKS.index(blk_i)
                kb.dma("sp", io["QT"][qi, :, t0:t0 + 512], fo[fi][:], reads=[fo_b[fi]],
                       writes=[io["QT_b"]])
            else:
                ki = K_BLOCKS.index(blk_i)
                kb.dma("sp", io["KT_fn"](ki)[:, t0:t0 + 512], fo[fi][:], reads=[fo_b[fi]],
                       writes=[io["KT_bf"](ki)])
        for a in range(4):
            if "notm" in DBG:
                break
            jj = g * 4 + a
            vi = cnt["vo"] % 2
            cnt["vo"] += 1
            for half_i in range(2):
                c0 = NFM * 128 + half_i * 454
                p, pb = next_ps()
                for c in range(DC):
                    kb.op("pe", lambda e: e.matmul(p[:, 0:454], lhsT=hT[:, c, jj * 128:(jj + 1) * 128],
                                                   rhs=W[:, c, c0:c0 + 454],
                                                   start=(c == 0), stop=(c == DC - 1)),
                          reads=[W_b, hT_b[g]], writes=[pb])
                if half_i == 0:
                    kb.op("act", lambda e: e.activation(out=vo[vi][:, 0:454], in_=p[:, 0:454],
                                                        func=AF.Copy),
                          reads=[pb], writes=[vo_b[vi]])
                else:
                    kb.op("dve", lambda e: e.tensor_copy(out=vo[vi][:, 454:896], in_=p[:, 0:442]),
                          reads=[pb], writes=[vo_b[vi]])
                    kb.op("act", lambda e: e.activation(out=go[:, jj * 12:(jj + 1) * 12],
                                                        in_=p[:, 442:454], func=AF.Sigmoid),
                          reads=[pb], writes=[go_b])
            if "nov" not in DBG:
                kb.dma("sp", io["V_fn"](jj), vo[vi][:, 0:io.get("V_cols", 896)], reads=[vo_b[vi]],
                       writes=[io["V_bf"](jj)])
            if jj == NT - 1:
                kb.dma("sp", io["G"], go[:], reads=[go_b], writes=[io["G_b"]])
    kb.pop_scope()


def core_rows(c):
    b, r = divmod(c, RANKS)
    idx = np.concatenate([np.arange((4 * j + r) * 128, (4 * j + r + 1) * 128) for j in range(NT)])
    return b, r, idx


def host_consts_A(norm_mix_l, qk_gain_nsa_l, qk_gain_dil_l):
    gmix = np.ascontiguousarray(norm_mix_l.reshape(DC, 128).T)
    gfm = np.ones((128, NFM), np.float32)
    two = lambda v: np.concatenate([v, v])
    gfm[:, 0] = two(qk_gain_nsa_l[0])
    gfm[:, 1] = two(qk_gain_nsa_l[0])
    gfm[:, 2] = two(qk_gain_nsa_l[2])
    gfm[:, 3] = two(qk_gain_nsa_l[3])
    for g in range(3):
        gfm[:, 5 + g] = two(qk_gain_dil_l[0])
        gfm[:, 8 + g] = two(qk_gain_dil_l[1])
    return gmix, gfm


def build_A():
    kb = KB()
    io = {}
    x = kb.dram("x_own", [TOK, D_MODEL], F32, "ExternalInput")
    io["winp"] = kb.dram("winp", [D_MODEL, WP], F32, "ExternalInput")
    io["gmix"] = kb.dram("gmix", [128, DC], F32, "ExternalInput")
    io["gfm"] = kb.dram("gfm", [128, NFM], F32, "ExternalInput")
    io["ident"] = kb.dram("ident", [128, 128], F32, "ExternalInput")
    io["QT"] = kb.dram_bf16("QT", [NQB, 128, TOK], "ExternalOutput")
    io["KT"] = kb.dram_bf16("KT", [NKB, 128, TOK], "ExternalOutput")
    io["V"] = kb.dram_bf16("V", [TOK, NVH * 64], "ExternalOutput")
    io["G"] = kb.dram("G", [128, NT * 12], F32, "ExternalOutput")
    for n in ("QT", "KT", "V", "G"):
        io[n + "_b"] = kb.buf(n)
    io["KT_bf"] = lambda ki: io["KT_b"]
    io["V_bf"] = lambda jj: io["V_b"]
    io["KT_fn"] = lambda ki: io["KT"][ki]
    io["V_fn"] = lambda jj: io["V"][jj * 128:(jj + 1) * 128, :]
    emit_phase_A(kb, lambda j: (x[j * 128:(j + 1) * 128, :], None), None, io)
    kb.finish([io[n + "_b"] for n in ("QT", "KT", "V", "G")])
    kb.close()
    return kb


DIL_PAIRS = ((128, 1), (512, 4), (2048, 16))
WIN_NSA = 512


def alibi_slopes_np():
    i = np.arange(1, 11, dtype=np.float64)
    return np.exp2(-8.0 * i / 10.0)


def dil_combos(g):
    w = DIL_PAIRS[g][0] // 128
    out = []
    for dj in range(0, (w + 3) // 4 + 1):
        for rp in range(4):
            if any(0 <= 4 * dj + r - rp <= w for r in range(4)):
                out.append((dj, rp))
    return out


DIL_COMBOS = [dil_combos(g) for g in range(3)]
N_DIL_TAB = sum(len(c) for c in DIL_COMBOS) * 2


def bf16_pack(a):
    a = np.ascontiguousarray(np.asarray(a, dtype=np.float32).astype(NPBF))
    return a.view(np.float32)


def host_tables(r):
    sl = alibi_slopes_np()
    sl_dil, sl_nsa = sl[:6], sl[6:]
    ki = np.arange(128)[:, None].astype(np.float64)
    qi = np.arange(128)[None, :].astype(np.float64)
    T = {}
    ms = np.zeros((128, 4, 128))
    mc = np.zeros((128, 4, 128))
    for rp in range(4):
        if rp < r:
            ms[:, rp] = 1
            mc[:, rp] = 1
        elif rp == r:
            ms[:, rp] = (ki < qi)
            mc[:, rp] = (ki <= qi)
    T["mstrict"] = bf16_pack(ms.reshape(128, 512))
    T["mcausal"] = bf16_pack(mc.reshape(128, 512))
    tw = np.zeros((128, 4, 2, 4, 128))
    for h in range(4):
        for dj in range(2):
            for rp in range(4):
                d = 128 * (4 * dj + r - rp) + qi - ki
                tw[:, h, dj, rp] = ((d >= 0) & (d < WIN_NSA)) * np.exp(-sl_nsa[h] * np.maximum(d, 0))
    T["twin"] = bf16_pack(tw.reshape(128, -1))
    td = np.zeros((128, N_DIL_TAB, 128))
    idx = 0
    for g, (w, rd) in enumerate(DIL_PAIRS):
        for hh in range(2):
            for (dj, rp) in DIL_COMBOS[g]:
                d = 128 * (4 * dj + r - rp) + qi - ki
                ok = (d >= 0) & (d <= w) & (np.mod(d, rd) == 0)
                td[:, idx] = ok * np.exp(-sl_dil[2 * g + hh] * np.maximum(d, 0))
                idx += 1
    T["tdil"] = bf16_pack(td.reshape(128, -1))
    cm = np.zeros((128, 2, 4, 128))
    ni = ki
    for dl in range(2):
        for a in range(4):
            cm[:, dl, a] = (2048 * dl + 128 * (4 * a + r) + qi - 16 * ni - 31 >= 0)
    T["cmask"] = bf16_pack(cm.reshape(128, -1))
    bs = np.zeros((128, 4, 4, 128), np.float32)
    qcol = np.arange(128)[:, None]
    jj = np.arange(128)[None, :]
    for i in range(4):
        for a in range(4):
            cur = 32 * i + 8 * a + 2 * r + (qcol >= 64)
            b = np.where((jj == cur) | (jj == cur - 1), 1e4, 0.0) + np.where(jj == 0, 1e4, 0.0)
            b = np.where(jj > cur, -1e30, b)
            bs[:, i, a] = b
    T["bsel"] = bs.reshape(128, -1)
    u = np.arange(64)[None, :]
    bsl = np.zeros((128, 4, 64), np.float32)
    bcm = np.zeros((128, 4, 4), np.float32)
    for h in range(4):
        bsl[:, h] = sl_nsa[h] * (128 * (u - 48) + ki)
        for dl in range(4):
            bcm[:, h, dl] = (sl_nsa[h] * (-2048 * dl + 16 * ni + 31))[:, 0]
    T["bias_sel"] = bsl.reshape(128, -1)
    T["bias_cmp"] = bcm.reshape(128, -1)
    ov = np.zeros((128, 4, 128))
    for nt in range(4):
        n = 128 * nt + np.arange(128)[:, None]
        ov[:, nt] = (16 * n <= 64 * jj + 63) & (16 * n + 31 >= 64 * jj)
    T["ov"] = bf16_pack(ov.reshape(128, -1))
    e32 = np.zeros((128, 32, 128))
    kk = np.arange(128)[None, :]
    jj32 = (np.arange(128) % 64)[:, None]
    for uu in range(32):
        e32[:, uu] = 32768.0 * (jj32 == 2 * uu + kk // 64)
    T["e32"] = bf16_pack(e32.reshape(128, -1))
    tri = (np.arange(128)[:, None] >= np.arange(128)[None, :]).astype(np.float64)
    T["tri"] = bf16_pack(tri)
    T["identb"] = bf16_pack(np.eye(128))
    return T


TABLE_SHAPES = {
    "mstrict": (512, True), "mcausal": (512, True), "twin": (4 * 2 * 4 * 128, True),
    "tdil": (N_DIL_TAB * 128, True), "cmask": (2 * 512, True), "bsel": (4 * 4 * 128, False),
    "bias_sel": (256, False), "bias_cmp": (16, False), "ov": (512, True), "e32": (32 * 128, True),
    "tri": (128, True), "identb": (128, True),
}

TABH_ORDER = ("mstrict", "mcausal", "twin", "tdil", "cmask", "ov", "e32", "tri", "identb")
TABF_ORDER = ("bsel", "bias_sel", "bias_cmp")
TABH_OFF = {}
_o = 0
for _n in TABH_ORDER:
    TABH_OFF[_n] = _o
    _o += TABLE_SHAPES[_n][0]
TABH_COLS = _o
TABF_OFF = {}
_o = 0
for _n in TABF_ORDER:
    TABF_OFF[_n] = _o
    _o += TABLE_SHAPES[_n][0]
TABF_COLS = _o


def host_table_arrays(r):
    T = host_tables(r)
    tabh = np.concatenate([T[n] for n in TABH_ORDER], axis=1)
    tabf = np.concatenate([T[n] for n in TABF_ORDER], axis=1).astype(np.float32)
    assert tabh.shape == (128, TABH_COLS // 2) and tabf.shape == (128, TABF_COLS)
    return np.ascontiguousarray(tabh), np.ascontiguousarray(tabf)


def run_pipeline(units, stages):
    n = len(units)
    S = len(stages)
    for it in range(n + S - 1):
        for s in range(S):
            idx = it - s
            if 0 <= idx < n:
                stages[s](units[idx])


def emit_mixers(kb, io, cat, cat_b):
    nc = kb.nc
    kb.push_scope()
    bank, bank_b = kb.bank, kb.bank_b
    QTs, QT_b = kb.sbt("M_QT", [128, NQB, TOK], BF16)
    tabh, tabh_b = kb.sbt("M_tabh", [128, TABH_COLS], BF16)
    tabf, tabf_b = kb.sbt("M_tabf", [128, TABF_COLS], F32)
    gs, gs_b = kb.sbt("M_G", [128, NT * 12], F32)
    ones_bf, cst_b = kb.sbt("M_ones", [128, 128], BF16)
    onec = kb.sb("M_onec", [128, 1], F32)
    ocomb, ocomb_b = kb.sbt("M_ocomb", [128, NT, 4, HD], F32)
    negselT, negselT_b = kb.sbt("M_negselT", [128, 2, TOK], BF16)
    kslot = [kb.sbt("M_k%d" % i, [128, RANKS, TOK], BF16) for i in range(2)]
    vslot = [kb.sbt("M_v%d" % i, [128, RANKS * NT, HD + 1], BF16) for i in range(2)]
    cnt = {"k": 0, "v": 0}

    def tab(name, lo, n):
        o = TABH_OFF[name] + lo
        return tabh[:, o:o + n]

    def tabfp(name, lo, n):
        o = TABF_OFF[name] + lo
        return tabf[:, o:o + n]

    kb.dma("sp", QTs[:], io["QT"].rearrange("b p t -> p b t"), reads=[io["QT_b"]], writes=[QT_b])
    hc = TABH_COLS // 4
    for q in range(4):
        kb.dma("sp", tabh[:, q * hc:(q + 1) * hc], io["tabh"][:, q * hc:(q + 1) * hc], writes=[tabh_b])
    kb.dma("sp", tabf[:], io["tabf"], writes=[tabf_b])
    kb.dma("sp", gs[:], io["G"], reads=[io["G_b"]], writes=[gs_b])
    kb.op("pool", lambda e: e.memset(ones_bf[:], 1.0), writes=[cst_b])
    kb.op("pool", lambda e: e.memset(onec[:], 1.0), writes=[cst_b])
    for i in range(2):
        kb.op("pool", lambda e: e.memset(vslot[i][0][:, :, HD:HD + 1], 1.0), writes=[vslot[i][1]])

    def load_k(kbi):
        s = cnt["k"] % 2
        cnt["k"] += 1
        t, b = kslot[s]
        for rp in range(RANKS):
            kb.dma("sp", t[:, rp, :], io["KTa_fn"](rp, kbi), reads=[io["KTa_bf"](kbi)], writes=[b])
        return t, b

    def load_v(vh):
        s = cnt["v"] % 2
        cnt["v"] += 1
        t, b = vslot[s]
        for rp in range(RANKS):
            for q in range(4):
                src = io["Va_fn"](rp, q, vh)
                kb.dma("sp", t[:, rp * NT + q * 4:rp * NT + q * 4 + 4, 0:HD],
                       src.rearrange("(j p) d -> p j d", p=128), reads=[io["Va_bf"](q)], writes=[b])
        return t, b

    class U:
        pass

    def sb_units(hh, ks, vs):
        units = []
        n = 0
        for i in range(NG):
            first = True
            for jp in range(4 * i + 3, -1, -1):
                for rp in range(3, -1, -1):
                    u = U()
                    u.n = n
                    n += 1
                    u.i, u.jp, u.rp = i, jp, rp
                    u.masked = jp >= 4 * i
                    u.a0 = jp - 4 * i if u.masked else 0
                    u.c0 = 128 * u.a0
                    u.first = first
                    first = False
                    u.last = (jp == 0 and rp == 0)
                    units.append(u)
        return units

    def emit_sb_head(h, ks, vs):
        kt, ktb = ks
        vt, vtb = vs
        hp, hh = divmod(h, 2)
        rows = slice(64 * hh, 64 * hh + 64)
        qb = 5 + hp

        def s1(u):
            zb, zbb = bank[u.n % 2], bank_b[u.n % 2]
            q0 = 512 * u.i + u.c0
            q1 = 512 * u.i + 512
            kb.op("pe", lambda e: e.matmul(zb[:, u.c0:512], lhsT=kt[rows, u.rp, u.jp * 128:(u.jp + 1) * 128],
                                           rhs=QTs[rows, qb, q0:q1], start=True, stop=True),
                  reads=[ktb, QT_b], writes=[zbb])
            et, etb = e_sb[u.n % 4]
            kb.op("act", lambda e: e.activation(out=et[:, u.c0:512], in_=zb[:, u.c0:512], func=AF.Exp,
                                                scale=0.125),
                  reads=[zbb], writes=[etb])

        def s1b(u):
            et, etb = e_sb[u.n % 4]
            st, stb = sp_sb[u.n % 3]
            kb.op("act", lambda e: e.activation(out=st[:, u.c0:512], in_=et[:, u.c0:512], func=AF.Ln,
                                                bias=onec[:, 0:1]),
                  reads=[etb, cst_b], writes=[stb])
            if u.masked:
                kb.op("pool", lambda e: e.tensor_tensor(out=st[:, u.c0:u.c0 + 128], in0=st[:, u.c0:u.c0 + 128],
                                                        in1=tab("mstrict", u.rp * 128, 128), op=ALU.mult),
                      reads=[stb, tabh_b], writes=[stb])

        def s2(u):
            gb, gbb = bank[2 + u.n % 2], bank_b[2 + u.n % 2]
            st, stb = sp_sb[u.n % 3]
            q0 = 512 * u.i + u.c0
            q1 = 512 * u.i + 512
            if u.first:
                kb.op("pool", lambda e: e.memset(spacc[:], 0.0), writes=[spacc_b])
            kb.op("pe", lambda e: e.matmul(gb[:, u.c0:512], lhsT=tab("tri", 0, 128), rhs=st[:, u.c0:512],
                                           start=True, stop=u.first),
                  reads=[tabh_b, stb], writes=[gbb])
            if not u.first:
                kb.op("pe", lambda e: e.matmul(gb[:, u.c0:512], lhsT=ones_bf[:], rhs=spacc[:, u.c0:512],
                                               start=False, stop=True),
                      reads=[cst_b, spacc_b], writes=[gbb])
            if not u.last:
                kb.op("dve", lambda e: e.tensor_tensor(out=spacc[:, u.c0:512], in0=spacc[:, u.c0:512],
                                                       in1=st[:, u.c0:512], op=ALU.add),
                      reads=[spacc_b, stb], writes=[spacc_b])
            at, atb = aT_sb[u.n % 2]
            xt, xtb = x_sb[u.n % 2]
            et, etb = e_sb[u.n % 4]
            kb.op("act", lambda e: e.activation(out=xt[:, u.c0:512], in_=gb[:, u.c0:512], func=AF.Exp,
                                                scale=-1.0),
                  reads=[gbb], writes=[xtb])
            kb.op("dve", lambda e: e.tensor_tensor(out=at[:, u.c0:512], in0=xt[:, u.c0:512], in1=et[:, u.c0:512],
                                                   op=ALU.mult),
                  reads=[xtb, etb], writes=[atb])
            if u.masked:
                kb.op("pool", lambda e: e.tensor_tensor(out=at[:, u.c0:u.c0 + 128], in0=at[:, u.c0:u.c0 + 128],
                                                        in1=tab("mstrict", u.rp * 128, 128), op=ALU.mult),
                      reads=[atb, tabh_b], writes=[atb])

        def s3(u):
            ab, abb = bank[4 + u.i % 2], bank_b[4 + u.i % 2]
            at, atb = aT_sb[u.n % 2]
            for a in range(u.a0, 4):
                kb.op("pe", lambda e: e.matmul(ab[:, a * HD:(a + 1) * HD], lhsT=at[:, a * 128:(a + 1) * 128],
                                               rhs=vt[:, u.rp * NT + u.jp, 0:HD],
                                               start=(u.first and a == u.a0), stop=(u.last and a == 3)),
                      reads=[atb, vtb], writes=[abb])
            if u.last:
                kb.op("dve", lambda e: e.tensor_copy(
                    out=cat[:, 4 * u.i:4 * u.i + 4, 384 + h * HD:384 + (h + 1) * HD],
                    in_=ab[:, 0:4 * HD].rearrange("p (a d) -> p a d", d=HD)),
                    reads=[abb], writes=[cat_b])

        run_pipeline(sb_units(hh, ks, vs), [s1, s1b, s2, s3])


    HD1 = HD + 1
    p_sb = [kb.sbt("M_p%d" % i, [128, 512], BF16) for i in range(3)]
    r4 = [kb.sbt("M_r4%d" % i, [128, 8], F32) for i in range(4)]
    r4c = [0]

    def acc_view(ab, lo, n):
        return ab[:, 0:4 * HD1].rearrange("p (a d) -> p a d", d=HD1)[:, :, lo:lo + n]

    def recip_den(ab, abb, i, gate_col):
        t, tb = r4[r4c[0] % 4]
        r4c[0] += 1
        kb.op("dve", lambda e: e.tensor_scalar(out=t[:, 0:4], in0=acc_view(ab, HD, 1), scalar1=1e-30,
                                               scalar2=None, op0=ALU.max),
              reads=[abb], writes=[tb])
        kb.op("dve", lambda e: e.reciprocal(out=t[:, 0:4], in_=t[:, 0:4]), reads=[tb], writes=[tb])
        if gate_col is not None:
            gv = gs[:, 4 * i * 12:(4 * i + 4) * 12].rearrange("p (a k) -> p a k", k=12)[:, :, gate_col]
            kb.op("dve", lambda e: e.tensor_tensor(out=t[:, 4:8], in0=t[:, 0:4], in1=gv, op=ALU.mult),
                  reads=[tb, gs_b], writes=[tb])
        return t, tb

    def evac_nsa(ab, abb, i, h, br, first):
        t, tb = recip_den(ab, abb, i, 3 * h + br)
        for a in range(4):
            src = ab[:, a * HD1:a * HD1 + HD]
            dst = ocomb[:, 4 * i + a, h, :]
            if first:
                kb.op("dve", lambda e: e.tensor_scalar(out=dst, in0=src, scalar1=t[:, 4 + a:5 + a], scalar2=None,
                                                       op0=ALU.mult),
                      reads=[abb, tb], writes=[ocomb_b])
            else:
                kb.op("dve", lambda e: e.scalar_tensor_tensor(out=dst, in0=src, scalar=t[:, 4 + a:5 + a], in1=dst,
                                                              op0=ALU.mult, op1=ALU.add),
                      reads=[abb, tb, ocomb_b], writes=[ocomb_b])
        return t, tb

    def emit_banded(kt, ktb, vt, vtb, rows, qblk, tabname, tab0, combos, bank_s, bank_acc, evac):
        units = []
        n = 0
        for i in range(NG):
            glist = []
            for a in range(4):
                j = 4 * i + a
                valid = [(ci, dj, rp) for ci, (dj, rp) in enumerate(combos) if j - dj >= 0]
                for c0 in range(0, len(valid), 4):
                    u = U()
                    u.i, u.a, u.j = i, a, j
                    u.sub = valid[c0:c0 + 4]
                    glist.append(u)
            for k, u in enumerate(glist):
                u.n = n
                n += 1
                u.first = (k == 0)
                u.last = (k == len(glist) - 1)
                units.append(u)

        def s1(u):
            sbk, sbb = bank[bank_s[u.n % 2]], bank_b[bank_s[u.n % 2]]
            for q, (ci, dj, rp) in enumerate(u.sub):
                jp = u.j - dj
                kb.op("pe", lambda e: e.matmul(sbk[:, q * 128:(q + 1) * 128],
                                               lhsT=kt[rows, rp, jp * 128:(jp + 1) * 128],
                                               rhs=QTs[rows, qblk, u.j * 128:(u.j + 1) * 128],
                                               start=True, stop=True),
                      reads=[ktb, QT_b], writes=[sbb])

        def s2(u):
            sbk, sbb = bank[bank_s[u.n % 2]], bank_b[bank_s[u.n % 2]]
            pt, ptb = p_sb[u.n % 3]
            w = 128 * len(u.sub)
            kb.op("act", lambda e: e.activation(out=pt[:, 0:w], in_=sbk[:, 0:w], func=AF.Exp, scale=0.125),
                  reads=[sbb], writes=[ptb])
            ci0 = u.sub[0][0]
            kb.op("dve", lambda e: e.tensor_tensor(out=pt[:, 0:w], in0=pt[:, 0:w],
                                                   in1=tab(tabname, (tab0 + ci0) * 128, w), op=ALU.mult),
                  reads=[ptb, tabh_b], writes=[ptb])

        def s3(u):
            ab, abb = bank[bank_acc[u.i % 2]], bank_b[bank_acc[u.i % 2]]
            pt, ptb = p_sb[u.n % 3]
            for q, (ci, dj, rp) in enumerate(u.sub):
                jp = u.j - dj
                kb.op("pe", lambda e: e.matmul(ab[:, u.a * HD1:(u.a + 1) * HD1], lhsT=pt[:, q * 128:(q + 1) * 128],
                                               rhs=vt[:, rp * NT + jp, 0:HD1],
                                               start=(u.first and q == 0),
                                               stop=(u.last and q == len(u.sub) - 1)),
                      reads=[ptb, vtb], writes=[abb])
            if u.last:
                evac(ab, abb, u.i)

        run_pipeline(units, [s1, s2, s3])

    def emit_dil():
        kb.push_scope()
        dacc, dacc_b = kb.sbt("M_dacc", [128, NT, 2, HD1], F32)
        tab0 = 0
        for g in range(3):
            ks = load_k(3 + g)
            for hh in range(2):
                vs = load_v(2 + 2 * g + hh)

                def evac_d(ab, abb, i, g=g, hh=hh):
                    dst = dacc[:, 4 * i:4 * i + 4, hh, :]
                    src = acc_view(ab, 0, HD1)
                    if g == 0:
                        kb.op("dve", lambda e: e.tensor_copy(out=dst, in_=src), reads=[abb], writes=[dacc_b])
                    else:
                        kb.op("dve", lambda e: e.tensor_tensor(out=dst, in0=src, in1=dst, op=ALU.add),
                              reads=[abb, dacc_b], writes=[dacc_b])

                emit_banded(ks[0], ks[1], vs[0], vs[1], slice(64 * hh, 64 * hh + 64), 2 + g, "tdil", tab0,
                            DIL_COMBOS[g], (0, 1), (2, 3), evac_d)
                tab0 += len(DIL_COMBOS[g])
        dr, dr_b = kb.sbt("M_dr", [128, NT * 2], F32)
        dflat = dacc[:].rearrange("p j h d -> p (j h) d")
        kb.op("dve", lambda e: e.reciprocal(out=dr[:], in_=dflat[:, :, HD]), reads=[dacc_b], writes=[dr_b])
        for j in range(NT):
            for hh in range(2):
                kb.op("dve", lambda e: e.tensor_scalar(out=cat[:, j, 256 + hh * HD:256 + (hh + 1) * HD],
                                                       in0=dacc[:, j, hh, 0:HD],
                                                       scalar1=dr[:, 2 * j + hh:2 * j + hh + 1], scalar2=None,
                                                       op0=ALU.mult),
                      reads=[dacc_b, dr_b], writes=[cat_b])
        kb.pop_scope()


    def emit_nsa():
        kb.push_scope()
        _xs = cnt["k"] % 2
        cnt["k"] += 1
        XT_b = kslot[_xs][1]
        W1s, W1_b = kb.sbt("N_W1", [128, 32, 128], BF16)
        W2s, W2_b = kb.sbt("N_W2", [128, 128], BF16)
        peTf, pe_b = kb.sbt("N_peTf", [128, 32], F32)
        peTb, peb_b = kb.sbt("N_peTb", [128, 32], BF16)
        gkc, gkc_b = kb.sbt("N_gkc", [128, HD], F32)
        hTc = [kb.sbt("N_hT%d" % c, [128, 512], BF16) for c in range(2)]
        u_sb, u_b = kb.sbt("N_u", [128, 512], F32)
        t_sb, t_b = kb.sbt("N_t", [128, 512], F32)
        bias2, bias2_b = kb.sbt("N_bias2", [128, 2], F32)
        KcT2, KcT_b = kb.sbt("N_KcT2", [128, 512], BF16)
        Vc, Vc_b = kb.sbt("N_Vc", [128, 4, HD1], BF16)
        kc2, kc2_b = kb.sbt("N_kc2", [128, 128], BF16)
        stt, stt_b = kb.sbt("N_st", [128, 4], F32)
        epsb, epsb_b = kb.sbt("N_eps", [128, 1], F32)
        imp, imp_b = kb.sbt("N_imp", [128, 512], F32)
        impb, impb_b = kb.sbt("N_impb", [128, 512], F32)
        tmpm, tmpm_b = kb.sbt("N_tmpm", [128, 128], F32)
        m8, m8_b = kb.sbt("N_m8", [128, 16], F32)
        nsel, nsel_b = kb.sbt("N_nsel", [128, 128], BF16)
        nsel2, nsel2_b = kb.sbt("N_nsel2", [128, 128], BF16)

        XTflat = kslot[_xs][0][:].rearrange("p r t -> p (r t)")
        xv = XTflat.rearrange("p (j r q) -> p j r q", r=RANKS, q=128)
        for rp in range(RANKS):
            kb.dma("sp", xv[:, :, rp, :], io["KTa_fn"](rp, 2).rearrange("p (j q) -> p j q", q=128),
                   reads=[io["KTa_bf"](2)], writes=[XT_b])
        for hf in range(2):
            kb.dma("pool", W1s[:, hf * 16:(hf + 1) * 16, :].rearrange("p l h -> p (l h)"),
                   io["w1p"][:, hf * 2048:(hf + 1) * 2048], writes=[W1_b])
        kb.dma("pool", W2s[:], io["w2p"], writes=[W2_b])
        kb.dma("sp", peTf[:], io["peT"], writes=[pe_b])
        kb.dma("sp", gkc[:], io["gkc"], writes=[gkc_b])
        kb.op("dve", lambda e: e.tensor_copy(out=peTb[:], in_=peTf[:]), reads=[pe_b], writes=[peb_b])
        kb.op("pool", lambda e: e.memset(epsb[:], RMS_EPS), writes=[epsb_b])
        kb.op("pool", lambda e: e.memset(Vc[:], 1.0), writes=[Vc_b])
        for c in range(2):
            kb.op("pool", lambda e: e.memset(hTc[c][0][:], 0.0), writes=[hTc[c][1]])
        xs = XTflat.rearrange("p (n s) -> p n s", s=16)
        if "n1" in DBG:
            kb.pop_scope()
            return
        for c in range(2):
            rows = slice(64 * c, 64 * c + 64)
            for l in range(32):
                kb.op("pe", lambda e: e.matmul(bank[2][:, 32 * c + l:32 * c + l + 1], lhsT=W1s[rows, l, :],
                                               rhs=peTb[rows, l:l + 1], start=True, stop=True),
                      reads=[W1_b, peb_b], writes=[bank_b[2]])
        kb.op("dve", lambda e: e.tensor_reduce(out=bias2[:], in_=bank[2][:, 0:64].rearrange("p (c l) -> p c l", l=32),
                                               axis=mybir.AxisListType.X, op=ALU.add),
              reads=[bank_b[2]], writes=[bias2_b])
        for c in range(2):
            rows = slice(64 * c, 64 * c + 64)
            pbk, pbb = bank[c], bank_b[c]
            for l in range(32):
                rhs = xs[rows, 0:511, l] if l < 16 else xs[rows, 1:512, l - 16]
                kb.op("pe", lambda e: e.matmul(pbk[:, 0:511], lhsT=W1s[rows, l, :], rhs=rhs,
                                               start=(l == 0), stop=(l == 31)),
                      reads=[W1_b, XT_b], writes=[pbb])
            kb.op("act", lambda e: e.activation(out=u_sb[:, 0:511], in_=pbk[:, 0:511], func=AF.Identity,
                                                bias=bias2[:, c:c + 1]),
                  reads=[pbb, bias2_b], writes=[u_b])
            kb.op("dve", lambda e: e.tensor_tensor(out=t_sb[:, 0:511], in0=u_sb[:, 0:511], in1=u_sb[:, 0:511],
                                                   op=ALU.mult), reads=[u_b], writes=[t_b])
            kb.op("dve", lambda e: e.tensor_scalar(out=t_sb[:, 0:511], in0=t_sb[:, 0:511], scalar1=0.044715,
                                                   scalar2=1.0, op0=ALU.mult, op1=ALU.add),
                  reads=[t_b], writes=[t_b])
            kb.op("dve", lambda e: e.tensor_tensor(out=t_sb[:, 0:511], in0=t_sb[:, 0:511], in1=u_sb[:, 0:511],
                                                   op=ALU.mult), reads=[t_b, u_b], writes=[t_b])
            kb.op("act", lambda e: e.activation(out=t_sb[:, 0:511], in_=t_sb[:, 0:511], func=AF.Sigmoid,
                                                scale=1.5957691216057308), reads=[t_b], writes=[t_b])
            kb.op("dve", lambda e: e.tensor_tensor(out=hTc[c][0][:, 0:511], in0=t_sb[:, 0:511],
                                                   in1=u_sb[:, 0:511], op=ALU.mult),
                  reads=[t_b, u_b], writes=[hTc[c][1]])
        if "n2" in DBG:
            kb.pop_scope()
            return
        for c in range(2):
            for nt in range(4):
                pk, pkb = bank[3 + nt % 2], bank_b[3 + nt % 2]
                kb.op("pe", lambda e: e.matmul(pk[:, 0:HD], lhsT=hTc[c][0][:, nt * 128:(nt + 1) * 128],
                                               rhs=W2s[:, c * HD:(c + 1) * HD], start=True, stop=True),
                      reads=[hTc[c][1], W2_b], writes=[pkb])
                if c == 1:
                    kb.op("act", lambda e: e.activation(out=Vc[:, nt, 0:HD], in_=pk[:, 0:HD], func=AF.Copy),
                          reads=[pkb], writes=[Vc_b])
                    continue
                kb.op("act", lambda e: e.activation(out=t_sb[:, 0:HD], in_=pk[:, 0:HD], func=AF.Square,
                                                    accum_out=stt[:, 0:1]),
                      reads=[pkb], writes=[t_b, stt_b])
                kb.op("dve", lambda e: e.tensor_scalar(out=stt[:, 1:2], in0=stt[:, 0:1], scalar1=1.0 / HD,
                                                       scalar2=RMS_EPS, op0=ALU.mult, op1=ALU.add),
                      reads=[stt_b], writes=[stt_b])
                kb.op("act", lambda e: e.activation(out=stt[:, 2:3], in_=stt[:, 1:2], func=AF.Sqrt),
                      reads=[stt_b], writes=[stt_b])
                kb.op("dve", lambda e: e.reciprocal(out=stt[:, 3:4], in_=stt[:, 2:3]), reads=[stt_b],
                      writes=[stt_b])
                kb.op("dve", lambda e: e.scalar_tensor_tensor(out=kc2[:, 0:HD], in0=pk[:, 0:HD],
                                                              scalar=stt[:, 3:4], in1=gkc[:], op0=ALU.mult,
                                                              op1=ALU.mult),
                      reads=[pkb, stt_b, gkc_b], writes=[kc2_b])
                kb.op("dve", lambda e: e.tensor_copy(out=kc2[:, HD:2 * HD], in_=kc2[:, 0:HD]), reads=[kc2_b],
                      writes=[kc2_b])
                kb.op("pe", lambda e: e.transpose(out=kb.bankh[:, 0:128], in_=kc2[:], identity=tab("identb", 0, 128)),
                      reads=[kc2_b, tabh_b], writes=[kb.bankh_b])
                kb.op("act", lambda e: e.activation(out=KcT2[:, nt * 128:(nt + 1) * 128], in_=kb.bankh[:, 0:128],
                                                    func=AF.Copy), reads=[kb.bankh_b], writes=[KcT_b])
        if "n3" in DBG:
            kb.op("dve", lambda e: e.tensor_copy(out=cat[:, 0, 0:512], in_=KcT2[:]), reads=[KcT_b], writes=[cat_b])
            kb.op("dve", lambda e: e.tensor_copy(out=cat[:, 1, 0:260], in_=Vc[:].rearrange("p a d -> p (a d)")),
                  reads=[Vc_b], writes=[cat_b])
            kb.op("dve", lambda e: e.tensor_copy(out=cat[:, 2, 0:512], in_=hTc[0][0][:]), reads=[hTc[0][1]], writes=[cat_b])
            kb.op("dve", lambda e: e.tensor_copy(out=cat[:, 3, 0:512], in_=hTc[1][0][:]), reads=[hTc[1][1]], writes=[cat_b])
            kb.op("dve", lambda e: e.tensor_copy(out=cat[:, 4, 0:2], in_=bias2[:]), reads=[bias2_b], writes=[cat_b])
            kb.pop_scope()
            return

        for i in range(NG):
            for h in range(4):
                rows = slice(64 * (h % 2), 64 * (h % 2) + 64)
                accC, accC_b = bank[2 + 2 * (h % 2)], bank_b[2 + 2 * (h % 2)]
                accI, accI_b = bank[3 + 2 * (h % 2)], bank_b[3 + 2 * (h % 2)]
                for nt in range(i + 1):
                    dl = i - nt
                    sbk, sbb = bank[nt % 2], bank_b[nt % 2]
                    pt, ptb = p_sb[nt % 3]
                    kb.op("pe", lambda e: e.matmul(sbk[:, :], lhsT=KcT2[rows, nt * 128:(nt + 1) * 128],
                                                   rhs=QTs[rows, h // 2, 512 * i:512 * i + 512],
                                                   start=True, stop=True),
                          reads=[KcT_b, QT_b], writes=[sbb])
                    kb.op("act", lambda e: e.activation(out=pt[:], in_=sbk[:, :], func=AF.Exp, scale=0.125,
                                                        bias=tabfp("bias_cmp", 4 * h + dl, 1)),
                          reads=[sbb, tabf_b], writes=[ptb])
                    if dl <= 1:
                        kb.op("pool", lambda e: e.tensor_tensor(out=pt[:], in0=pt[:],
                                                                in1=tab("cmask", 512 * dl, 512), op=ALU.mult),
                              reads=[ptb, tabh_b], writes=[ptb])
                    for a in range(4):
                        kb.op("pe", lambda e: e.matmul(accC[:, a * HD1:(a + 1) * HD1],
                                                       lhsT=pt[:, a * 128:(a + 1) * 128], rhs=Vc[:, nt, :],
                                                       start=(nt == 0 and a == 0), stop=(nt == i and a == 3)),
                              reads=[ptb, Vc_b], writes=[accC_b])
                        kb.op("pe", lambda e: e.matmul(accI[:, a * 128:(a + 1) * 128],
                                                       lhsT=pt[:, a * 128:(a + 1) * 128],
                                                       rhs=tab("ov", nt * 128, 128),
                                                       start=(nt == 0 and a == 0), stop=(nt == i and a == 3)),
                              reads=[ptb, tabh_b], writes=[accI_b])
                t, tb = evac_nsa(accC, accC_b, i, h, 0, True)
                for a in range(4):
                    src = accI[:, a * 128:(a + 1) * 128]
                    dst = imp[:, a * 128:(a + 1) * 128]
                    if h == 0:
                        kb.op("dve", lambda e: e.tensor_scalar(out=dst, in0=src, scalar1=t[:, a:a + 1],
                                                               scalar2=None, op0=ALU.mult),
                              reads=[accI_b, tb], writes=[imp_b])
                    else:
                        kb.op("dve", lambda e: e.scalar_tensor_tensor(out=dst, in0=src, scalar=t[:, a:a + 1],
                                                                      in1=dst, op0=ALU.mult, op1=ALU.add),
                              reads=[accI_b, tb, imp_b], writes=[imp_b])
            kb.op("dve", lambda e: e.tensor_tensor(out=impb[:], in0=imp[:], in1=tabfp("bsel", 512 * i, 512),
                                                   op=ALU.add), reads=[imp_b, tabf_b], writes=[impb_b])
            for a in range(4):
                iv = impb[:, a * 128:(a + 1) * 128]
                kb.op("dve", lambda e: e.max(out=m8[:, 0:8], in_=iv), reads=[impb_b], writes=[m8_b])
                kb.op("dve", lambda e: e.match_replace(out=tmpm[:], in_to_replace=m8[:, 0:8], in_values=iv,
                                                       imm_value=-3.0e38),
                      reads=[impb_b, m8_b], writes=[tmpm_b])
                kb.op("dve", lambda e: e.max(out=m8[:, 8:16], in_=tmpm[:]), reads=[tmpm_b], writes=[m8_b])
                kb.op("dve", lambda e: e.tensor_scalar(out=nsel[:], in0=iv, scalar1=m8[:, 15:16], scalar2=-1.0,
                                                       op0=ALU.is_ge, op1=ALU.add),
                      reads=[impb_b, m8_b], writes=[nsel_b])
                for bh in range(2):
                    kb.op("dve", lambda e: e.tensor_copy(out=nsel2[:, 0:64], in_=nsel[:, 64 * bh:64 * bh + 64]),
                          reads=[nsel_b], writes=[nsel2_b])
                    kb.op("dve", lambda e: e.tensor_copy(out=nsel2[:, 64:128], in_=nsel[:, 64 * bh:64 * bh + 64]),
                          reads=[nsel_b], writes=[nsel2_b])
                    kb.op("pe", lambda e: e.transpose(out=kb.bankh[:, 128:256], in_=nsel2[:],
                                                      identity=tab("identb", 0, 128)),
                          reads=[nsel2_b, tabh_b], writes=[kb.bankh_b])
                    kb.op("act", lambda e: e.activation(
                        out=negselT[:, bh, 512 * i + 128 * a:512 * i + 128 * (a + 1)],
                        in_=kb.bankh[:, 128:256], func=AF.Copy),
                        reads=[kb.bankh_b], writes=[negselT_b])
        kb.pop_scope()
        if "n4" in DBG:
            kb.op("act", lambda e: e.activation(out=cat[:, :, 0:256], in_=ocomb[:].rearrange("p j h d -> p j (h d)"),
                                                func=AF.Copy), reads=[ocomb_b], writes=[cat_b])
            return

        ks = load_k(0)
        vs = load_v(0)
        kt, ktb = ks
        vt, vtb = vs
        for h in range(4):
            rows = slice(64 * (h % 2), 64 * (h % 2) + 64)
            units = []
            n = 0
            for i in range(NG):
                tot = 4 * (4 * i + 4)
                k = 0
                for jp in range(4 * i + 4):
                    for rp in range(4):
                        u = U()
                        u.n = n
                        n += 1
                        u.i, u.jp, u.rp = i, jp, rp
                        u.masked = jp >= 4 * i
                        u.a0 = jp - 4 * i if u.masked else 0
                        u.c0 = 128 * u.a0
                        u.first = (k == 0)
                        u.last = (k == tot - 1)
                        k += 1
                        units.append(u)

            def s1(u):
                sbk, sbb = bank[u.n % 2], bank_b[u.n % 2]
                q0, q1 = 512 * u.i + u.c0, 512 * u.i + 512
                mk = 4 * u.jp + u.rp
                q4, uu = divmod(mk, 32)
                kb.op("pe", lambda e: e.matmul(sbk[:, u.c0:512], lhsT=kt[rows, u.rp, u.jp * 128:(u.jp + 1) * 128],
                                               rhs=QTs[rows, h // 2, q0:q1], start=True, stop=False),
                      reads=[ktb, QT_b], writes=[sbb])
                kb.op("pe", lambda e: e.matmul(sbk[:, u.c0:512],
                                               lhsT=tab("e32", uu * 128, 128)[rows, :],
                                               rhs=negselT[rows, q4, q0:q1], start=False, stop=True),
                      reads=[tabh_b, negselT_b], writes=[sbb])

            def s2(u):
                sbk, sbb = bank[u.n % 2], bank_b[u.n % 2]
                pt, ptb = p_sb[u.n % 3]
                col = 64 * h + 4 * (u.jp - 4 * u.i) + u.rp + 48
                kb.op("act", lambda e: e.activation(out=pt[:, u.c0:512], in_=sbk[:, u.c0:512], func=AF.Exp,
                                                    scale=0.125, bias=tabfp("bias_sel", col, 1)),
                      reads=[sbb, tabf_b], writes=[ptb])
                if u.masked:
                    kb.op("pool", lambda e: e.tensor_tensor(out=pt[:, u.c0:u.c0 + 128], in0=pt[:, u.c0:u.c0 + 128],
                                                            in1=tab("mcausal", u.rp * 128, 128), op=ALU.mult),
                          reads=[ptb, tabh_b], writes=[ptb])

            def s3(u):
                ab, abb = bank[2 + u.i % 2], bank_b[2 + u.i % 2]
                pt, ptb = p_sb[u.n % 3]
                for a in range(u.a0, 4):
                    kb.op("pe", lambda e: e.matmul(ab[:, a * HD1:(a + 1) * HD1], lhsT=pt[:, a * 128:(a + 1) * 128],
                                                   rhs=vt[:, u.rp * NT + u.jp, :],
                                                   start=(u.first and a == u.a0), stop=(u.last and a == 3)),
                          reads=[ptb, vtb], writes=[abb])
                if u.last:
                    evac_nsa(ab, abb, u.i, h, 1, False)

            run_pipeline(units, [s1, s2, s3])

        if "n5" in DBG:
            kb.op("act", lambda e: e.activation(out=cat[:, :, 0:256], in_=ocomb[:].rearrange("p j h d -> p j (h d)"),
                                                func=AF.Copy), reads=[ocomb_b], writes=[cat_b])
            return
        ks = load_k(1)
        vs = load_v(1)
        wcombos = [(dj, rp) for dj in range(2) for rp in range(4)]
        for h in range(4):
            emit_banded(ks[0], ks[1], vs[0], vs[1], slice(64 * (h % 2), 64 * (h % 2) + 64), h // 2, "twin", 8 * h,
                        wcombos, (0, 1), (2, 3),
                        lambda ab, abb, i, h=h: evac_nsa(ab, abb, i, h, 2, False))
        kb.op("act", lambda e: e.activation(out=cat[:, :, 0:256], in_=ocomb[:].rearrange("p j h d -> p j (h d)"),
                                            func=AF.Copy), reads=[ocomb_b], writes=[cat_b])


    if "nonsa" not in DBG:
        emit_nsa()
    if "nodil" not in DBG:
        emit_dil()

    kb.push_scope()
    e_sb = [kb.sbt("M_e%d" % i, [128, 512], F32) for i in range(4)]
    x_sb = [kb.sbt("M_x%d" % i, [128, 512], F32) for i in range(2)]
    sp_sb = [kb.sbt("M_sp%d" % i, [128, 512], BF16) for i in range(3)]
    aT_sb = [kb.sbt("M_aT%d" % i, [128, 512], BF16) for i in range(2)]
    spacc, spacc_b = kb.sbt("M_spacc", [128, 512], BF16)

    if "nosb" not in DBG:
        for hp in range(3):
            ks = load_k(6 + hp)
            for hh in range(2):
                if "sb1" in DBG and (hp, hh) != (0, 0):
                    continue
                vs = load_v(8 + 2 * hp + hh)
                emit_sb_head(2 * hp + hh, ks, vs)
    kb.pop_scope()

    kb.pop_scope()


def declare_mixer_io(kb, io):
    io["QT"] = kb.dram_bf16("QT", [NQB, 128, TOK], "ExternalInput")
    io["KTa"] = kb.dram_bf16("KTa", [RANKS, NKB, 128, TOK], "ExternalInput")
    io["Va"] = kb.dram_bf16("Va", [RANKS, TOK, NVH * HD], "ExternalInput")
    io["G"] = kb.dram("G", [128, NT * 12], F32, "ExternalInput")
    io["tabh"] = kb.dram_bf16("tabh", [128, TABH_COLS], "ExternalInput")
    io["tabf"] = kb.dram("tabf", [128, TABF_COLS], F32, "ExternalInput")
    io["w1p"] = kb.dram("w1p", [128, 32 * 128], F32, "ExternalInput")
    io["w2p"] = kb.dram("w2p", [128, 128], F32, "ExternalInput")
    io["peT"] = kb.dram("peT", [128, 32], F32, "ExternalInput")
    io["gkc"] = kb.dram("gkc", [128, HD], F32, "ExternalInput")
    for n in ("QT", "KTa", "Va", "G"):
        io[n + "_b"] = kb.buf(n)
    io["KTa_fn"] = lambda rp, kbi: io["KTa"][rp, kbi]
    io["KTa_bf"] = lambda kbi: io["KTa_b"]
    io["Va_fn"] = lambda rp, q, vh: io["Va"][rp, q * 512:(q + 1) * 512, vh * HD:(vh + 1) * HD]
    io["Va_bf"] = lambda q: io["Va_b"]


def build_B_debug():
    kb = KB()
    io = {}
    declare_mixer_io(kb, io)
    cat_out = kb.dram_bf16("cat_out", [128, NT * D_CAT], "ExternalOutput")
    cat_out_b = kb.buf("cat_out")
    cat, cat_b = kb.sbt("cat", [128, NT, D_CAT], BF16)
    kb.op("pool", lambda e: e.memset(cat[:], 0.0), writes=[cat_b])
    emit_mixers(kb, io, cat, cat_b)
    kb.dma("sp", cat_out, cat[:].rearrange("p j d -> p (j d)"), reads=[cat_b], writes=[cat_out_b])
    kb.finish([cat_out_b])
    kb.close()
    return kb


def emit_phase_C(kb, io, cat, cat_b, x_src, x_dst):
    bank, bank_b = kb.bank, kb.bank_b
    kb.push_scope()
    xres, xres_b = kb.sbt("C_xres", [128, NT, D_MODEL], F32)
    xres_bj = [kb.buf("C_xres%d" % j) for j in range(NT)]
    h2T = kb.sb("C_h2T", [128, DC, TOK], BF16)
    h2T_b = [kb.buf("C_h2T%d" % g) for g in range(NG)]
    gm, cst_b = kb.sbt("C_gm", [128, DC], F32)
    ident, ident_b = kb.sbt("C_ident", [128, 128], F32)
    identb, identb_b = kb.sbt("C_identb", [128, 128], BF16)
    kb.dma("sp", gm[:], io["gmlp"], writes=[cst_b])
    kb.dma("sp", ident[:], io["ident"], writes=[ident_b])
    kb.op("dve", lambda e: e.tensor_copy(out=identb[:], in_=ident[:]), reads=[ident_b], writes=[identb_b])

    kb.push_scope()
    catT, catT_b = kb.sbt("C_catT", [128, 6, 128], BF16), None
    catT = [kb.sbt("C_catT%d" % i, [128, 6, 128], BF16) for i in range(2)]
    Wo, Wo_b = kb.sbt("C_Wo", [128, 6, D_MODEL], BF16)
    xt = [kb.sbt("C_xt%d" % i, [128, D_MODEL], F32) for i in range(2)]
    xn = [kb.sbt("C_xn%d" % i, [128, D_MODEL], F32) for i in range(2)]
    junk, junk_b = kb.sbt("C_junk", [128, D_MODEL], F32)
    st = [kb.sbt("C_st%d" % i, [128, 4], F32) for i in range(2)]
    for c in range(6):
        kb.dma("pool", Wo[:, c, :], io["wout"][c * 128:(c + 1) * 128, :], writes=[Wo_b])
    for j in range(NT):
        s = j % 2
        g = j // 4
        src_ap, src_b = x_src(j)
        kb.dma("sp", xt[s][0][:], src_ap, reads=[src_b] if src_b else [], writes=[xt[s][1]])
        for c in range(6):
            kb.op("pe", lambda e: e.transpose(out=kb.bankh[:, c * 128:(c + 1) * 128],
                                              in_=cat[:, j, c * 128:(c + 1) * 128], identity=identb[:]),
                  reads=[cat_b, identb_b], writes=[kb.bankh_b])
        ct, ctb = catT[s]
        kb.op("act", lambda e: e.activation(out=ct[:], in_=kb.bankh[:, 0:768].rearrange("p (c t) -> p c t", t=128),
                                            func=AF.Copy), reads=[kb.bankh_b], writes=[ctb])
        for hf in range(2):
            p, pb = bank[hf], bank_b[hf]
            for c in range(6):
                kb.op("pe", lambda e: e.matmul(p[:, :], lhsT=ct[:, c, :], rhs=Wo[:, c, hf * 512:(hf + 1) * 512],
                                               start=(c == 0), stop=(c == 5)),
                      reads=[ctb, Wo_b], writes=[pb])
            kb.op("dve", lambda e: e.tensor_tensor(out=xres[:, j, hf * 512:(hf + 1) * 512], in0=p[:, :],
                                                   in1=xt[s][0][:, hf * 512:(hf + 1) * 512], op=ALU.add),
                  reads=[pb, xt[s][1]], writes=[xres_bj[j]])
        kb.op("act", lambda e: e.activation(out=junk[:], in_=xres[:, j, :], func=AF.Square,
                                            accum_out=st[s][0][:, 0:1]),
              reads=[xres_bj[j]], writes=[junk_b, st[s][1]])
        kb.op("dve", lambda e: e.tensor_scalar(out=st[s][0][:, 1:2], in0=st[s][0][:, 0:1], scalar1=1.0 / D_MODEL,
                                               scalar2=RMS_EPS, op0=ALU.mult, op1=ALU.add),
              reads=[st[s][1]], writes=[st[s][1]])
        kb.op("act", lambda e: e.activation(out=st[s][0][:, 2:3], in_=st[s][0][:, 1:2], func=AF.Sqrt),
              reads=[st[s][1]], writes=[st[s][1]])
        kb.op("dve", lambda e: e.reciprocal(out=st[s][0][:, 3:4], in_=st[s][0][:, 2:3]),
              reads=[st[s][1]], writes=[st[s][1]])
        kb.op("dve", lambda e: e.tensor_scalar(out=xn[s][0][:], in0=xres[:, j, :], scalar1=st[s][0][:, 3:4],
                                               scalar2=None, op0=ALU.mult),
              reads=[xres_bj[j], st[s][1]], writes=[xn[s][1]])
        for hf in range(2):
            p, pb = bank[2 + hf], bank_b[2 + hf]
            for cc in range(4):
                c = hf * 4 + cc
                kb.op("pe", lambda e: e.transpose(out=p[:, cc * 128:(cc + 1) * 128],
                                                  in_=xn[s][0][:, c * 128:(c + 1) * 128], identity=ident[:]),
                      reads=[xn[s][1], ident_b], writes=[pb])
            for cc in range(4):
                c = hf * 4 + cc
                if cc % 2 == 0:
                    kb.op("act", lambda e: e.activation(out=h2T[:, c, j * 128:(j + 1) * 128],
                                                        in_=p[:, cc * 128:(cc + 1) * 128], func=AF.Copy,
                                                        scale=gm[:, c:c + 1]),
                          reads=[pb, cst_b], writes=[h2T_b[g]])
                else:
                    kb.op("dve", lambda e: e.tensor_scalar(out=h2T[:, c, j * 128:(j + 1) * 128],
                                                           in0=p[:, cc * 128:(cc + 1) * 128],
                                                           scalar1=gm[:, c:c + 1], scalar2=None, op0=ALU.mult),
                          reads=[pb, cst_b], writes=[h2T_b[g]])
    kb.pop_scope()

    kb.push_scope()
    NE = 8
    FE = D_FF // NE
    Wu = [kb.sbt("C_Wu%d" % i, [128, DC, FE], BF16) for i in range(2)]
    Wd = [kb.sbt("C_Wd%d" % i, [128, FE // 128, D_MODEL], BF16) for i in range(2)]
    actT = [kb.sbt("C_act%d" % i, [128, FE // 128, 512], BF16) for i in range(2)]
    rl = [kb.sbt("C_rl%d" % i, [128, 512], F32) for i in range(2)]
    n = 0
    for e8 in range(NE):
        ws = e8 % 2
        for c in range(DC):
            kb.dma("pool", Wu[ws][0][:, c, :], io["wup"][c * 128:(c + 1) * 128, e8 * FE:(e8 + 1) * FE],
                   writes=[Wu[ws][1]])
        for fc in range(FE // 128):
            r0 = e8 * FE + fc * 128
            kb.dma("pool", Wd[ws][0][:, fc, :], io["wdown"][r0:r0 + 128, :], writes=[Wd[ws][1]])
        for g in range(NG):
            at, atb = actT[g % 2]
            for fc in range(FE // 128):
                p, pb = bank[n % 2], bank_b[n % 2]
                rt, rtb = rl[n % 2]
                n += 1
                for c in range(DC):
                    kb.op("pe", lambda e: e.matmul(p[:, :], lhsT=Wu[ws][0][:, c, fc * 128:(fc + 1) * 128],
                                                   rhs=h2T[:, c, g * 512:(g + 1) * 512],
                                                   start=(c == 0), stop=(c == DC - 1)),
                          reads=[Wu[ws][1], h2T_b[g]], writes=[pb])
                kb.op("act", lambda e: e.activation(out=rt[:], in_=p[:, :], func=AF.Relu), reads=[pb], writes=[rtb])
                kb.op("dve", lambda e: e.tensor_tensor(out=at[:, fc, :], in0=rt[:], in1=rt[:], op=ALU.mult),
                      reads=[rtb], writes=[atb])
            for a in range(4):
                j = 4 * g + a
                for hf in range(2):
                    p, pb = bank[2 + (2 * a + hf) % 4], bank_b[2 + (2 * a + hf) % 4]
                    for fc in range(FE // 128):
                        kb.op("pe", lambda e: e.matmul(p[:, :], lhsT=at[:, fc, a * 128:(a + 1) * 128],
                                                       rhs=Wd[ws][0][:, fc, hf * 512:(hf + 1) * 512],
                                                       start=(fc == 0), stop=(fc == FE // 128 - 1)),
                              reads=[atb, Wd[ws][1]], writes=[pb])
                    kb.op("dve", lambda e: e.tensor_tensor(out=xres[:, j, hf * 512:(hf + 1) * 512], in0=p[:, :],
                                                           in1=xres[:, j, hf * 512:(hf + 1) * 512], op=ALU.add),
                          reads=[pb, xres_bj[j]], writes=[xres_bj[j]])
                if e8 == NE - 1:
                    dst_ap, dst_b = x_dst(j)
                    kb.dma("sp", dst_ap, xres[:, j, :], reads=[xres_bj[j]], writes=[dst_b])
    kb.pop_scope()
    kb.pop_scope()


def build_B():
    kb = KB()
    io = {}
    declare_mixer_io(kb, io)
    x = kb.dram("x_own", [TOK, D_MODEL], F32, "ExternalInput")
    xo = kb.dram("x_out", [TOK, D_MODEL], F32, "ExternalOutput")
    xo_b = kb.buf("x_out")
    io["wout"] = kb.dram("wout", [D_CAT, D_MODEL], F32, "ExternalInput")
    io["wup"] = kb.dram("wup", [D_MODEL, D_FF], F32, "ExternalInput")
    io["wdown"] = kb.dram("wdown", [D_FF, D_MODEL], F32, "ExternalInput")
    io["gmlp"] = kb.dram("gmlp", [128, DC], F32, "ExternalInput")
    io["ident"] = kb.dram("ident", [128, 128], F32, "ExternalInput")
    cat, cat_b = kb.sbt("cat", [128, NT, D_CAT], BF16)
    emit_mixers(kb, io, cat, cat_b)
    emit_phase_C(kb, io, cat, cat_b, lambda j: (x[j * 128:(j + 1) * 128, :], None),
                 lambda j: (xo[j * 128:(j + 1) * 128, :], xo_b))
    kb.finish([xo_b])
    kb.close()
    return kb


PIECE = 256
NPIECE = 9
PIECE_ORDER = (0, 1, 5, 6, 7, 8, 2, 3, 4)
LAYER_W = ("winp", "gmix", "gfm", "w1p", "w2p", "peT", "gkc", "wout", "wup", "wdown", "gmlp")


def build_fused():
    kb = KB()
    nc = kb.nc
    x = kb.dram("x_own", [TOK, D_MODEL], F32, "ExternalInput")
    xo = kb.dram("x_out", [TOK, D_MODEL], F32, "ExternalOutput")
    xo_b = kb.buf("x_out")
    shapes = {"winp": [D_MODEL, WP], "gmix": [128, DC], "gfm": [128, NFM], "w1p": [128, 4096], "w2p": [128, 128],
              "peT": [128, 32], "gkc": [128, HD], "wout": [D_CAT, D_MODEL], "wup": [D_MODEL, D_FF],
              "wdown": [D_FF, D_MODEL], "gmlp": [128, DC]}
    ext = {n: kb.dram(n, [DEPTH] + shapes[n], F32, "ExternalInput") for n in LAYER_W}
    ident = kb.dram("ident", [128, 128], F32, "ExternalInput")
    tabh = kb.dram_bf16("tabh", [128, TABH_COLS], "ExternalInput")
    tabf = kb.dram("tabf", [128, TABF_COLS], F32, "ExternalInput")
    QT = kb.dram("QT_i", [NQB, 128, TOK], BF16, "Internal")
    G = kb.dram("G_i", [128, NT * 12], F32, "Internal")
    xmid = kb.dram("xmid_i", [TOK, D_MODEL], F32, "Internal")
    xmid_b = kb.buf("xmid")
    gbo = [[kb.dram("gbo%d_%d" % (l, i), [PIECE, 2048], BF16, "Internal") for i in range(NPIECE)]
           for l in range(DEPTH)]
    gba = [[kb.dram("gba%d_%d" % (l, i), [RANKS * PIECE, 2048], BF16, "Internal") for i in range(NPIECE)]
           for l in range(DEPTH)]
    cat, cat_b = kb.sbt("cat", [128, NT, D_CAT], BF16)
    QT_b, G_b = kb.buf("QT"), kb.buf("G")
    for l in range(DEPTH):
        io = {n: ext[n][l] for n in LAYER_W}
        io["ident"], io["tabh"], io["tabf"] = ident, tabh, tabf
        io["QT"], io["QT_b"], io["G"], io["G_b"] = QT, QT_b, G, G_b
        gbo_b = [kb.buf("gbo%d_%d" % (l, i)) for i in range(NPIECE)]
        gba_b = [kb.buf("gba%d_%d" % (l, i)) for i in range(NPIECE)]
        go, ga = gbo[l], gba[l]
        io["KT_fn"] = lambda ki, go=go: go[ki // 2][(ki % 2) * 128:(ki % 2) * 128 + 128, :]
        io["KT_bf"] = lambda ki, gbo_b=gbo_b: gbo_b[ki // 2]
        io["V_fn"] = lambda jj, go=go: go[5 + jj // 4][:, :].rearrange("a (u d) -> (a u) d", d=1024)[
            (jj % 4) * 128:(jj % 4) * 128 + 128, :]
        io["V_cols"] = 1024
        io["zero_fill"] = [(go[4][128:256, :], gbo_b[4])]
        io["V_bf"] = lambda jj, gbo_b=gbo_b: gbo_b[5 + jj // 4]
        io["KTa_fn"] = lambda rp, kbi, ga=ga: ga[kbi // 2][rp * PIECE + (kbi % 2) * 128:
                                                         rp * PIECE + (kbi % 2) * 128 + 128, :]
        io["KTa_bf"] = lambda kbi, gba_b=gba_b: gba_b[kbi // 2]
        io["Va_fn"] = lambda rp, q, vh, ga=ga: ga[5 + q][rp * PIECE:(rp + 1) * PIECE, :] \
            .rearrange("a (u d) -> (a u) d", d=1024)[:, vh * HD:(vh + 1) * HD]
        io["Va_bf"] = lambda q, gba_b=gba_b: gba_b[5 + q]
        if l == 0:
            x_src = lambda j: (x[j * 128:(j + 1) * 128, :], None)
        else:
            x_src = lambda j: (xmid[j * 128:(j + 1) * 128, :], xmid_b)
        if l == DEPTH - 1:
            x_dst = lambda j: (xo[j * 128:(j + 1) * 128, :], xo_b)
        else:
            x_dst = lambda j: (xmid[j * 128:(j + 1) * 128, :], xmid_b)
        emit_phase_A(kb, x_src, None, io)
        for i in PIECE_ORDER:
            if "nocc" in DBG:
                for rr in range(RANKS):
                    kb.dma("sp", ga[i][rr * PIECE:(rr + 1) * PIECE, :], go[i][:, :], reads=[gbo_b[i]],
                           writes=[gba_b[i]])
                continue
            kb.dma("pool", None, None, reads=[gbo_b[i]], writes=[gba_b[i]],
                   fn=lambda e, i=i: e.collective_compute(
                       "AllGather", ALU.bypass, replica_groups=[[0, 1, 2, 3], [4, 5, 6, 7]],
                       ins=[go[i][:, :]], outs=[ga[i][:, :]]))
        emit_mixers(kb, io, cat, cat_b)
        emit_phase_C(kb, io, cat, cat_b, x_src, x_dst)
    kb.finish([xo_b])
    kb.close()
    return kb


_PROGS = {}
LAST_EXEC_NS = None


def _prog(name):
    if name not in _PROGS:
        _PROGS[name] = {"A": build_A, "B": build_B, "F": build_fused}[name]()
    return _PROGS[name]


def kernel_unfused(x, norm_mix, norm_mlp, w_in, qk_gain_nsa, qk_gain_dil, cmp_pe, cmp_w1, cmp_w2, w_out, w_up, w_down):
    f32 = lambda a: np.ascontiguousarray(np.asarray(a, dtype=np.float32))
    x = f32(x)
    perm = win_column_perm()
    ident = np.eye(128, dtype=np.float32)
    rows = [core_rows(c) for c in range(NCORE)]
    tabs = [host_table_arrays(r) for r in range(RANKS)]
    xc = [np.ascontiguousarray(x[b, idx]) for (b, r, idx) in rows]
    for l in range(DEPTH):
        gmix, gfm = host_consts_A(f32(norm_mix[l]), f32(qk_gain_nsa[l]), f32(qk_gain_dil[l]))
        winp = np.ascontiguousarray(f32(w_in[l])[:, perm])
        resA = run_bass_kernel_spmd(_prog("A").nc, [
            {"x_own": xc[c], "winp": winp, "gmix": gmix, "gfm": gfm, "ident": ident} for c in range(NCORE)],
            core_ids=list(range(NCORE))).results
        w1p = np.ascontiguousarray(f32(cmp_w1[l]).reshape(2, 32, 64, 128).transpose(0, 2, 1, 3).reshape(128, 4096))
        w2p = np.ascontiguousarray(f32(cmp_w2[l]).transpose(1, 0, 2).reshape(128, 128))
        peT = np.ascontiguousarray(f32(cmp_pe[l]).transpose(0, 2, 1).reshape(128, 32))
        gkc = np.ascontiguousarray(np.broadcast_to(f32(qk_gain_nsa[l])[1], (128, HD)))
        gmlp = np.ascontiguousarray(f32(norm_mlp[l]).reshape(DC, 128).T)
        in_maps = []
        for c in range(NCORE):
            b, r, idx = rows[c]
            KTa = np.stack([resA[4 * b + rr]["KT"] for rr in range(RANKS)])
            Va = np.stack([resA[4 * b + rr]["V"] for rr in range(RANKS)])
            in_maps.append({"QT": resA[c]["QT"], "KTa": KTa, "Va": Va, "G": resA[c]["G"],
                            "tabh": tabs[r][0], "tabf": tabs[r][1], "w1p": w1p, "w2p": w2p, "peT": peT,
                            "gkc": gkc, "x_own": xc[c], "wout": f32(w_out[l]), "wup": f32(w_up[l]),
                            "wdown": f32(w_down[l]), "gmlp": gmlp, "ident": ident})
        resB = run_bass_kernel_spmd(_prog("B").nc, in_maps, core_ids=list(range(NCORE))).results
        xc = [resB[c]["x_out"] for c in range(NCORE)]
    out = np.empty((BATCH, SEQ, D_MODEL), np.float32)
    for c in range(NCORE):
        b, r, idx = rows[c]
        out[b, idx] = xc[c]
    return out


def kernel(x, norm_mix, norm_mlp, w_in, qk_gain_nsa, qk_gain_dil, cmp_pe, cmp_w1, cmp_w2, w_out, w_up, w_down):
    f32 = lambda a: np.ascontiguousarray(np.asarray(a, dtype=np.float32))
    x = f32(x)
    perm = win_column_perm()
    rows = [core_rows(c) for c in range(NCORE)]
    tabs = [host_table_arrays(r) for r in range(RANKS)]
    L = {n: [] for n in LAYER_W}
    for l in range(DEPTH):
        gmix, gfm = host_consts_A(f32(norm_mix[l]), f32(qk_gain_nsa[l]), f32(qk_gain_dil[l]))
        L["gmix"].append(gmix)
        L["gfm"].append(gfm)
        L["winp"].append(f32(w_in[l])[:, perm])
        L["w1p"].append(f32(cmp_w1[l]).reshape(2, 32, 64, 128).transpose(0, 2, 1, 3).reshape(128, 4096))
        L["w2p"].append(f32(cmp_w2[l]).transpose(1, 0, 2).reshape(128, 128))
        L["peT"].append(f32(cmp_pe[l]).transpose(0, 2, 1).reshape(128, 32))
        L["gkc"].append(np.broadcast_to(f32(qk_gain_nsa[l])[1], (128, HD)))
        L["gmlp"].append(f32(norm_mlp[l]).reshape(DC, 128).T)
        L["wout"].append(f32(w_out[l]))
        L["wup"].append(f32(w_up[l]))
        L["wdown"].append(f32(w_down[l]))
    shared = {n: np.ascontiguousarray(np.stack(v)).astype(np.float32) for n, v in L.items()}
    shared["ident"] = np.eye(128, dtype=np.float32)
    in_maps = []
    for c in range(NCORE):
        b, r, idx = rows[c]
        m = dict(shared)
        m["x_own"] = np.ascontiguousarray(x[b, idx])
        m["tabh"], m["tabf"] = tabs[r]
        in_maps.append(m)
    _r = run_bass_kernel_spmd(_prog("F").nc, in_maps, core_ids=list(range(NCORE)))
    global LAST_EXEC_NS
    LAST_EXEC_NS = getattr(_r, "exec_time_ns", None)
    res = _r.results
    out = np.empty((BATCH, SEQ, D_MODEL), np.float32)
    for c in range(NCORE):
        b, r, idx = rows[c]
        out[b, idx] = res[c]["x_out"]
    return out
```

```python
from contextlib import ExitStack
import numpy as np
import ml_dtypes
import concourse.bass as bass
import concourse.mybir as mybir
from concourse.bass_utils import run_bass_kernel_spmd

F32 = mybir.dt.float32
BF16 = mybir.dt.bfloat16
AF = mybir.ActivationFunctionType
ALU = mybir.AluOpType
NPBF = ml_dtypes.bfloat16

SAME_ENGINE_SYNC = True
DBG = set()


class Buf:
    __slots__ = ("name", "w", "r", "semkey", "semv")

    def __init__(self, name):
        self.name = name
        self.w = None
        self.r = {}
        self.semkey = None
        self.semv = 0


class KB:
    def __init__(self):
        self.nc = bass.Bass("TRN2", target_bir_lowering=False)
        nc = self.nc
        self.es = ExitStack()
        self.engs = {"pe": nc.tensor, "act": nc.scalar, "dve": nc.vector,
                     "pool": nc.gpsimd, "sp": nc.sync}
        self.semobj = {}
        self.cnt = {}
        self.seen = {}
        for n in self.engs:
            self.semobj[n] = self.es.enter_context(nc.semaphore("s_" + n))
            self.cnt[n] = 0
            self.seen[n] = {}
        self.nbuf = 0
        self.ninstr = 0
        self.nwait = 0
        self.root_es = self.es
        self.dmasems = {}
        self.bank = [self.es.enter_context(nc.psum_tensor("bank%d" % i, [128, 512], F32)) for i in range(7)]
        self.bank_b = [Buf("bank%d" % i) for i in range(7)]
        self.bankh = self.es.enter_context(nc.psum_tensor("bankh", [128, 1024], BF16))
        self.bankh_b = Buf("bankh")

    def push_scope(self):
        self._outer = getattr(self, "_outer", [])
        self._outer.append(self.es)
        self.es = ExitStack()

    def pop_scope(self):
        self.barrier()
        self.es.close()
        self.es = self._outer.pop()

    def barrier(self):
        deps = {n: self.cnt[n] for n in self.engs if self.cnt[n] > 0}
        deps.update(self.dmasems)
        for n in self.engs:
            self._wait(n, dict(deps))

    def sbt(self, name, shape, dt):
        return self.sb(name, shape, dt), self.buf(name)

    def sb(self, name, shape, dt):
        self.nsb = getattr(self, "nsb", 0) + 1
        return self.es.enter_context(self.nc.sbuf_tensor("%s_%d" % (name, self.nsb), list(shape), dt))

    def ps(self, name, shape, dt=F32):
        return self.es.enter_context(self.nc.psum_tensor(name, list(shape), dt))

    def dram(self, name, shape, dt, kind):
        return self.nc.dram_tensor(name, list(shape), dt, kind=kind).ap()

    def dram_bf16(self, name, shape, kind):
        shp = list(shape)
        assert shp[-1] % 2 == 0
        shp[-1] //= 2
        return self.nc.dram_tensor(name, shp, F32, kind=kind).ap().bitcast(BF16)

    def buf(self, name=None):
        self.nbuf += 1
        return Buf(name or f"b{self.nbuf}")

    def _deps(self, reads, writes):
        deps = {}
        for b in reads:
            if b.w is not None:
                k, v = b.w
                if deps.get(k, 0) < v:
                    deps[k] = v
        for b in writes:
            if b.w is not None:
                k, v = b.w
                if deps.get(k, 0) < v:
                    deps[k] = v
            for k, v in b.r.items():
                if deps.get(k, 0) < v:
                    deps[k] = v
        return deps

    def _wait(self, eng, deps):
        seen = self.seen[eng]
        e = self.engs[eng]
        for k, v in deps.items():
            if k == eng and (eng in ("pe", "sp") or not SAME_ENGINE_SYNC):
                continue
            if seen.get(k, 0) >= v:
                continue
            e.wait_ge(self.semobj[k], v)
            self.nwait += 1
            seen[k] = v

    def op(self, eng, fn, reads=(), writes=()):
        self._wait(eng, self._deps(reads, writes))
        ins = fn(self.engs[eng])
        self.cnt[eng] += 1
        tok = (eng, self.cnt[eng])
        ins.then_inc(self.semobj[eng], 1)
        self.ninstr += 1
        for b in reads:
            if b.r.get(eng, 0) < tok[1]:
                b.r[eng] = tok[1]
        for b in writes:
            b.w = tok
            b.r = {}
        return ins

    def dma(self, q, out, in_, reads=(), writes=(), fn=None, **kw):
        self._wait(q, self._deps(reads, writes))
        wb = writes[0]
        if wb.semkey is None:
            wb.semkey = "d%d_%s" % (self.nbuf, wb.name)
            self.nbuf += 1
            self.semobj[wb.semkey] = self.root_es.enter_context(self.nc.semaphore(wb.semkey))
        if fn is not None:
            ins = fn(self.engs[q])
            ins.then_inc(self.semobj[wb.semkey])
            wb.semv += 1
        else:
            ins = self.engs[q].dma_start(out=out, in_=in_, **kw)
            ins.then_inc(self.semobj[wb.semkey], 16)
            wb.semv += 16
        tok = (wb.semkey, wb.semv)
        self.dmasems[wb.semkey] = wb.semv
        self.ninstr += 1
        for b in reads:
            if b.r.get(tok[0], 0) < tok[1]:
                b.r[tok[0]] = tok[1]
        for b in writes:
            b.w = tok
            b.r = {}
        return ins

    def finish(self, outs):
        deps = {}
        for b in outs:
            if b.w is not None:
                deps[b.w[0]] = max(deps.get(b.w[0], 0), b.w[1])
        self._wait("sp", deps)

    def close(self):
        self.es.close()


D_MODEL = 1024
BATCH = 2
SEQ = 8192
DEPTH = 2
HD = 64
NCORE = 8
RANKS = 4
NT = 16
TOK = NT * 128
NG = 4
DC = D_MODEL // 128
D_PROJ = 2956
NFM = 17
NTM = 908
WP = NFM * 128 + NTM
D_FF = 4096
D_CAT = 768
RMS_EPS = 1e-6
NORM_BLOCKS = (0, 1, 2, 3, 5, 6, 7, 8, 9, 10)
Q_BLOCKS = (0, 1, 5, 6, 7, 11, 12, 13)
K_BLOCKS = (2, 3, 4, 8, 9, 10, 14, 15, 16)
NQB = len(Q_BLOCKS)
NKB = len(K_BLOCKS)
NVH = 14


def win_column_perm():
    cols = []
    A_KV = 256
    B0 = 652
    C0 = 1804
    cols += list(range(0, 128))
    cols += list(range(128, 256))
    ksel = list(range(A_KV + 128, A_KV + 192))
    kwin = list(range(A_KV + 256, A_KV + 320))
    cols += ksel + ksel
    cols += kwin + kwin
    cols += list(range(A_KV, A_KV + 128))
    for g in range(3):
        cols += list(range(B0 + g * 128, B0 + g * 128 + 128))
    for g in range(3):
        cols += list(range(B0 + (3 + g) * 128, B0 + (3 + g) * 128 + 128))
    for p in range(3):
        cols += list(range(C0 + p * 128, C0 + p * 128 + 128))
    for p in range(3):
        cols += list(range(C0 + 384 + p * 128, C0 + 384 + p * 128 + 128))
    assert len(cols) == NFM * 128
    cols += list(range(A_KV + 192, A_KV + 256))
    cols += list(range(A_KV + 320, A_KV + 384))
    cols += list(range(B0 + 768, B0 + 1152))
    cols += list(range(C0 + 768, C0 + 1152))
    cols += list(range(640, 652))
    assert len(cols) == WP
    return np.array(cols, dtype=np.int64)


def emit_phase_A(kb, x_src, xb_fn, io):
    nc = kb.nc
    kb.push_scope()
    W = kb.sb("A_W", [128, DC, WP], BF16)
    W_b = kb.buf("A_W")
    hT = kb.sb("A_hT", [128, DC, TOK], BF16)
    hT_b = [kb.buf("A_hT%d" % g) for g in range(NG)]
    gmix = kb.sb("A_gmix", [128, DC], F32)
    gfm = kb.sb("A_gfm", [128, NFM], F32)
    ident = kb.sb("A_ident", [128, 128], F32)
    blk = kb.sb("A_blk", [128, 128], BF16)
    cst_b = kb.buf("A_cst")
    blk_b = kb.buf("A_blk")
    xt = [kb.sb("A_xt%d" % i, [128, D_MODEL], F32) for i in range(2)]
    xt_b = [kb.buf("A_xt%d" % i) for i in range(2)]
    junk = kb.sb("A_junk", [128, D_MODEL], F32)
    junk_b = kb.buf("A_junk")
    xn = [kb.sb("A_xn%d" % i, [128, D_MODEL], F32) for i in range(2)]
    xn_b = [kb.buf("A_xn%d" % i) for i in range(2)]
    st = [kb.sb("A_st%d" % i, [128, 4], F32) for i in range(2)]
    st_b = [kb.buf("A_st%d" % i) for i in range(2)]
    NPS = 6
    ps = kb.bank[:NPS]
    ps_b = kb.bank_b[:NPS]
    sq = [kb.sb("A_sq%d" % i, [128, 512], BF16) for i in range(2)]
    sq_b = [kb.buf("A_sq%d" % i) for i in range(2)]
    lnb = [kb.sb("A_ln%d" % i, [128, 512], F32) for i in range(2)]
    lnb_b = [kb.buf("A_ln%d" % i) for i in range(2)]
    fo = [kb.sb("A_fo%d" % i, [128, 512], BF16) for i in range(3)]
    fo_b = [kb.buf("A_fo%d" % i) for i in range(3)]
    vo = [kb.sb("A_vo%d" % i, [128, 1024], BF16) for i in range(2)]
    vo_b = [kb.buf("A_vo%d" % i) for i in range(2)]
    go = kb.sb("A_go", [128, NT * 12], F32)
    go_b = kb.buf("A_go")
    epsb = kb.sb("A_eps", [128, 1], F32)
    eps_ap = epsb[:, 0:1]

    kb.op("pool", lambda e: e.memset(epsb[:], RMS_EPS), writes=[cst_b])
    for i in range(2):
        kb.op("pool", lambda e: e.memset(vo[i][:, 896:1024], 0.0), writes=[vo_b[i]])
    if io.get("zero_fill"):
        zt = kb.sb("A_zero", [128, 2048], BF16)
        zt_b = kb.buf("A_zero")
        kb.op("pool", lambda e: e.memset(zt[:], 0.0), writes=[zt_b])
        for (zap, zb) in io["zero_fill"]:
            kb.dma("sp", zap, zt[:], reads=[zt_b], writes=[zb])
    kb.dma("sp", gmix[:], io["gmix"], writes=[cst_b])
    kb.dma("sp", gfm[:], io["gfm"], writes=[cst_b])
    kb.dma("sp", ident[:], io["ident"], writes=[cst_b])
    kb.op("pool", lambda e: e.memset(blk[:], 0.0), writes=[blk_b])
    kb.op("pool", lambda e: e.memset(blk[0:64, 0:64], 1.0), writes=[blk_b])
    kb.op("pool", lambda e: e.memset(blk[64:128, 64:128], 1.0), writes=[blk_b])
    half = WP // 2
    for c in range(DC):
        for h0 in (0, half):
            kb.dma("pool", W[:, c, h0:h0 + half], io["winp"][c * 128:(c + 1) * 128, h0:h0 + half],
                   writes=[W_b])

    psi = [0]

    def next_ps():
        i = psi[0] % NPS
        psi[0] += 1
        return ps[i], ps_b[i]

    cnt = {"sq": 0, "fo": 0, "vo": 0}

    def tile_work(j):
        s = j % 2
        src_ap, src_b = x_src(j)
        kb.dma("pool", xt[s][:], src_ap, reads=[src_b] if src_b else [], writes=[xt_b[s]])
        kb.op("act", lambda e: e.activation(out=junk[:], in_=xt[s][:], func=AF.Square,
                                            accum_out=st[s][:, 0:1]),
              reads=[xt_b[s]], writes=[junk_b, st_b[s]])
        kb.op("dve", lambda e: e.tensor_scalar(out=st[s][:, 1:2], in0=st[s][:, 0:1],
                                               scalar1=1.0 / D_MODEL, scalar2=RMS_EPS,
                                               op0=ALU.mult, op1=ALU.add),
              reads=[st_b[s]], writes=[st_b[s]])
        kb.op("act", lambda e: e.activation(out=st[s][:, 2:3], in_=st[s][:, 1:2], func=AF.Sqrt),
              reads=[st_b[s]], writes=[st_b[s]])
        kb.op("dve", lambda e: e.reciprocal(out=st[s][:, 3:4], in_=st[s][:, 2:3]),
              reads=[st_b[s]], writes=[st_b[s]])
        kb.op("dve", lambda e: e.tensor_scalar(out=xn[s][:], in0=xt[s][:], scalar1=st[s][:, 3:4],
                                               scalar2=None, op0=ALU.mult),
              reads=[xt_b[s], st_b[s]], writes=[xn_b[s]])
        g = j // 4
        for hf in range(2):
            p, pb = next_ps()
            for cc in range(4):
                c = hf * 4 + cc
                kb.op("pe", lambda e: e.transpose(out=p[:, cc * 128:(cc + 1) * 128],
                                                  in_=xn[s][:, c * 128:(c + 1) * 128],
                                                  identity=ident[:]),
                      reads=[xn_b[s], cst_b], writes=[pb])
            for cc in range(4):
                c = hf * 4 + cc
                eng = "act" if cc % 2 == 0 else "dve"
                if eng == "act":
                    kb.op("act", lambda e: e.activation(out=hT[:, c, j * 128:(j + 1) * 128],
                                                        in_=p[:, cc * 128:(cc + 1) * 128],
                                                        func=AF.Copy, scale=gmix[:, c:c + 1]),
                          reads=[pb, cst_b], writes=[hT_b[g]])
                else:
                    kb.op("dve", lambda e: e.tensor_scalar(out=hT[:, c, j * 128:(j + 1) * 128],
                                                           in0=p[:, cc * 128:(cc + 1) * 128],
                                                           scalar1=gmix[:, c:c + 1], scalar2=None,
                                                           op0=ALU.mult),
                          reads=[pb, cst_b], writes=[hT_b[g]])

    def proj_gen(g):
        t0 = g * 512
        for blk_i in range(NFM):
            if "nofm" in DBG:
                break
            p, pb = next_ps()
            for c in range(DC):
                kb.op("pe", lambda e: e.matmul(p[:, :], lhsT=W[:, c, blk_i * 128:(blk_i + 1) * 128],
                                               rhs=hT[:, c, t0:t0 + 512],
                                               start=(c == 0), stop=(c == DC - 1)),
                      reads=[W_b, hT_b[g]], writes=[pb])
            fi = cnt["fo"] % 3
            cnt["fo"] += 1
            if blk_i in NORM_BLOCKS:
                si = cnt["sq"] % 2
                cnt["sq"] += 1
                kb.op("act", lambda e: e.activation(out=sq[si][:], in_=p[:, :], func=AF.Square),
                      reads=[pb], writes=[sq_b[si]])
                p2, p2b = next_ps()
                kb.op("pe", lambda e: e.matmul(p2[:, :], lhsT=blk[:], rhs=sq[si][:],
                                               start=True, stop=True),
                      reads=[blk_b, sq_b[si]], writes=[p2b])
                kb.op("act", lambda e: e.activation(out=lnb[si][:], in_=p2[:, :], func=AF.Ln,
                                                    scale=1.0 / HD, bias=eps_ap),
                      reads=[p2b, cst_b], writes=[lnb_b[si]])
                kb.op("act", lambda e: e.activation(out=lnb[si][:], in_=lnb[si][:], func=AF.Exp,
                                                    scale=-0.5),
                      reads=[lnb_b[si]], writes=[lnb_b[si]])
                kb.op("dve", lambda e: e.scalar_tensor_tensor(out=fo[fi][:], in0=p[:, :],
                                                              scalar=gfm[:, blk_i:blk_i + 1],
                                                              in1=lnb[si][:], op0=ALU.mult,
                                                              op1=ALU.mult),
                      reads=[pb, cst_b, lnb_b[si]], writes=[fo_b[fi]])
            else:
                kb.op("dve", lambda e: e.tensor_copy(out=fo[fi][:], in_=p[:, :]),
                      reads=[pb], writes=[fo_b[fi]])
            if blk_i in Q_BLOCKS:
                qi = Q_BLOCKS.index(blk_i)
                kb.dma("sp", io["QT"][qi, :, t0:t0 + 512], fo[fi][:], reads=[fo_b[fi]],
                       writes=[io["QT_b"]])
            else:
                ki = K_BLOCKS.index(blk_i)
                kb.dma("sp", io["KT_fn"](ki)[:, t0:t0 + 512], fo[fi][:], reads=[fo_b[fi]],
                       writes=[io["KT_bf"](ki)])
            yield
        for a in range(4):
            if "notm" in DBG:
                break
            jj = g * 4 + a
            vi = cnt["vo"] % 2
            cnt["vo"] += 1
            for half_i in range(2):
                c0 = NFM * 128 + half_i * 454
                p, pb = next_ps()
                for c in range(DC):
                    kb.op("pe", lambda e: e.matmul(p[:, 0:454], lhsT=hT[:, c, jj * 128:(jj + 1) * 128],
                                                   rhs=W[:, c, c0:c0 + 454],
                                                   start=(c == 0), stop=(c == DC - 1)),
                          reads=[W_b, hT_b[g]], writes=[pb])
                if half_i == 0:
                    kb.op("act", lambda e: e.activation(out=vo[vi][:, 0:454], in_=p[:, 0:454],
                                                        func=AF.Copy),
                          reads=[pb], writes=[vo_b[vi]])
                else:
                    kb.op("dve", lambda e: e.tensor_copy(out=vo[vi][:, 454:896], in_=p[:, 0:442]),
                          reads=[pb], writes=[vo_b[vi]])
                    kb.op("act", lambda e: e.activation(out=go[:, jj * 12:(jj + 1) * 12],
                                                        in_=p[:, 442:454], func=AF.Sigmoid),
                          reads=[pb], writes=[go_b])
            if "nov" not in DBG:
                kb.dma("sp", io["V_fn"](jj), vo[vi][:, 0:io.get("V_cols", 896)], reads=[vo_b[vi]],
                       writes=[io["V_bf"](jj)])
            if jj == NT - 1:
                kb.dma("sp", io["G"], go[:], reads=[go_b], writes=[io["G_b"]])
            yield

    for j in range(4):
        tile_work(j)
    for g in range(NG):
        if "noproj" in DBG:
            for a in range(4):
                if g + 1 < NG:
                    tile_work(4 * (g + 1) + a)
            continue
        nxt = [4 * (g + 1) + a for a in range(4)] if g + 1 < NG else []
        for k, _ in enumerate(proj_gen(g)):
            if k in (2, 7, 12, 17) and nxt:
                tile_work(nxt.pop(0))
        for j in nxt:
            tile_work(j)
    kb.pop_scope()


def core_rows(c):
    b, r = divmod(c, RANKS)
    idx = np.concatenate([np.arange((4 * j + r) * 128, (4 * j + r + 1) * 128) for j in range(NT)])
    return b, r, idx


def host_consts_A(norm_mix_l, qk_gain_nsa_l, qk_gain_dil_l):
    gmix = np.ascontiguousarray(norm_mix_l.reshape(DC, 128).T)
    gfm = np.ones((128, NFM), np.float32)
    two = lambda v: np.concatenate([v, v])
    gfm[:, 0] = two(qk_gain_nsa_l[0])
    gfm[:, 1] = two(qk_gain_nsa_l[0])
    gfm[:, 2] = two(qk_gain_nsa_l[2])
    gfm[:, 3] = two(qk_gain_nsa_l[3])
    for g in range(3):
        gfm[:, 5 + g] = two(qk_gain_dil_l[0])
        gfm[:, 8 + g] = two(qk_gain_dil_l[1])
    return gmix, gfm


def build_A():
    kb = KB()
    io = {}
    x = kb.dram("x_own", [TOK, D_MODEL], F32, "ExternalInput")
    io["winp"] = kb.dram("winp", [D_MODEL, WP], F32, "ExternalInput")
    io["gmix"] = kb.dram("gmix", [128, DC], F32, "ExternalInput")
    io["gfm"] = kb.dram("gfm", [128, NFM], F32, "ExternalInput")
    io["ident"] = kb.dram("ident", [128, 128], F32, "ExternalInput")
    io["QT"] = kb.dram_bf16("QT", [NQB, 128, TOK], "ExternalOutput")
    io["KT"] = kb.dram_bf16("KT", [NKB, 128, TOK], "ExternalOutput")
    io["V"] = kb.dram_bf16("V", [TOK, NVH * 64], "ExternalOutput")
    io["G"] = kb.dram("G", [128, NT * 12], F32, "ExternalOutput")
    for n in ("QT", "KT", "V", "G"):
        io[n + "_b"] = kb.buf(n)
    io["KT_bf"] = lambda ki: io["KT_b"]
    io["V_bf"] = lambda jj: io["V_b"]
    io["KT_fn"] = lambda ki: io["KT"][ki]
    io["V_fn"] = lambda jj: io["V"][jj * 128:(jj + 1) * 128, :]
    emit_phase_A(kb, lambda j: (x[j * 128:(j + 1) * 128, :], None), None, io)
    kb.finish([io[n + "_b"] for n in ("QT", "KT", "V", "G")])
    kb.close()
    return kb


DIL_PAIRS = ((128, 1), (512, 4), (2048, 16))
WIN_NSA = 512


def alibi_slopes_np():
    i = np.arange(1, 11, dtype=np.float64)
    return np.exp2(-8.0 * i / 10.0)


def dil_combos(g):
    w = DIL_PAIRS[g][0] // 128
    out = []
    for dj in range(0, (w + 3) // 4 + 1):
        for rp in range(4):
            if any(0 <= 4 * dj + r - rp <= w for r in range(4)):
                out.append((dj, rp))
    return out


DIL_COMBOS = [dil_combos(g) for g in range(3)]
N_DIL_TAB = sum(len(c) for c in DIL_COMBOS) * 2


def bf16_pack(a):
    a = np.ascontiguousarray(np.asarray(a, dtype=np.float32).astype(NPBF))
    return a.view(np.float32)


def host_tables(r):
    sl = alibi_slopes_np()
    sl_dil, sl_nsa = sl[:6], sl[6:]
    ki = np.arange(128)[:, None].astype(np.float64)
    qi = np.arange(128)[None, :].astype(np.float64)
    T = {}
    ms = np.zeros((128, 4, 128))
    mc = np.zeros((128, 4, 128))
    for rp in range(4):
        if rp < r:
            ms[:, rp] = 1
            mc[:, rp] = 1
        elif rp == r:
            ms[:, rp] = (ki < qi)
            mc[:, rp] = (ki <= qi)
    T["mstrict"] = bf16_pack(ms.reshape(128, 512))
    T["mcausal"] = bf16_pack(mc.reshape(128, 512))
    tw = np.zeros((128, 4, 2, 4, 128))
    for h in range(4):
        for dj in range(2):
            for rp in range(4):
                d = 128 * (4 * dj + r - rp) + qi - ki
                tw[:, h, dj, rp] = ((d >= 0) & (d < WIN_NSA)) * np.exp(-sl_nsa[h] * np.maximum(d, 0))
    T["twin"] = bf16_pack(tw.reshape(128, -1))
    td = np.zeros((128, N_DIL_TAB, 128))
    idx = 0
    for g, (w, rd) in enumerate(DIL_PAIRS):
        for hh in range(2):
            for (dj, rp) in DIL_COMBOS[g]:
                d = 128 * (4 * dj + r - rp) + qi - ki
                ok = (d >= 0) & (d <= w) & (np.mod(d, rd) == 0)
                td[:, idx] = ok * np.exp(-sl_dil[2 * g + hh] * np.maximum(d, 0))
                idx += 1
    T["tdil"] = bf16_pack(td.reshape(128, -1))
    cm = np.zeros((128, 2, 4, 128))
    ni = ki
    for dl in range(2):
        for a in range(4):
            cm[:, dl, a] = (2048 * dl + 128 * (4 * a + r) + qi - 16 * ni - 31 >= 0)
    T["cmask"] = bf16_pack(cm.reshape(128, -1))
    bs = np.zeros((128, 4, 4, 128), np.float32)
    qcol = np.arange(128)[:, None]
    jj = np.arange(128)[None, :]
    for i in range(4):
        for a in range(4):
            cur = 32 * i + 8 * a + 2 * r + (qcol >= 64)
            b = np.where((jj == cur) | (jj == cur - 1), 1e4, 0.0) + np.where(jj == 0, 1e4, 0.0)
            b = np.where(jj > cur, -1e30, b)
            bs[:, i, a] = b
    T["bsel"] = bs.reshape(128, -1)
    u = np.arange(64)[None, :]
    bsl = np.zeros((128, 4, 64), np.float32)
    bcm = np.zeros((128, 4, 4), np.float32)
    for h in range(4):
        bsl[:, h] = sl_nsa[h] * (128 * (u - 48) + ki)
        for dl in range(4):
            bcm[:, h, dl] = (sl_nsa[h] * (-2048 * dl + 16 * ni + 31))[:, 0]
    T["bias_sel"] = bsl.reshape(128, -1)
    T["bias_cmp"] = bcm.reshape(128, -1)
    ov = np.zeros((128, 4, 128))
    for nt in range(4):
        n = 128 * nt + np.arange(128)[:, None]
        ov[:, nt] = (16 * n <= 64 * jj + 63) & (16 * n + 31 >= 64 * jj)
    T["ov"] = bf16_pack(ov.reshape(128, -1))
    e32 = np.zeros((128, 32, 128))
    kk = np.arange(128)[None, :]
    jj32 = (np.arange(128) % 64)[:, None]
    for uu in range(32):
        e32[:, uu] = 32768.0 * (jj32 == 2 * uu + kk // 64)
    T["e32"] = bf16_pack(e32.reshape(128, -1))
    tri = (np.arange(128)[:, None] >= np.arange(128)[None, :]).astype(np.float64)
    T["tri"] = bf16_pack(tri)
    T["identb"] = bf16_pack(np.eye(128))
    return T


TABLE_SHAPES = {
    "mstrict": (512, True), "mcausal": (512, True), "twin": (4 * 2 * 4 * 128, True),
    "tdil": (N_DIL_TAB * 128, True), "cmask": (2 * 512, True), "bsel": (4 * 4 * 128, False),
    "bias_sel": (256, False), "bias_cmp": (16, False), "ov": (512, True), "e32": (32 * 128, True),
    "tri": (128, True), "identb": (128, True),
}

TABH_ORDER = ("mstrict", "mcausal", "twin", "tdil", "cmask", "ov", "e32", "tri", "identb")
TABF_ORDER = ("bsel", "bias_sel", "bias_cmp")
TABH_OFF = {}
_o = 0
for _n in TABH_ORDER:
    TABH_OFF[_n] = _o
    _o += TABLE_SHAPES[_n][0]
TABH_COLS = _o
TABF_OFF = {}
_o = 0
for _n in TABF_ORDER:
    TABF_OFF[_n] = _o
    _o += TABLE_SHAPES[_n][0]
TABF_COLS = _o


def host_table_arrays(r):
    T = host_tables(r)
    tabh = np.concatenate([T[n] for n in TABH_ORDER], axis=1)
    tabf = np.concatenate([T[n] for n in TABF_ORDER], axis=1).astype(np.float32)
    assert tabh.shape == (128, TABH_COLS // 2) and tabf.shape == (128, TABF_COLS)
    return np.ascontiguousarray(tabh), np.ascontiguousarray(tabf)


def run_pipeline(units, stages):
    n = len(units)
    S = len(stages)
    for it in range(n + S - 1):
        for s in range(S):
            idx = it - s
            if 0 <= idx < n:
                stages[s](units[idx])


def emit_mixers(kb, io, cat, cat_b):
    nc = kb.nc
    kb.push_scope()
    bank, bank_b = kb.bank, kb.bank_b
    QTs, QT_b = kb.sbt("M_QT", [128, NQB, TOK], BF16)
    tabh, tabh_b = kb.sbt("M_tabh", [128, TABH_COLS], BF16)
    tabf, tabf_b = kb.sbt("M_tabf", [128, TABF_COLS], F32)
    gs, gs_b = kb.sbt("M_G", [128, NT * 12], F32)
    ones_bf, cst_b = kb.sbt("M_ones", [128, 128], BF16)
    onec = kb.sb("M_onec", [128, 1], F32)
    ocomb, ocomb_b = kb.sbt("M_ocomb", [128, NT, 4, HD], F32)
    negselT, negselT_b = kb.sbt("M_negselT", [128, 2, TOK], BF16)
    kslot = [kb.sbt("M_k%d" % i, [128, RANKS, TOK], BF16) for i in range(2)]
    vslot = [kb.sbt("M_v%d" % i, [128, RANKS * NT, HD + 1], BF16) for i in range(2)]
    cnt = {"k": 0, "v": 0}

    def tab(name, lo, n):
        o = TABH_OFF[name] + lo
        return tabh[:, o:o + n]

    def tabfp(name, lo, n):
        o = TABF_OFF[name] + lo
        return tabf[:, o:o + n]

    kb.dma("sp", QTs[:], io["QT"].rearrange("b p t -> p b t"), reads=[io["QT_b"]], writes=[QT_b])
    hc = TABH_COLS // 4
    for q in range(4):
        kb.dma("sp", tabh[:, q * hc:(q + 1) * hc], io["tabh"][:, q * hc:(q + 1) * hc], writes=[tabh_b])
    kb.dma("sp", tabf[:], io["tabf"], writes=[tabf_b])
    kb.dma("sp", gs[:], io["G"], reads=[io["G_b"]], writes=[gs_b])
    kb.op("pool", lambda e: e.memset(ones_bf[:], 1.0), writes=[cst_b])
    kb.op("pool", lambda e: e.memset(onec[:], 1.0), writes=[cst_b])
    for i in range(2):
        kb.op("pool", lambda e: e.memset(vslot[i][0][:, :, HD:HD + 1], 1.0), writes=[vslot[i][1]])

    def load_k(kbi):
        s = cnt["k"] % 2
        cnt["k"] += 1
        t, b = kslot[s]
        for rp in range(RANKS):
            kb.dma("sp", t[:, rp, :], io["KTa_fn"](rp, kbi), reads=[io["KTa_bf"](kbi)], writes=[b])
        return t, b

    def load_v(vh):
        s = cnt["v"] % 2
        cnt["v"] += 1
        t, b = vslot[s]
        for rp in range(RANKS):
            for q in range(4):
                src = io["Va_fn"](rp, q, vh)
                kb.dma("sp", t[:, rp * NT + q * 4:rp * NT + q * 4 + 4, 0:HD],
                       src.rearrange("(j p) d -> p j d", p=128), reads=[io["Va_bf"](q)], writes=[b])
        return t, b

    class U:
        pass

    def sb_units(hh, ks, vs):
        units = []
        n = 0
        for i in range(NG):
            first = True
            for jp in range(4 * i + 3, -1, -1):
                for rp in range(3, -1, -1):
                    u = U()
                    u.n = n
                    n += 1
                    u.i, u.jp, u.rp = i, jp, rp
                    u.masked = jp >= 4 * i
                    u.a0 = jp - 4 * i if u.masked else 0
                    u.c0 = 128 * u.a0
                    u.first = first
                    first = False
                    u.last = (jp == 0 and rp == 0)
                    units.append(u)
        return units

    def emit_sb_head(h, ks, vs):
        kt, ktb = ks
        vt, vtb = vs
        hp, hh = divmod(h, 2)
        rows = slice(64 * hh, 64 * hh + 64)
        qb = 5 + hp

        ZB = (0, 1, 6)

        def s1(u):
            zb, zbb = bank[ZB[u.n % 3]], bank_b[ZB[u.n % 3]]
            q0 = 512 * u.i + u.c0
            q1 = 512 * u.i + 512
            kb.op("pe", lambda e: e.matmul(zb[:, u.c0:512], lhsT=kt[rows, u.rp, u.jp * 128:(u.jp + 1) * 128],
                                           rhs=QTs[rows, qb, q0:q1], start=True, stop=True),
                  reads=[ktb, QT_b], writes=[zbb])

        def s1a(u):
            zb, zbb = bank[ZB[u.n % 3]], bank_b[ZB[u.n % 3]]
            et, etb = e_sb[u.n % 6]
            kb.op("act", lambda e: e.activation(out=et[:, u.c0:512], in_=zb[:, u.c0:512], func=AF.Exp,
                                                scale=0.125),
                  reads=[zbb], writes=[etb])

        def s1b(u):
            et, etb = e_sb[u.n % 6]
            st, stb = sp_sb[u.n % 3]
            kb.op("act", lambda e: e.activation(out=st[:, u.c0:512], in_=et[:, u.c0:512], func=AF.Ln,
                                                bias=onec[:, 0:1]),
                  reads=[etb, cst_b], writes=[stb])
            if u.masked:
                kb.op("pool", lambda e: e.tensor_tensor(out=st[:, u.c0:u.c0 + 128], in0=st[:, u.c0:u.c0 + 128],
                                                        in1=tab("mstrict", u.rp * 128, 128), op=ALU.mult),
                      reads=[stb, tabh_b], writes=[stb])

        def s2(u):
            gb, gbb = bank[2 + u.n % 2], bank_b[2 + u.n % 2]
            st, stb = sp_sb[u.n % 3]
            q0 = 512 * u.i + u.c0
            q1 = 512 * u.i + 512
            if u.first:
                kb.op("pool", lambda e: e.memset(spacc[:], 0.0), writes=[spacc_b])
            kb.op("pe", lambda e: e.matmul(gb[:, u.c0:512], lhsT=tab("tri", 0, 128), rhs=st[:, u.c0:512],
                                           start=True, stop=u.first),
                  reads=[tabh_b, stb], writes=[gbb])
            if not u.first:
                kb.op("pe", lambda e: e.matmul(gb[:, u.c0:512], lhsT=ones_bf[:], rhs=spacc[:, u.c0:512],
                                               start=False, stop=True),
                      reads=[cst_b, spacc_b], writes=[gbb])
            if not u.last:
                kb.op("dve", lambda e: e.tensor_tensor(out=spacc[:, u.c0:512], in0=spacc[:, u.c0:512],
                                                       in1=st[:, u.c0:512], op=ALU.add),
                      reads=[spacc_b, stb], writes=[spacc_b])

        def s2b(u):
            gb, gbb = bank[2 + u.n % 2], bank_b[2 + u.n % 2]
            at, atb = aT_sb[u.n % 3]
            xt, xtb = x_sb[u.n % 2]
            et, etb = e_sb[u.n % 6]
            kb.op("act", lambda e: e.activation(out=xt[:, u.c0:512], in_=gb[:, u.c0:512], func=AF.Exp,
                                                scale=-1.0),
                  reads=[gbb], writes=[xtb])
            kb.op("dve", lambda e: e.tensor_tensor(out=at[:, u.c0:512], in0=xt[:, u.c0:512], in1=et[:, u.c0:512],
                                                   op=ALU.mult),
                  reads=[xtb, etb], writes=[atb])
            if u.masked:
                kb.op("pool", lambda e: e.tensor_tensor(out=at[:, u.c0:u.c0 + 128], in0=at[:, u.c0:u.c0 + 128],
                                                        in1=tab("mstrict", u.rp * 128, 128), op=ALU.mult),
                      reads=[atb, tabh_b], writes=[atb])

        def s3(u):
            ab, abb = bank[4 + u.i % 2], bank_b[4 + u.i % 2]
            at, atb = aT_sb[u.n % 3]
            for a in range(u.a0, 4):
                kb.op("pe", lambda e: e.matmul(ab[:, a * HD:(a + 1) * HD], lhsT=at[:, a * 128:(a + 1) * 128],
                                               rhs=vt[:, u.rp * NT + u.jp, 0:HD],
                                               start=(u.first and a == u.a0), stop=(u.last and a == 3)),
                      reads=[atb, vtb], writes=[abb])
            if u.last:
                kb.op("dve", lambda e: e.tensor_copy(
                    out=cat[:, 4 * u.i:4 * u.i + 4, 384 + h * HD:384 + (h + 1) * HD],
                    in_=ab[:, 0:4 * HD].rearrange("p (a d) -> p a d", d=HD)),
                    reads=[abb], writes=[cat_b])

        run_pipeline(sb_units(hh, ks, vs), [s1, s1a, s1b, s2, s2b, s3])


    HD1 = HD + 1
    p_sb = [kb.sbt("M_p%d" % i, [128, 512], BF16) for i in range(4)]
    r4 = [kb.sbt("M_r4%d" % i, [128, 8], F32) for i in range(4)]
    r4c = [0]

    def acc_view(ab, lo, n):
        return ab[:, 0:4 * HD1].rearrange("p (a d) -> p a d", d=HD1)[:, :, lo:lo + n]

    def recip_den(ab, abb, i, gate_col):
        t, tb = r4[r4c[0] % 4]
        r4c[0] += 1
        kb.op("dve", lambda e: e.tensor_scalar(out=t[:, 0:4], in0=acc_view(ab, HD, 1), scalar1=1e-30,
                                               scalar2=None, op0=ALU.max),
              reads=[abb], writes=[tb])
        kb.op("dve", lambda e: e.reciprocal(out=t[:, 0:4], in_=t[:, 0:4]), reads=[tb], writes=[tb])
        if gate_col is not None:
            gv = gs[:, 4 * i * 12:(4 * i + 4) * 12].rearrange("p (a k) -> p a k", k=12)[:, :, gate_col]
            kb.op("dve", lambda e: e.tensor_tensor(out=t[:, 4:8], in0=t[:, 0:4], in1=gv, op=ALU.mult),
                  reads=[tb, gs_b], writes=[tb])
        return t, tb

    def evac_nsa(ab, abb, i, h, br, first):
        t, tb = recip_den(ab, abb, i, 3 * h + br)
        for a in range(4):
            src = ab[:, a * HD1:a * HD1 + HD]
            dst = ocomb[:, 4 * i + a, h, :]
            if first:
                kb.op("dve", lambda e: e.tensor_scalar(out=dst, in0=src, scalar1=t[:, 4 + a:5 + a], scalar2=None,
                                                       op0=ALU.mult),
                      reads=[abb, tb], writes=[ocomb_b])
            else:
                kb.op("dve", lambda e: e.scalar_tensor_tensor(out=dst, in0=src, scalar=t[:, 4 + a:5 + a], in1=dst,
                                                              op0=ALU.mult, op1=ALU.add),
                      reads=[abb, tb, ocomb_b], writes=[ocomb_b])
        return t, tb

    def emit_banded(kt, ktb, vt, vtb, rows, qblk, tabname, tab0, combos, bank_s, bank_acc, evac):
        units = []
        n = 0
        for i in range(NG):
            glist = []
            for a in range(4):
                j = 4 * i + a
                valid = [(ci, dj, rp) for ci, (dj, rp) in enumerate(combos) if j - dj >= 0]
                for c0 in range(0, len(valid), 4):
                    u = U()
                    u.i, u.a, u.j = i, a, j
                    u.sub = valid[c0:c0 + 4]
                    glist.append(u)
            for k, u in enumerate(glist):
                u.n = n
                n += 1
                u.first = (k == 0)
                u.last = (k == len(glist) - 1)
                units.append(u)

        def s1(u):
            sbk, sbb = bank[bank_s[u.n % 2]], bank_b[bank_s[u.n % 2]]
            for q, (ci, dj, rp) in enumerate(u.sub):
                jp = u.j - dj
                kb.op("pe", lambda e: e.matmul(sbk[:, q * 128:(q + 1) * 128],
                                               lhsT=kt[rows, rp, jp * 128:(jp + 1) * 128],
                                               rhs=QTs[rows, qblk, u.j * 128:(u.j + 1) * 128],
                                               start=True, stop=True),
                      reads=[ktb, QT_b], writes=[sbb])

        def s2(u):
            sbk, sbb = bank[bank_s[u.n % 2]], bank_b[bank_s[u.n % 2]]
            pt, ptb = p_sb[u.n % 4]
            w = 128 * len(u.sub)
            kb.op("act", lambda e: e.activation(out=pt[:, 0:w], in_=sbk[:, 0:w], func=AF.Exp, scale=0.125),
                  reads=[sbb], writes=[ptb])

        def s2b(u):
            pt, ptb = p_sb[u.n % 4]
            w = 128 * len(u.sub)
            ci0 = u.sub[0][0]
            kb.op("dve", lambda e: e.tensor_tensor(out=pt[:, 0:w], in0=pt[:, 0:w],
                                                   in1=tab(tabname, (tab0 + ci0) * 128, w), op=ALU.mult),
                  reads=[ptb, tabh_b], writes=[ptb])

        def s3(u):
            ab, abb = bank[bank_acc[u.i % 2]], bank_b[bank_acc[u.i % 2]]
            pt, ptb = p_sb[u.n % 4]
            for q, (ci, dj, rp) in enumerate(u.sub):
                jp = u.j - dj
                kb.op("pe", lambda e: e.matmul(ab[:, u.a * HD1:(u.a + 1) * HD1], lhsT=pt[:, q * 128:(q + 1) * 128],
                                               rhs=vt[:, rp * NT + jp, 0:HD1],
                                               start=(u.first and q == 0),
                                               stop=(u.last and q == len(u.sub) - 1)),
                      reads=[ptb, vtb], writes=[abb])
            if u.last:
                evac(ab, abb, u.i)

        run_pipeline(units, [s1, s2, s2b, s3])

    def emit_dil():
        kb.push_scope()
        dacc, dacc_b = kb.sbt("M_dacc", [128, NT, 2, HD1], F32)
        tab0 = 0
        for g in range(3):
            ks = load_k(3 + g)
            for hh in range(2):
                vs = load_v(2 + 2 * g + hh)

                def evac_d(ab, abb, i, g=g, hh=hh):
                    dst = dacc[:, 4 * i:4 * i + 4, hh, :]
                    src = acc_view(ab, 0, HD1)
                    if g == 0:
                        kb.op("dve", lambda e: e.tensor_copy(out=dst, in_=src), reads=[abb], writes=[dacc_b])
                    else:
                        kb.op("dve", lambda e: e.tensor_tensor(out=dst, in0=src, in1=dst, op=ALU.add),
                              reads=[abb, dacc_b], writes=[dacc_b])

                emit_banded(ks[0], ks[1], vs[0], vs[1], slice(64 * hh, 64 * hh + 64), 2 + g, "tdil", tab0,
                            DIL_COMBOS[g], (0, 1), (2, 3), evac_d)
                tab0 += len(DIL_COMBOS[g])
        dr, dr_b = kb.sbt("M_dr", [128, NT * 2], F32)
        dflat = dacc[:].rearrange("p j h d -> p (j h) d")
        kb.op("dve", lambda e: e.reciprocal(out=dr[:], in_=dflat[:, :, HD]), reads=[dacc_b], writes=[dr_b])
        for j in range(NT):
            for hh in range(2):
                kb.op("dve", lambda e: e.tensor_scalar(out=cat[:, j, 256 + hh * HD:256 + (hh + 1) * HD],
                                                       in0=dacc[:, j, hh, 0:HD],
                                                       scalar1=dr[:, 2 * j + hh:2 * j + hh + 1], scalar2=None,
                                                       op0=ALU.mult),
                      reads=[dacc_b, dr_b], writes=[cat_b])
        kb.pop_scope()


    def emit_nsa():
        kb.push_scope()
        _xs = cnt["k"] % 2
        cnt["k"] += 1
        XT_b = kslot[_xs][1]
        W1s, W1_b = kb.sbt("N_W1", [128, 32, 128], BF16)
        W2s, W2_b = kb.sbt("N_W2", [128, 128], BF16)
        peTf, pe_b = kb.sbt("N_peTf", [128, 32], F32)
        peTb, peb_b = kb.sbt("N_peTb", [128, 32], BF16)
        gkc, gkc_b = kb.sbt("N_gkc", [128, HD], F32)
        hTc = [kb.sbt("N_hT%d" % c, [128, 512], BF16) for c in range(2)]
        u_sb, u_b = kb.sbt("N_u", [128, 512], F32)
        t_sb, t_b = kb.sbt("N_t", [128, 512], F32)
        bias2, bias2_b = kb.sbt("N_bias2", [128, 2], F32)
        KcT2, KcT_b = kb.sbt("N_KcT2", [128, 512], BF16)
        Vc, Vc_b = kb.sbt("N_Vc", [128, 4, HD1], BF16)
        kc2, kc2_b = kb.sbt("N_kc2", [128, 128], BF16)
        stt, stt_b = kb.sbt("N_st", [128, 4], F32)
        epsb, epsb_b = kb.sbt("N_eps", [128, 1], F32)
        imp, imp_b = kb.sbt("N_imp", [128, 512], F32)
        impb, impb_b = kb.sbt("N_impb", [128, 512], F32)
        tmpm, tmpm_b = kb.sbt("N_tmpm", [128, 128], F32)
        m8, m8_b = kb.sbt("N_m8", [128, 16], F32)
        nsel, nsel_b = kb.sbt("N_nsel", [128, 128], BF16)
        nsel2, nsel2_b = kb.sbt("N_nsel2", [128, 128], BF16)

        XTflat = kslot[_xs][0][:].rearrange("p r t -> p (r t)")
        xv = XTflat.rearrange("p (j r q) -> p j r q", r=RANKS, q=128)
        for rp in range(RANKS):
            kb.dma("sp", xv[:, :, rp, :], io["KTa_fn"](rp, 2).rearrange("p (j q) -> p j q", q=128),
                   reads=[io["KTa_bf"](2)], writes=[XT_b])
        for hf in range(2):
            kb.dma("pool", W1s[:, hf * 16:(hf + 1) * 16, :].rearrange("p l h -> p (l h)"),
                   io["w1p"][:, hf * 2048:(hf + 1) * 2048], writes=[W1_b])
        kb.dma("pool", W2s[:], io["w2p"], writes=[W2_b])
        kb.dma("sp", peTf[:], io["peT"], writes=[pe_b])
        kb.dma("sp", gkc[:], io["gkc"], writes=[gkc_b])
        kb.op("dve", lambda e: e.tensor_copy(out=peTb[:], in_=peTf[:]), reads=[pe_b], writes=[peb_b])
        kb.op("pool", lambda e: e.memset(epsb[:], RMS_EPS), writes=[epsb_b])
        kb.op("pool", lambda e: e.memset(Vc[:], 1.0), writes=[Vc_b])
        for c in range(2):
            kb.op("pool", lambda e: e.memset(hTc[c][0][:], 0.0), writes=[hTc[c][1]])
        xs = XTflat.rearrange("p (n s) -> p n s", s=16)
        if "n1" in DBG:
            kb.pop_scope()
            return
        for c in range(2):
            rows = slice(64 * c, 64 * c + 64)
            for l in range(32):
                kb.op("pe", lambda e: e.matmul(bank[2][:, 32 * c + l:32 * c + l + 1], lhsT=W1s[rows, l, :],
                                               rhs=peTb[rows, l:l + 1], start=True, stop=True),
                      reads=[W1_b, peb_b], writes=[bank_b[2]])
        kb.op("dve", lambda e: e.tensor_reduce(out=bias2[:], in_=bank[2][:, 0:64].rearrange("p (c l) -> p c l", l=32),
                                               axis=mybir.AxisListType.X, op=ALU.add),
              reads=[bank_b[2]], writes=[bias2_b])
        for c in range(2):
            rows = slice(64 * c, 64 * c + 64)
            pbk, pbb = bank[c], bank_b[c]
            for l in range(32):
                rhs = xs[rows, 0:511, l] if l < 16 else xs[rows, 1:512, l - 16]
                kb.op("pe", lambda e: e.matmul(pbk[:, 0:511], lhsT=W1s[rows, l, :], rhs=rhs,
                                               start=(l == 0), stop=(l == 31)),
                      reads=[W1_b, XT_b], writes=[pbb])
            kb.op("act", lambda e: e.activation(out=u_sb[:, 0:511], in_=pbk[:, 0:511], func=AF.Identity,
                                                bias=bias2[:, c:c + 1]),
                  reads=[pbb, bias2_b], writes=[u_b])
            kb.op("dve", lambda e: e.tensor_tensor(out=t_sb[:, 0:511], in0=u_sb[:, 0:511], in1=u_sb[:, 0:511],
                                                   op=ALU.mult), reads=[u_b], writes=[t_b])
            kb.op("dve", lambda e: e.tensor_scalar(out=t_sb[:, 0:511], in0=t_sb[:, 0:511], scalar1=0.044715,
                                                   scalar2=1.0, op0=ALU.mult, op1=ALU.add),
                  reads=[t_b], writes=[t_b])
            kb.op("dve", lambda e: e.tensor_tensor(out=t_sb[:, 0:511], in0=t_sb[:, 0:511], in1=u_sb[:, 0:511],
                                                   op=ALU.mult), reads=[t_b, u_b], writes=[t_b])
            kb.op("act", lambda e: e.activation(out=t_sb[:, 0:511], in_=t_sb[:, 0:511], func=AF.Sigmoid,
                                                scale=1.5957691216057308), reads=[t_b], writes=[t_b])
            kb.op("dve", lambda e: e.tensor_tensor(out=hTc[c][0][:, 0:511], in0=t_sb[:, 0:511],
                                                   in1=u_sb[:, 0:511], op=ALU.mult),
                  reads=[t_b, u_b], writes=[hTc[c][1]])
        if "n2" in DBG:
            kb.pop_scope()
            return
        for c in range(2):
            for nt in range(4):
                pk, pkb = bank[3 + nt % 2], bank_b[3 + nt % 2]
                kb.op("pe", lambda e: e.matmul(pk[:, 0:HD], lhsT=hTc[c][0][:, nt * 128:(nt + 1) * 128],
                                               rhs=W2s[:, c * HD:(c + 1) * HD], start=True, stop=True),
                      reads=[hTc[c][1], W2_b], writes=[pkb])
                if c == 1:
                    kb.op("act", lambda e: e.activation(out=Vc[:, nt, 0:HD], in_=pk[:, 0:HD], func=AF.Copy),
                          reads=[pkb], writes=[Vc_b])
                    continue
                kb.op("act", lambda e: e.activation(out=t_sb[:, 0:HD], in_=pk[:, 0:HD], func=AF.Square,
                                                    accum_out=stt[:, 0:1]),
                      reads=[pkb], writes=[t_b, stt_b])
                kb.op("dve", lambda e: e.tensor_scalar(out=stt[:, 1:2], in0=stt[:, 0:1], scalar1=1.0 / HD,
                                                       scalar2=RMS_EPS, op0=ALU.mult, op1=ALU.add),
                      reads=[stt_b], writes=[stt_b])
                kb.op("act", lambda e: e.activation(out=stt[:, 2:3], in_=stt[:, 1:2], func=AF.Sqrt),
                      reads=[stt_b], writes=[stt_b])
                kb.op("dve", lambda e: e.reciprocal(out=stt[:, 3:4], in_=stt[:, 2:3]), reads=[stt_b],
                      writes=[stt_b])
                kb.op("dve", lambda e: e.scalar_tensor_tensor(out=kc2[:, 0:HD], in0=pk[:, 0:HD],
                                                              scalar=stt[:, 3:4], in1=gkc[:], op0=ALU.mult,
                                                              op1=ALU.mult),
                      reads=[pkb, stt_b, gkc_b], writes=[kc2_b])
                kb.op("dve", lambda e: e.tensor_copy(out=kc2[:, HD:2 * HD], in_=kc2[:, 0:HD]), reads=[kc2_b],
                      writes=[kc2_b])
                kb.op("pe", lambda e: e.transpose(out=kb.bankh[:, 0:128], in_=kc2[:], identity=tab("identb", 0, 128)),
                      reads=[kc2_b, tabh_b], writes=[kb.bankh_b])
                kb.op("act", lambda e: e.activation(out=KcT2[:, nt * 128:(nt + 1) * 128], in_=kb.bankh[:, 0:128],
                                                    func=AF.Copy), reads=[kb.bankh_b], writes=[KcT_b])
        if "n3" in DBG:
            kb.op("dve", lambda e: e.tensor_copy(out=cat[:, 0, 0:512], in_=KcT2[:]), reads=[KcT_b], writes=[cat_b])
            kb.op("dve", lambda e: e.tensor_copy(out=cat[:, 1, 0:260], in_=Vc[:].rearrange("p a d -> p (a d)")),
                  reads=[Vc_b], writes=[cat_b])
            kb.op("dve", lambda e: e.tensor_copy(out=cat[:, 2, 0:512], in_=hTc[0][0][:]), reads=[hTc[0][1]], writes=[cat_b])
            kb.op("dve", lambda e: e.tensor_copy(out=cat[:, 3, 0:512], in_=hTc[1][0][:]), reads=[hTc[1][1]], writes=[cat_b])
            kb.op("dve", lambda e: e.tensor_copy(out=cat[:, 4, 0:2], in_=bias2[:]), reads=[bias2_b], writes=[cat_b])
            kb.pop_scope()
            return

        for i in range(NG):
            for h in range(4):
                rows = slice(64 * (h % 2), 64 * (h % 2) + 64)
                accC, accC_b = bank[2 + 2 * (h % 2)], bank_b[2 + 2 * (h % 2)]
                accI, accI_b = bank[3 + 2 * (h % 2)], bank_b[3 + 2 * (h % 2)]
                for nt in range(i + 1):
                    dl = i - nt
                    sbk, sbb = bank[nt % 2], bank_b[nt % 2]
                    pt, ptb = p_sb[nt % 3]
                    kb.op("pe", lambda e: e.matmul(sbk[:, :], lhsT=KcT2[rows, nt * 128:(nt + 1) * 128],
                                                   rhs=QTs[rows, h // 2, 512 * i:512 * i + 512],
                                                   start=True, stop=True),
                          reads=[KcT_b, QT_b], writes=[sbb])
                    kb.op("act", lambda e: e.activation(out=pt[:], in_=sbk[:, :], func=AF.Exp, scale=0.125,
                                                        bias=tabfp("bias_cmp", 4 * h + dl, 1)),
                          reads=[sbb, tabf_b], writes=[ptb])
                    if dl <= 1:
                        kb.op("pool", lambda e: e.tensor_tensor(out=pt[:], in0=pt[:],
                                                                in1=tab("cmask", 512 * dl, 512), op=ALU.mult),
                              reads=[ptb, tabh_b], writes=[ptb])
                    for a in range(4):
                        kb.op("pe", lambda e: e.matmul(accC[:, a * HD1:(a + 1) * HD1],
                                                       lhsT=pt[:, a * 128:(a + 1) * 128], rhs=Vc[:, nt, :],
                                                       start=(nt == 0 and a == 0), stop=(nt == i and a == 3)),
                              reads=[ptb, Vc_b], writes=[accC_b])
                        kb.op("pe", lambda e: e.matmul(accI[:, a * 128:(a + 1) * 128],
                                                       lhsT=pt[:, a * 128:(a + 1) * 128],
                                                       rhs=tab("ov", nt * 128, 128),
                                                       start=(nt == 0 and a == 0), stop=(nt == i and a == 3)),
                              reads=[ptb, tabh_b], writes=[accI_b])
                t, tb = evac_nsa(accC, accC_b, i, h, 0, True)
                for a in range(4):
                    src = accI[:, a * 128:(a + 1) * 128]
                    dst = imp[:, a * 128:(a + 1) * 128]
                    if h == 0:
                        kb.op("dve", lambda e: e.tensor_scalar(out=dst, in0=src, scalar1=t[:, a:a + 1],
                                                               scalar2=None, op0=ALU.mult),
                              reads=[accI_b, tb], writes=[imp_b])
                    else:
                        kb.op("dve", lambda e: e.scalar_tensor_tensor(out=dst, in0=src, scalar=t[:, a:a + 1],
                                                                      in1=dst, op0=ALU.mult, op1=ALU.add),
                              reads=[accI_b, tb, imp_b], writes=[imp_b])
            kb.op("dve", lambda e: e.tensor_tensor(out=impb[:], in0=imp[:], in1=tabfp("bsel", 512 * i, 512),
                                                   op=ALU.add), reads=[imp_b, tabf_b], writes=[impb_b])
            for a in range(4):
                iv = impb[:, a * 128:(a + 1) * 128]
                kb.op("dve", lambda e: e.max(out=m8[:, 0:8], in_=iv), reads=[impb_b], writes=[m8_b])
                kb.op("dve", lambda e: e.match_replace(out=tmpm[:], in_to_replace=m8[:, 0:8], in_values=iv,
                                                       imm_value=-3.0e38),
                      reads=[impb_b, m8_b], writes=[tmpm_b])
                kb.op("dve", lambda e: e.max(out=m8[:, 8:16], in_=tmpm[:]), reads=[tmpm_b], writes=[m8_b])
                kb.op("dve", lambda e: e.tensor_scalar(out=nsel[:], in0=iv, scalar1=m8[:, 15:16], scalar2=-1.0,
                                                       op0=ALU.is_ge, op1=ALU.add),
                      reads=[impb_b, m8_b], writes=[nsel_b])
                for bh in range(2):
                    kb.op("dve", lambda e: e.tensor_copy(out=nsel2[:, 0:64], in_=nsel[:, 64 * bh:64 * bh + 64]),
                          reads=[nsel_b], writes=[nsel2_b])
                    kb.op("dve", lambda e: e.tensor_copy(out=nsel2[:, 64:128], in_=nsel[:, 64 * bh:64 * bh + 64]),
                          reads=[nsel_b], writes=[nsel2_b])
                    kb.op("pe", lambda e: e.transpose(out=kb.bankh[:, 128:256], in_=nsel2[:],
                                                      identity=tab("identb", 0, 128)),
                          reads=[nsel2_b, tabh_b], writes=[kb.bankh_b])
                    kb.op("act", lambda e: e.activation(
                        out=negselT[:, bh, 512 * i + 128 * a:512 * i + 128 * (a + 1)],
                        in_=kb.bankh[:, 128:256], func=AF.Copy),
                        reads=[kb.bankh_b], writes=[negselT_b])
        kb.pop_scope()
        if "n4" in DBG:
            kb.op("act", lambda e: e.activation(out=cat[:, :, 0:256], in_=ocomb[:].rearrange("p j h d -> p j (h d)"),
                                                func=AF.Copy), reads=[ocomb_b], writes=[cat_b])
            return

        ks = load_k(0)
        vs = load_v(0)
        kt, ktb = ks
        vt, vtb = vs
        for h in range(4):
            rows = slice(64 * (h % 2), 64 * (h % 2) + 64)
            units = []
            n = 0
            for i in range(NG):
                tot = 4 * (4 * i + 4)
                k = 0
                for jp in range(4 * i + 4):
                    for rp in range(4):
                        u = U()
                        u.n = n
                        n += 1
                        u.i, u.jp, u.rp = i, jp, rp
                        u.masked = jp >= 4 * i
                        u.a0 = jp - 4 * i if u.masked else 0
                        u.c0 = 128 * u.a0
                        u.first = (k == 0)
                        u.last = (k == tot - 1)
                        k += 1
                        units.append(u)

            def s1(u):
                sbk, sbb = bank[u.n % 2], bank_b[u.n % 2]
                q0, q1 = 512 * u.i + u.c0, 512 * u.i + 512
                mk = 4 * u.jp + u.rp
                q4, uu = divmod(mk, 32)
                kb.op("pe", lambda e: e.matmul(sbk[:, u.c0:512], lhsT=kt[rows, u.rp, u.jp * 128:(u.jp + 1) * 128],
                                               rhs=QTs[rows, h // 2, q0:q1], start=True, stop=False),
                      reads=[ktb, QT_b], writes=[sbb])
                kb.op("pe", lambda e: e.matmul(sbk[:, u.c0:512],
                                               lhsT=tab("e32", uu * 128, 128)[rows, :],
                                               rhs=negselT[rows, q4, q0:q1], start=False, stop=True),
                      reads=[tabh_b, negselT_b], writes=[sbb])

            def s2(u):
                sbk, sbb = bank[u.n % 2], bank_b[u.n % 2]
                pt, ptb = p_sb[u.n % 3]
                col = 64 * h + 4 * (u.jp - 4 * u.i) + u.rp + 48
                kb.op("act", lambda e: e.activation(out=pt[:, u.c0:512], in_=sbk[:, u.c0:512], func=AF.Exp,
                                                    scale=0.125, bias=tabfp("bias_sel", col, 1)),
                      reads=[sbb, tabf_b], writes=[ptb])
                if u.masked:
                    kb.op("pool", lambda e: e.tensor_tensor(out=pt[:, u.c0:u.c0 + 128], in0=pt[:, u.c0:u.c0 + 128],
                                                            in1=tab("mcausal", u.rp * 128, 128), op=ALU.mult),
                          reads=[ptb, tabh_b], writes=[ptb])

            def s3(u):
                ab, abb = bank[2 + u.i % 2], bank_b[2 + u.i % 2]
                pt, ptb = p_sb[u.n % 3]
                for a in range(u.a0, 4):
                    kb.op("pe", lambda e: e.matmul(ab[:, a * HD1:(a + 1) * HD1], lhsT=pt[:, a * 128:(a + 1) * 128],
                                                   rhs=vt[:, u.rp * NT + u.jp, :],
                                                   start=(u.first and a == u.a0), stop=(u.last and a == 3)),
                          reads=[ptb, vtb], writes=[abb])
                if u.last:
                    evac_nsa(ab, abb, u.i, h, 1, False)

            run_pipeline(units, [s1, s2, s3])

        if "n5" in DBG:
            kb.op("act", lambda e: e.activation(out=cat[:, :, 0:256], in_=ocomb[:].rearrange("p j h d -> p j (h d)"),
                                                func=AF.Copy), reads=[ocomb_b], writes=[cat_b])
            return
        ks = load_k(1)
        vs = load_v(1)
        wcombos = [(dj, rp) for dj in range(2) for rp in range(4)]
        for h in range(4):
            emit_banded(ks[0], ks[1], vs[0], vs[1], slice(64 * (h % 2), 64 * (h % 2) + 64), h // 2, "twin", 8 * h,
                        wcombos, (0, 1), (2, 3),
                        lambda ab, abb, i, h=h: evac_nsa(ab, abb, i, h, 2, False))
        kb.op("act", lambda e: e.activation(out=cat[:, :, 0:256], in_=ocomb[:].rearrange("p j h d -> p j (h d)"),
                                            func=AF.Copy), reads=[ocomb_b], writes=[cat_b])


    if "nonsa" not in DBG:
        emit_nsa()
    if "nodil" not in DBG:
        emit_dil()

    kb.push_scope()
    e_sb = [kb.sbt("M_e%d" % i, [128, 512], F32) for i in range(6)]
    x_sb = [kb.sbt("M_x%d" % i, [128, 512], F32) for i in range(2)]
    sp_sb = [kb.sbt("M_sp%d" % i, [128, 512], BF16) for i in range(3)]
    aT_sb = [kb.sbt("M_aT%d" % i, [128, 512], BF16) for i in range(3)]
    spacc, spacc_b = kb.sbt("M_spacc", [128, 512], BF16)

    if "nosb" not in DBG:
        for hp in range(3):
            ks = load_k(6 + hp)
            for hh in range(2):
                if "sb1" in DBG and (hp, hh) != (0, 0):
                    continue
                vs = load_v(8 + 2 * hp + hh)
                emit_sb_head(2 * hp + hh, ks, vs)
    kb.pop_scope()

    kb.pop_scope()


def declare_mixer_io(kb, io):
    io["QT"] = kb.dram_bf16("QT", [NQB, 128, TOK], "ExternalInput")
    io["KTa"] = kb.dram_bf16("KTa", [RANKS, NKB, 128, TOK], "ExternalInput")
    io["Va"] = kb.dram_bf16("Va", [RANKS, TOK, NVH * HD], "ExternalInput")
    io["G"] = kb.dram("G", [128, NT * 12], F32, "ExternalInput")
    io["tabh"] = kb.dram_bf16("tabh", [128, TABH_COLS], "ExternalInput")
    io["tabf"] = kb.dram("tabf", [128, TABF_COLS], F32, "ExternalInput")
    io["w1p"] = kb.dram("w1p", [128, 32 * 128], F32, "ExternalInput")
    io["w2p"] = kb.dram("w2p", [128, 128], F32, "ExternalInput")
    io["peT"] = kb.dram("peT", [128, 32], F32, "ExternalInput")
    io["gkc"] = kb.dram("gkc", [128, HD], F32, "ExternalInput")
    for n in ("QT", "KTa", "Va", "G"):
        io[n + "_b"] = kb.buf(n)
    io["KTa_fn"] = lambda rp, kbi: io["KTa"][rp, kbi]
    io["KTa_bf"] = lambda kbi: io["KTa_b"]
    io["Va_fn"] = lambda rp, q, vh: io["Va"][rp, q * 512:(q + 1) * 512, vh * HD:(vh + 1) * HD]
    io["Va_bf"] = lambda q: io["Va_b"]


def build_B_debug():
    kb = KB()
    io = {}
    declare_mixer_io(kb, io)
    cat_out = kb.dram_bf16("cat_out", [128, NT * D_CAT], "ExternalOutput")
    cat_out_b = kb.buf("cat_out")
    cat, cat_b = kb.sbt("cat", [128, NT, D_CAT], BF16)
    kb.op("pool", lambda e: e.memset(cat[:], 0.0), writes=[cat_b])
    emit_mixers(kb, io, cat, cat_b)
    kb.dma("sp", cat_out, cat[:].rearrange("p j d -> p (j d)"), reads=[cat_b], writes=[cat_out_b])
    kb.finish([cat_out_b])
    kb.close()
    return kb


def emit_phase_C(kb, io, cat, cat_b, x_src, x_dst):
    bank, bank_b = kb.bank, kb.bank_b
    kb.push_scope()
    xres, xres_b = kb.sbt("C_xres", [128, NT, D_MODEL], F32)
    xres_bj = [kb.buf("C_xres%d" % j) for j in range(NT)]
    h2T = kb.sb("C_h2T", [128, DC, TOK], BF16)
    h2T_b = [kb.buf("C_h2T%d" % g) for g in range(NG)]
    gm, cst_b = kb.sbt("C_gm", [128, DC], F32)
    ident, ident_b = kb.sbt("C_ident", [128, 128], F32)
    identb, identb_b = kb.sbt("C_identb", [128, 128], BF16)
    kb.dma("sp", gm[:], io["gmlp"], writes=[cst_b])
    kb.dma("sp", ident[:], io["ident"], writes=[ident_b])
    kb.op("dve", lambda e: e.tensor_copy(out=identb[:], in_=ident[:]), reads=[ident_b], writes=[identb_b])

    kb.push_scope()
    catT, catT_b = kb.sbt("C_catT", [128, 6, 128], BF16), None
    catT = [kb.sbt("C_catT%d" % i, [128, 6, 128], BF16) for i in range(2)]
    Wo, Wo_b = kb.sbt("C_Wo", [128, 6, D_MODEL], BF16)
    xt = [kb.sbt("C_xt%d" % i, [128, D_MODEL], F32) for i in range(2)]
    xn = [kb.sbt("C_xn%d" % i, [128, D_MODEL], F32) for i in range(2)]
    junk, junk_b = kb.sbt("C_junk", [128, D_MODEL], F32)
    st = [kb.sbt("C_st%d" % i, [128, 4], F32) for i in range(2)]
    for c in range(6):
        kb.dma("pool", Wo[:, c, :], io["wout"][c * 128:(c + 1) * 128, :], writes=[Wo_b])
    for j in range(NT):
        s = j % 2
        g = j // 4
        src_ap, src_b = x_src(j)
        kb.dma("sp", xt[s][0][:], src_ap, reads=[src_b] if src_b else [], writes=[xt[s][1]])
        for c in range(6):
            kb.op("pe", lambda e: e.transpose(out=kb.bankh[:, c * 128:(c + 1) * 128],
                                              in_=cat[:, j, c * 128:(c + 1) * 128], identity=identb[:]),
                  reads=[cat_b, identb_b], writes=[kb.bankh_b])
        ct, ctb = catT[s]
        kb.op("act", lambda e: e.activation(out=ct[:], in_=kb.bankh[:, 0:768].rearrange("p (c t) -> p c t", t=128),
                                            func=AF.Copy), reads=[kb.bankh_b], writes=[ctb])
        for hf in range(2):
            p, pb = bank[hf], bank_b[hf]
            for c in range(6):
                kb.op("pe", lambda e: e.matmul(p[:, :], lhsT=ct[:, c, :], rhs=Wo[:, c, hf * 512:(hf + 1) * 512],
                                               start=(c == 0), stop=(c == 5)),
                      reads=[ctb, Wo_b], writes=[pb])
            kb.op("dve", lambda e: e.tensor_tensor(out=xres[:, j, hf * 512:(hf + 1) * 512], in0=p[:, :],
                                                   in1=xt[s][0][:, hf * 512:(hf + 1) * 512], op=ALU.add),
                  reads=[pb, xt[s][1]], writes=[xres_bj[j]])
        kb.op("act", lambda e: e.activation(out=junk[:], in_=xres[:, j, :], func=AF.Square,
                                            accum_out=st[s][0][:, 0:1]),
              reads=[xres_bj[j]], writes=[junk_b, st[s][1]])
        kb.op("dve", lambda e: e.tensor_scalar(out=st[s][0][:, 1:2], in0=st[s][0][:, 0:1], scalar1=1.0 / D_MODEL,
                                               scalar2=RMS_EPS, op0=ALU.mult, op1=ALU.add),
              reads=[st[s][1]], writes=[st[s][1]])
        kb.op("act", lambda e: e.activation(out=st[s][0][:, 2:3], in_=st[s][0][:, 1:2], func=AF.Sqrt),
              reads=[st[s][1]], writes=[st[s][1]])
        kb.op("dve", lambda e: e.reciprocal(out=st[s][0][:, 3:4], in_=st[s][0][:, 2:3]),
              reads=[st[s][1]], writes=[st[s][1]])
        kb.op("dve", lambda e: e.tensor_scalar(out=xn[s][0][:], in0=xres[:, j, :], scalar1=st[s][0][:, 3:4],
                                               scalar2=None, op0=ALU.mult),
              reads=[xres_bj[j], st[s][1]], writes=[xn[s][1]])
        for hf in range(2):
            p, pb = bank[2 + hf], bank_b[2 + hf]
            for cc in range(4):
                c = hf * 4 + cc
                kb.op("pe", lambda e: e.transpose(out=p[:, cc * 128:(cc + 1) * 128],
                                                  in_=xn[s][0][:, c * 128:(c + 1) * 128], identity=ident[:]),
                      reads=[xn[s][1], ident_b], writes=[pb])
            for cc in range(4):
                c = hf * 4 + cc
                if cc % 2 == 0:
                    kb.op("act", lambda e: e.activation(out=h2T[:, c, j * 128:(j + 1) * 128],
                                                        in_=p[:, cc * 128:(cc + 1) * 128], func=AF.Copy,
                                                        scale=gm[:, c:c + 1]),
                          reads=[pb, cst_b], writes=[h2T_b[g]])
                else:
                    kb.op("dve", lambda e: e.tensor_scalar(out=h2T[:, c, j * 128:(j + 1) * 128],
                                                           in0=p[:, cc * 128:(cc + 1) * 128],
                                                           scalar1=gm[:, c:c + 1], scalar2=None, op0=ALU.mult),
                          reads=[pb, cst_b], writes=[h2T_b[g]])
    kb.pop_scope()

    kb.push_scope()
    NE = 8
    FE = D_FF // NE
    Wu = [kb.sbt("C_Wu%d" % i, [128, DC, FE], BF16) for i in range(2)]
    Wd = [kb.sbt("C_Wd%d" % i, [128, FE // 128, D_MODEL], BF16) for i in range(2)]
    actT = [kb.sbt("C_act%d" % i, [128, FE // 128, 512], BF16) for i in range(2)]
    rl = [kb.sbt("C_rl%d" % i, [128, 512], F32) for i in range(2)]
    n = 0
    for e8 in range(NE):
        ws = e8 % 2
        for c in range(DC):
            kb.dma("pool", Wu[ws][0][:, c, :], io["wup"][c * 128:(c + 1) * 128, e8 * FE:(e8 + 1) * FE],
                   writes=[Wu[ws][1]])
        for fc in range(FE // 128):
            r0 = e8 * FE + fc * 128
            kb.dma("pool", Wd[ws][0][:, fc, :], io["wdown"][r0:r0 + 128, :], writes=[Wd[ws][1]])
        for g in range(NG):
            at, atb = actT[g % 2]
            for fc in range(FE // 128):
                p, pb = bank[n % 2], bank_b[n % 2]
                rt, rtb = rl[n % 2]
                n += 1
                for c in range(DC):
                    kb.op("pe", lambda e: e.matmul(p[:, :], lhsT=Wu[ws][0][:, c, fc * 128:(fc + 1) * 128],
                                                   rhs=h2T[:, c, g * 512:(g + 1) * 512],
                                                   start=(c == 0), stop=(c == DC - 1)),
                          reads=[Wu[ws][1], h2T_b[g]], writes=[pb])
                kb.op("act", lambda e: e.activation(out=rt[:], in_=p[:, :], func=AF.Relu), reads=[pb], writes=[rtb])
                kb.op("dve", lambda e: e.tensor_tensor(out=at[:, fc, :], in0=rt[:], in1=rt[:], op=ALU.mult),
                      reads=[rtb], writes=[atb])
            for a in range(4):
                j = 4 * g + a
                for hf in range(2):
                    p, pb = bank[2 + (2 * a + hf) % 4], bank_b[2 + (2 * a + hf) % 4]
                    for fc in range(FE // 128):
                        kb.op("pe", lambda e: e.matmul(p[:, :], lhsT=at[:, fc, a * 128:(a + 1) * 128],
                                                       rhs=Wd[ws][0][:, fc, hf * 512:(hf + 1) * 512],
                                                       start=(fc == 0), stop=(fc == FE // 128 - 1)),
                              reads=[atb, Wd[ws][1]], writes=[pb])
                    kb.op("dve", lambda e: e.tensor_tensor(out=xres[:, j, hf * 512:(hf + 1) * 512], in0=p[:, :],
                                                           in1=xres[:, j, hf * 512:(hf + 1) * 512], op=ALU.add),
                          reads=[pb, xres_bj[j]], writes=[xres_bj[j]])
                if e8 == NE - 1:
                    dst_ap, dst_b = x_dst(j)
                    kb.dma("sp", dst_ap, xres[:, j, :], reads=[xres_bj[j]], writes=[dst_b])
    kb.pop_scope()
    kb.pop_scope()


def build_B():
    kb = KB()
    io = {}
    declare_mixer_io(kb, io)
    x = kb.dram("x_own", [TOK, D_MODEL], F32, "ExternalInput")
    xo = kb.dram("x_out", [TOK, D_MODEL], F32, "ExternalOutput")
    xo_b = kb.buf("x_out")
    io["wout"] = kb.dram("wout", [D_CAT, D_MODEL], F32, "ExternalInput")
    io["wup"] = kb.dram("wup", [D_MODEL, D_FF], F32, "ExternalInput")
    io["wdown"] = kb.dram("wdown", [D_FF, D_MODEL], F32, "ExternalInput")
    io["gmlp"] = kb.dram("gmlp", [128, DC], F32, "ExternalInput")
    io["ident"] = kb.dram("ident", [128, 128], F32, "ExternalInput")
    cat, cat_b = kb.sbt("cat", [128, NT, D_CAT], BF16)
    emit_mixers(kb, io, cat, cat_b)
    emit_phase_C(kb, io, cat, cat_b, lambda j: (x[j * 128:(j + 1) * 128, :], None),
                 lambda j: (xo[j * 128:(j + 1) * 128, :], xo_b))
    kb.finish([xo_b])
    kb.close()
    return kb


PIECE = 256
NPIECE = 9
PIECE_ORDER = (0, 1, 5, 6, 7, 8, 2, 3, 4)
LAYER_W = ("winp", "gmix", "gfm", "w1p", "w2p", "peT", "gkc", "wout", "wup", "wdown", "gmlp")


def build_fused():
    kb = KB()
    nc = kb.nc
    x = kb.dram("x_own", [TOK, D_MODEL], F32, "ExternalInput")
    xo = kb.dram("x_out", [TOK, D_MODEL], F32, "ExternalOutput")
    xo_b = kb.buf("x_out")
    shapes = {"winp": [D_MODEL, WP], "gmix": [128, DC], "gfm": [128, NFM], "w1p": [128, 4096], "w2p": [128, 128],
              "peT": [128, 32], "gkc": [128, HD], "wout": [D_CAT, D_MODEL], "wup": [D_MODEL, D_FF],
              "wdown": [D_FF, D_MODEL], "gmlp": [128, DC]}
    ext = {n: kb.dram(n, [DEPTH] + shapes[n], F32, "ExternalInput") for n in LAYER_W}
    ident = kb.dram("ident", [128, 128], F32, "ExternalInput")
    tabh = kb.dram_bf16("tabh", [128, TABH_COLS], "ExternalInput")
    tabf = kb.dram("tabf", [128, TABF_COLS], F32, "ExternalInput")
    QT = kb.dram("QT_i", [NQB, 128, TOK], BF16, "Internal")
    G = kb.dram("G_i", [128, NT * 12], F32, "Internal")
    xmid = kb.dram("xmid_i", [TOK, D_MODEL], F32, "Internal")
    xmid_b = kb.buf("xmid")
    gbo = [[kb.dram("gbo%d_%d" % (l, i), [PIECE, 2048], BF16, "Internal") for i in range(NPIECE)]
           for l in range(DEPTH)]
    gba = [[kb.dram("gba%d_%d" % (l, i), [RANKS * PIECE, 2048], BF16, "Internal") for i in range(NPIECE)]
           for l in range(DEPTH)]
    cat, cat_b = kb.sbt("cat", [128, NT, D_CAT], BF16)
    QT_b, G_b = kb.buf("QT"), kb.buf("G")
    for l in range(DEPTH):
        io = {n: ext[n][l] for n in LAYER_W}
        io["ident"], io["tabh"], io["tabf"] = ident, tabh, tabf
        io["QT"], io["QT_b"], io["G"], io["G_b"] = QT, QT_b, G, G_b
        gbo_b = [kb.buf("gbo%d_%d" % (l, i)) for i in range(NPIECE)]
        gba_b = [kb.buf("gba%d_%d" % (l, i)) for i in range(NPIECE)]
        go, ga = gbo[l], gba[l]
        io["KT_fn"] = lambda ki, go=go: go[ki // 2][(ki % 2) * 128:(ki % 2) * 128 + 128, :]
        io["KT_bf"] = lambda ki, gbo_b=gbo_b: gbo_b[ki // 2]
        io["V_fn"] = lambda jj, go=go: go[5 + jj // 4][:, :].rearrange("a (u d) -> (a u) d", d=1024)[
            (jj % 4) * 128:(jj % 4) * 128 + 128, :]
        io["V_cols"] = 1024
        io["zero_fill"] = [(go[4][128:256, :], gbo_b[4])]
        io["V_bf"] = lambda jj, gbo_b=gbo_b: gbo_b[5 + jj // 4]
        io["KTa_fn"] = lambda rp, kbi, ga=ga: ga[kbi // 2][rp * PIECE + (kbi % 2) * 128:
                                                         rp * PIECE + (kbi % 2) * 128 + 128, :]
        io["KTa_bf"] = lambda kbi, gba_b=gba_b: gba_b[kbi // 2]
        io["Va_fn"] = lambda rp, q, vh, ga=ga: ga[5 + q][rp * PIECE:(rp + 1) * PIECE, :] \
            .rearrange("a (u d) -> (a u) d", d=1024)[:, vh * HD:(vh + 1) * HD]
        io["Va_bf"] = lambda q, gba_b=gba_b: gba_b[5 + q]
        if l == 0:
            x_src = lambda j: (x[j * 128:(j + 1) * 128, :], None)
        else:
            x_src = lambda j: (xmid[j * 128:(j + 1) * 128, :], xmid_b)
        if l == DEPTH - 1:
            x_dst = lambda j: (xo[j * 128:(j + 1) * 128, :], xo_b)
        else:
            x_dst = lambda j: (xmid[j * 128:(j + 1) * 128, :], xmid_b)
        emit_phase_A(kb, x_src, None, io)
        for i in PIECE_ORDER:
            if "nocc" in DBG:
                for rr in range(RANKS):
                    kb.dma("sp", ga[i][rr * PIECE:(rr + 1) * PIECE, :], go[i][:, :], reads=[gbo_b[i]],
                           writes=[gba_b[i]])
                continue
            kb.dma("pool", None, None, reads=[gbo_b[i]], writes=[gba_b[i]],
                   fn=lambda e, i=i: e.collective_compute(
                       "AllGather", ALU.bypass, replica_groups=[[0, 1, 2, 3], [4, 5, 6, 7]],
                       ins=[go[i][:, :]], outs=[ga[i][:, :]]))
        emit_mixers(kb, io, cat, cat_b)
        emit_phase_C(kb, io, cat, cat_b, x_src, x_dst)
    kb.finish([xo_b])
    kb.close()
    return kb


_PROGS = {}
LAST_EXEC_NS = None


def _prog(name):
    if name not in _PROGS:
        _PROGS[name] = {"A": build_A, "B": build_B, "F": build_fused}[name]()
    return _PROGS[name]


def kernel_unfused(x, norm_mix, norm_mlp, w_in, qk_gain_nsa, qk_gain_dil, cmp_pe, cmp_w1, cmp_w2, w_out, w_up, w_down):
    f32 = lambda a: np.ascontiguousarray(np.asarray(a, dtype=np.float32))
    x = f32(x)
    perm = win_column_perm()
    ident = np.eye(128, dtype=np.float32)
    rows = [core_rows(c) for c in range(NCORE)]
    tabs = [host_table_arrays(r) for r in range(RANKS)]
    xc = [np.ascontiguousarray(x[b, idx]) for (b, r, idx) in rows]
    for l in range(DEPTH):
        gmix, gfm = host_consts_A(f32(norm_mix[l]), f32(qk_gain_nsa[l]), f32(qk_gain_dil[l]))
        winp = np.ascontiguousarray(f32(w_in[l])[:, perm])
        resA = run_bass_kernel_spmd(_prog("A").nc, [
            {"x_own": xc[c], "winp": winp, "gmix": gmix, "gfm": gfm, "ident": ident} for c in range(NCORE)],
            core_ids=list(range(NCORE))).results
        w1p = np.ascontiguousarray(f32(cmp_w1[l]).reshape(2, 32, 64, 128).transpose(0, 2, 1, 3).reshape(128, 4096))
        w2p = np.ascontiguousarray(f32(cmp_w2[l]).transpose(1, 0, 2).reshape(128, 128))
        peT = np.ascontiguousarray(f32(cmp_pe[l]).transpose(0, 2, 1).reshape(128, 32))
        gkc = np.ascontiguousarray(np.broadcast_to(f32(qk_gain_nsa[l])[1], (128, HD)))
        gmlp = np.ascontiguousarray(f32(norm_mlp[l]).reshape(DC, 128).T)
        in_maps = []
        for c in range(NCORE):
            b, r, idx = rows[c]
            KTa = np.stack([resA[4 * b + rr]["KT"] for rr in range(RANKS)])
            Va = np.stack([resA[4 * b + rr]["V"] for rr in range(RANKS)])
            in_maps.append({"QT": resA[c]["QT"], "KTa": KTa, "Va": Va, "G": resA[c]["G"],
                            "tabh": tabs[r][0], "tabf": tabs[r][1], "w1p": w1p, "w2p": w2p, "peT": peT,
                            "gkc": gkc, "x_own": xc[c], "wout": f32(w_out[l]), "wup": f32(w_up[l]),
                            "wdown": f32(w_down[l]), "gmlp": gmlp, "ident": ident})
        resB = run_bass_kernel_spmd(_prog("B").nc, in_maps, core_ids=list(range(NCORE))).results
        xc = [resB[c]["x_out"] for c in range(NCORE)]
    out = np.empty((BATCH, SEQ, D_MODEL), np.float32)
    for c in range(NCORE):
        b, r, idx = rows[c]
        out[b, idx] = xc[c]
    return out


def kernel(x, norm_mix, norm_mlp, w_in, qk_gain_nsa, qk_gain_dil, cmp_pe, cmp_w1, cmp_w2, w_out, w_up, w_down):
    f32 = lambda a: np.ascontiguousarray(np.asarray(a, dtype=np.float32))
    x = f32(x)
    perm = win_column_perm()
    rows = [core_rows(c) for c in range(NCORE)]
    tabs = [host_table_arrays(r) for r in range(RANKS)]
    L = {n: [] for n in LAYER_W}
    for l in range(DEPTH):
        gmix, gfm = host_consts_A(f32(norm_mix[l]), f32(qk_gain_nsa[l]), f32(qk_gain_dil[l]))
        L["gmix"].append(gmix)
        L["gfm"].append(gfm)
        L["winp"].append(f32(w_in[l])[:, perm])
        L["w1p"].append(f32(cmp_w1[l]).reshape(2, 32, 64, 128).transpose(0, 2, 1, 3).reshape(128, 4096))
        L["w2p"].append(f32(cmp_w2[l]).transpose(1, 0, 2).reshape(128, 128))
        L["peT"].append(f32(cmp_pe[l]).transpose(0, 2, 1).reshape(128, 32))
        L["gkc"].append(np.broadcast_to(f32(qk_gain_nsa[l])[1], (128, HD)))
        L["gmlp"].append(f32(norm_mlp[l]).reshape(DC, 128).T)
        L["wout"].append(f32(w_out[l]))
        L["wup"].append(f32(w_up[l]))
        L["wdown"].append(f32(w_down[l]))
    shared = {n: np.ascontiguousarray(np.stack(v)).astype(np.float32) for n, v in L.items()}
    shared["ident"] = np.eye(128, dtype=np.float32)
    in_maps = []
    for c in range(NCORE):
        b, r, idx = rows[c]
        m = dict(shared)
        m["x_own"] = np.ascontiguousarray(x[b, idx])
        m["tabh"], m["tabf"] = tabs[r]
        in_maps.append(m)
    _r = run_bass_kernel_spmd(_prog("F").nc, in_maps, core_ids=list(range(NCORE)))
    global LAST_EXEC_NS
    LAST_EXEC_NS = getattr(_r, "exec_time_ns", None)
    res = _r.results
    out = np.empty((BATCH, SEQ, D_MODEL), np.float32)
    for c in range(NCORE):
        b, r, idx = rows[c]
        out[b, idx] = res[c]["x_out"]
    return out
```

```python
from contextlib import ExitStack
import numpy as np
import ml_dtypes
import concourse.bass as bass
import concourse.mybir as mybir
from concourse.bass_utils import run_bass_kernel_spmd

F32 = mybir.dt.float32
BF16 = mybir.dt.bfloat16
AF = mybir.ActivationFunctionType
ALU = mybir.AluOpType
NPBF = ml_dtypes.bfloat16

SAME_ENGINE_SYNC = True
DBG = set()


class Buf:
    __slots__ = ("name", "w", "r", "semkey", "semv")

    def __init__(self, name):
        self.name = name
        self.w = None
        self.r = {}
        self.semkey = None
        self.semv = 0


class KB:
    def __init__(self):
        self.nc = bass.Bass("TRN2", target_bir_lowering=False)
        nc = self.nc
        self.es = ExitStack()
        self.engs = {"pe": nc.tensor, "act": nc.scalar, "dve": nc.vector,
                     "pool": nc.gpsimd, "sp": nc.sync}
        self.semobj = {}
        self.cnt = {}
        self.seen = {}
        for n in self.engs:
            self.semobj[n] = self.es.enter_context(nc.semaphore("s_" + n))
            self.cnt[n] = 0
            self.seen[n] = {}
        self.nbuf = 0
        self.ninstr = 0
        self.nwait = 0
        self.root_es = self.es
        self.dmasems = {}
        self.bank = [self.es.enter_context(nc.psum_tensor("bank%d" % i, [128, 512], F32)) for i in range(7)]
        self.bank_b = [Buf("bank%d" % i) for i in range(7)]
        self.bankh = self.es.enter_context(nc.psum_tensor("bankh", [128, 1024], BF16))
        self.bankh_b = Buf("bankh")

    def push_scope(self):
        self._outer = getattr(self, "_outer", [])
        self._outer.append(self.es)
        self.es = ExitStack()

    def pop_scope(self):
        self.barrier()
        self.es.close()
        self.es = self._outer.pop()

    def barrier(self):
        deps = {n: self.cnt[n] for n in self.engs if self.cnt[n] > 0}
        deps.update(self.dmasems)
        for n in self.engs:
            self._wait(n, dict(deps))

    def sbt(self, name, shape, dt):
        return self.sb(name, shape, dt), self.buf(name)

    def sb(self, name, shape, dt):
        self.nsb = getattr(self, "nsb", 0) + 1
        return self.es.enter_context(self.nc.sbuf_tensor("%s_%d" % (name, self.nsb), list(shape), dt))

    def ps(self, name, shape, dt=F32):
        return self.es.enter_context(self.nc.psum_tensor(name, list(shape), dt))

    def dram(self, name, shape, dt, kind):
        return self.nc.dram_tensor(name, list(shape), dt, kind=kind).ap()

    def dram_bf16(self, name, shape, kind):
        shp = list(shape)
        assert shp[-1] % 2 == 0
        shp[-1] //= 2
        return self.nc.dram_tensor(name, shp, F32, kind=kind).ap().bitcast(BF16)

    def buf(self, name=None):
        self.nbuf += 1
        return Buf(name or f"b{self.nbuf}")

    def _deps(self, reads, writes):
        deps = {}
        for b in reads:
            if b.w is not None:
                k, v = b.w
                if deps.get(k, 0) < v:
                    deps[k] = v
        for b in writes:
            if b.w is not None:
                k, v = b.w
                if deps.get(k, 0) < v:
                    deps[k] = v
            for k, v in b.r.items():
                if deps.get(k, 0) < v:
                    deps[k] = v
        return deps

    def _wait(self, eng, deps):
        seen = self.seen[eng]
        e = self.engs[eng]
        for k, v in deps.items():
            if k == eng and (eng in ("pe", "sp") or not SAME_ENGINE_SYNC):
                continue
            if seen.get(k, 0) >= v:
                continue
            e.wait_ge(self.semobj[k], v)
            self.nwait += 1
            seen[k] = v

    def op(self, eng, fn, reads=(), writes=()):
        self._wait(eng, self._deps(reads, writes))
        ins = fn(self.engs[eng])
        self.cnt[eng] += 1
        tok = (eng, self.cnt[eng])
        ins.then_inc(self.semobj[eng], 1)
        self.ninstr += 1
        for b in reads:
            if b.r.get(eng, 0) < tok[1]:
                b.r[eng] = tok[1]
        for b in writes:
            b.w = tok
            b.r = {}
        return ins

    def dma(self, q, out, in_, reads=(), writes=(), fn=None, **kw):
        self._wait(q, self._deps(reads, writes))
        wb = writes[0]
        if wb.semkey is None:
            wb.semkey = "d%d_%s" % (self.nbuf, wb.name)
            self.nbuf += 1
            self.semobj[wb.semkey] = self.root_es.enter_context(self.nc.semaphore(wb.semkey))
        if fn is not None:
            ins = fn(self.engs[q])
            ins.then_inc(self.semobj[wb.semkey])
            wb.semv += 1
        else:
            ins = self.engs[q].dma_start(out=out, in_=in_, **kw)
            ins.then_inc(self.semobj[wb.semkey], 16)
            wb.semv += 16
        tok = (wb.semkey, wb.semv)
        self.dmasems[wb.semkey] = wb.semv
        self.ninstr += 1
        for b in reads:
            if b.r.get(tok[0], 0) < tok[1]:
                b.r[tok[0]] = tok[1]
        for b in writes:
            b.w = tok
            b.r = {}
        return ins

    def finish(self, outs):
        deps = {}
        for b in outs:
            if b.w is not None:
                deps[b.w[0]] = max(deps.get(b.w[0], 0), b.w[1])
        self._wait("sp", deps)

    def close(self):
        self.es.close()


D_MODEL = 1024
BATCH = 2
SEQ = 8192
DEPTH = 2
HD = 64
NCORE = 8
RANKS = 4
NT = 16
TOK = NT * 128
NG = 4
DC = D_MODEL // 128
D_PROJ = 2956
NFM = 17
NTM = 908
WP = NFM * 128 + NTM
D_FF = 4096
D_CAT = 768
RMS_EPS = 1e-6
NORM_BLOCKS = (0, 1, 2, 3, 5, 6, 7, 8, 9, 10)
Q_BLOCKS = (0, 1, 5, 6, 7, 11, 12, 13)
K_BLOCKS = (2, 3, 4, 8, 9, 10, 14, 15, 16)
NQB = len(Q_BLOCKS)
NKB = len(K_BLOCKS)
NVH = 14


def win_column_perm():
    cols = []
    A_KV = 256
    B0 = 652
    C0 = 1804
    cols += list(range(0, 128))
    cols += list(range(128, 256))
    ksel = list(range(A_KV + 128, A_KV + 192))
    kwin = list(range(A_KV + 256, A_KV + 320))
    cols += ksel + ksel
    cols += kwin + kwin
    cols += list(range(A_KV, A_KV + 128))
    for g in range(3):
        cols += list(range(B0 + g * 128, B0 + g * 128 + 128))
    for g in range(3):
        cols += list(range(B0 + (3 + g) * 128, B0 + (3 + g) * 128 + 128))
    for p in range(3):
        cols += list(range(C0 + p * 128, C0 + p * 128 + 128))
    for p in range(3):
        cols += list(range(C0 + 384 + p * 128, C0 + 384 + p * 128 + 128))
    assert len(cols) == NFM * 128
    cols += list(range(A_KV + 192, A_KV + 256))
    cols += list(range(A_KV + 320, A_KV + 384))
    cols += list(range(B0 + 768, B0 + 1152))
    cols += list(range(C0 + 768, C0 + 1152))
    cols += list(range(640, 652))
    assert len(cols) == WP
    return np.array(cols, dtype=np.int64)


def emit_phase_A(kb, x_src, xb_fn, io):
    nc = kb.nc
    kb.push_scope()
    W = kb.sb("A_W", [128, DC, WP], BF16)
    W_b = kb.buf("A_W")
    hT = kb.sb("A_hT", [128, DC, TOK], BF16)
    hT_b = [kb.buf("A_hT%d" % g) for g in range(NG)]
    gmix = kb.sb("A_gmix", [128, DC], F32)
    gfm = kb.sb("A_gfm", [128, NFM], F32)
    ident = kb.sb("A_ident", [128, 128], F32)
    blk = kb.sb("A_blk", [128, 128], BF16)
    cst_b = kb.buf("A_cst")
    blk_b = kb.buf("A_blk")
    xt = [kb.sb("A_xt%d" % i, [128, D_MODEL], F32) for i in range(2)]
    xt_b = [kb.buf("A_xt%d" % i) for i in range(2)]
    junk = kb.sb("A_junk", [128, D_MODEL], F32)
    junk_b = kb.buf("A_junk")
    xn = [kb.sb("A_xn%d" % i, [128, D_MODEL], F32) for i in range(2)]
    xn_b = [kb.buf("A_xn%d" % i) for i in range(2)]
    st = [kb.sb("A_st%d" % i, [128, 4], F32) for i in range(2)]
    st_b = [kb.buf("A_st%d" % i) for i in range(2)]
    NPS = 6
    ps = kb.bank[:NPS]
    ps_b = kb.bank_b[:NPS]
    sq = [kb.sb("A_sq%d" % i, [128, 512], BF16) for i in range(2)]
    sq_b = [kb.buf("A_sq%d" % i) for i in range(2)]
    lnb = [kb.sb("A_ln%d" % i, [128, 512], F32) for i in range(2)]
    lnb_b = [kb.buf("A_ln%d" % i) for i in range(2)]
    fo = [kb.sb("A_fo%d" % i, [128, 512], BF16) for i in range(3)]
    fo_b = [kb.buf("A_fo%d" % i) for i in range(3)]
    vo = [kb.sb("A_vo%d" % i, [128, 1024], BF16) for i in range(2)]
    vo_b = [kb.buf("A_vo%d" % i) for i in range(2)]
    go = kb.sb("A_go", [128, NT * 12], F32)
    go_b = kb.buf("A_go")
    epsb = kb.sb("A_eps", [128, 1], F32)
    eps_ap = epsb[:, 0:1]

    kb.op("pool", lambda e: e.memset(epsb[:], RMS_EPS), writes=[cst_b])
    for i in range(2):
        kb.op("pool", lambda e: e.memset(vo[i][:, 896:1024], 0.0), writes=[vo_b[i]])
    if io.get("zero_fill"):
        zt = kb.sb("A_zero", [128, 2048], BF16)
        zt_b = kb.buf("A_zero")
        kb.op("pool", lambda e: e.memset(zt[:], 0.0), writes=[zt_b])
        for (zap, zb) in io["zero_fill"]:
            kb.dma("sp", zap, zt[:], reads=[zt_b], writes=[zb])
    kb.dma("sp", gmix[:], io["gmix"], writes=[cst_b])
    kb.dma("sp", gfm[:], io["gfm"], writes=[cst_b])
    kb.dma("sp", ident[:], io["ident"], writes=[cst_b])
    kb.op("pool", lambda e: e.memset(blk[:], 0.0), writes=[blk_b])
    kb.op("pool", lambda e: e.memset(blk[0:64, 0:64], 1.0), writes=[blk_b])
    kb.op("pool", lambda e: e.memset(blk[64:128, 64:128], 1.0), writes=[blk_b])
    half = WP // 2
    for c in range(DC):
        for h0 in (0, half):
            kb.dma("pool", W[:, c, h0:h0 + half], io["winp"][c * 128:(c + 1) * 128, h0:h0 + half],
                   writes=[W_b])

    psi = [0]

    def next_ps():
        i = psi[0] % NPS
        psi[0] += 1
        return ps[i], ps_b[i]

    cnt = {"sq": 0, "fo": 0, "vo": 0}

    def tile_work(j):
        s = j % 2
        src_ap, src_b = x_src(j)
        kb.dma("pool", xt[s][:], src_ap, reads=[src_b] if src_b else [], writes=[xt_b[s]])
        kb.op("act", lambda e: e.activation(out=junk[:], in_=xt[s][:], func=AF.Square,
                                            accum_out=st[s][:, 0:1]),
              reads=[xt_b[s]], writes=[junk_b, st_b[s]])
        kb.op("dve", lambda e: e.tensor_scalar(out=st[s][:, 1:2], in0=st[s][:, 0:1],
                                               scalar1=1.0 / D_MODEL, scalar2=RMS_EPS,
                                               op0=ALU.mult, op1=ALU.add),
              reads=[st_b[s]], writes=[st_b[s]])
        kb.op("act", lambda e: e.activation(out=st[s][:, 2:3], in_=st[s][:, 1:2], func=AF.Sqrt),
              reads=[st_b[s]], writes=[st_b[s]])
        kb.op("dve", lambda e: e.reciprocal(out=st[s][:, 3:4], in_=st[s][:, 2:3]),
              reads=[st_b[s]], writes=[st_b[s]])
        kb.op("dve", lambda e: e.tensor_scalar(out=xn[s][:], in0=xt[s][:], scalar1=st[s][:, 3:4],
                                               scalar2=None, op0=ALU.mult),
              reads=[xt_b[s], st_b[s]], writes=[xn_b[s]])
        g = j // 4
        for hf in range(2):
            p, pb = next_ps()
            for cc in range(4):
                c = hf * 4 + cc
                kb.op("pe", lambda e: e.transpose(out=p[:, cc * 128:(cc + 1) * 128],
                                                  in_=xn[s][:, c * 128:(c + 1) * 128],
                                                  identity=ident[:]),
                      reads=[xn_b[s], cst_b], writes=[pb])
            for cc in range(4):
                c = hf * 4 + cc
                eng = "act" if cc % 2 == 0 else "dve"
                if eng == "act":
                    kb.op("act", lambda e: e.activation(out=hT[:, c, j * 128:(j + 1) * 128],
                                                        in_=p[:, cc * 128:(cc + 1) * 128],
                                                        func=AF.Copy, scale=gmix[:, c:c + 1]),
                          reads=[pb, cst_b], writes=[hT_b[g]])
                else:
                    kb.op("dve", lambda e: e.tensor_scalar(out=hT[:, c, j * 128:(j + 1) * 128],
                                                           in0=p[:, cc * 128:(cc + 1) * 128],
                                                           scalar1=gmix[:, c:c + 1], scalar2=None,
                                                           op0=ALU.mult),
                          reads=[pb, cst_b], writes=[hT_b[g]])

    def proj_gen(g):
        t0 = g * 512
        for blk_i in range(NFM):
            if "nofm" in DBG:
                break
            p, pb = next_ps()
            for c in range(DC):
                kb.op("pe", lambda e: e.matmul(p[:, :], lhsT=W[:, c, blk_i * 128:(blk_i + 1) * 128],
                                               rhs=hT[:, c, t0:t0 + 512],
                                               start=(c == 0), stop=(c == DC - 1)),
                      reads=[W_b, hT_b[g]], writes=[pb])
            fi = cnt["fo"] % 3
            cnt["fo"] += 1
            if blk_i in NORM_BLOCKS:
                si = cnt["sq"] % 2
                cnt["sq"] += 1
                kb.op("act", lambda e: e.activation(out=sq[si][:], in_=p[:, :], func=AF.Square),
                      reads=[pb], writes=[sq_b[si]])
                p2, p2b = next_ps()
                kb.op("pe", lambda e: e.matmul(p2[:, :], lhsT=blk[:], rhs=sq[si][:],
                                               start=True, stop=True),
                      reads=[blk_b, sq_b[si]], writes=[p2b])
                kb.op("act", lambda e: e.activation(out=lnb[si][:], in_=p2[:, :], func=AF.Ln,
                                                    scale=1.0 / HD, bias=eps_ap),
                      reads=[p2b, cst_b], writes=[lnb_b[si]])
                kb.op("act", lambda e: e.activation(out=lnb[si][:], in_=lnb[si][:], func=AF.Exp,
                                                    scale=-0.5),
                      reads=[lnb_b[si]], writes=[lnb_b[si]])
                kb.op("dve", lambda e: e.scalar_tensor_tensor(out=fo[fi][:], in0=p[:, :],
                                                              scalar=gfm[:, blk_i:blk_i + 1],
                                                              in1=lnb[si][:], op0=ALU.mult,
                                                              op1=ALU.mult),
                      reads=[pb, cst_b, lnb_b[si]], writes=[fo_b[fi]])
            else:
                kb.op("dve", lambda e: e.tensor_copy(out=fo[fi][:], in_=p[:, :]),
                      reads=[pb], writes=[fo_b[fi]])
            if blk_i in Q_BLOCKS:
                qi = Q_BLOCKS.index(blk_i)
                kb.dma("sp", io["QT"][qi, :, t0:t0 + 512], fo[fi][:], reads=[fo_b[fi]],
                       writes=[io["QT_b"]])
            else:
                ki = K_BLOCKS.index(blk_i)
                kb.dma("sp", io["KT_fn"](ki)[:, t0:t0 + 512], fo[fi][:], reads=[fo_b[fi]],
                       writes=[io["KT_bf"](ki)])
            yield
        for a in range(4):
            if "notm" in DBG:
                break
            jj = g * 4 + a
            vi = cnt["vo"] % 2
            cnt["vo"] += 1
            for half_i in range(2):
                c0 = NFM * 128 + half_i * 454
                p, pb = next_ps()
                for c in range(DC):
                    kb.op("pe", lambda e: e.matmul(p[:, 0:454], lhsT=hT[:, c, jj * 128:(jj + 1) * 128],
                                                   rhs=W[:, c, c0:c0 + 454],
                                                   start=(c == 0), stop=(c == DC - 1)),
                          reads=[W_b, hT_b[g]], writes=[pb])
                if half_i == 0:
                    kb.op("act", lambda e: e.activation(out=vo[vi][:, 0:454], in_=p[:, 0:454],
                                                        func=AF.Copy),
                          reads=[pb], writes=[vo_b[vi]])
                else:
                    kb.op("dve", lambda e: e.tensor_copy(out=vo[vi][:, 454:896], in_=p[:, 0:442]),
                          reads=[pb], writes=[vo_b[vi]])
                    kb.op("act", lambda e: e.activation(out=go[:, jj * 12:(jj + 1) * 12],
                                                        in_=p[:, 442:454], func=AF.Sigmoid),
                          reads=[pb], writes=[go_b])
            if "nov" not in DBG:
                kb.dma("sp", io["V_fn"](jj), vo[vi][:, 0:io.get("V_cols", 896)], reads=[vo_b[vi]],
                       writes=[io["V_bf"](jj)])
            if jj == NT - 1:
                kb.dma("sp", io["G"], go[:], reads=[go_b], writes=[io["G_b"]])
            yield

    for j in range(4):
        tile_work(j)
    for g in range(NG):
        if "noproj" in DBG:
            for a in range(4):
                if g + 1 < NG:
                    tile_work(4 * (g + 1) + a)
            continue
        nxt = [4 * (g + 1) + a for a in range(4)] if g + 1 < NG else []
        for k, _ in enumerate(proj_gen(g)):
            if k in (2, 7, 12, 17) and nxt:
                tile_work(nxt.pop(0))
        for j in nxt:
            tile_work(j)
    kb.pop_scope()


def core_rows(c):
    b, r = divmod(c, RANKS)
    idx = np.concatenate([np.arange((4 * j + r) * 128, (4 * j + r + 1) * 128) for j in range(NT)])
    return b, r, idx


def host_consts_A(norm_mix_l, qk_gain_nsa_l, qk_gain_dil_l):
    gmix = np.ascontiguousarray(norm_mix_l.reshape(DC, 128).T)
    gfm = np.ones((128, NFM), np.float32)
    two = lambda v: np.concatenate([v, v])
    gfm[:, 0] = two(qk_gain_nsa_l[0])
    gfm[:, 1] = two(qk_gain_nsa_l[0])
    gfm[:, 2] = two(qk_gain_nsa_l[2])
    gfm[:, 3] = two(qk_gain_nsa_l[3])
    for g in range(3):
        gfm[:, 5 + g] = two(qk_gain_dil_l[0])
        gfm[:, 8 + g] = two(qk_gain_dil_l[1])
    return gmix, gfm


def build_A():
    kb = KB()
    io = {}
    x = kb.dram("x_own", [TOK, D_MODEL], F32, "ExternalInput")
    io["winp"] = kb.dram("winp", [D_MODEL, WP], F32, "ExternalInput")
    io["gmix"] = kb.dram("gmix", [128, DC], F32, "ExternalInput")
    io["gfm"] = kb.dram("gfm", [128, NFM], F32, "ExternalInput")
    io["ident"] = kb.dram("ident", [128, 128], F32, "ExternalInput")
    io["QT"] = kb.dram_bf16("QT", [NQB, 128, TOK], "ExternalOutput")
    io["KT"] = kb.dram_bf16("KT", [NKB, 128, TOK], "ExternalOutput")
    io["V"] = kb.dram_bf16("V", [TOK, NVH * 64], "ExternalOutput")
    io["G"] = kb.dram("G", [128, NT * 12], F32, "ExternalOutput")
    for n in ("QT", "KT", "V", "G"):
        io[n + "_b"] = kb.buf(n)
    io["KT_bf"] = lambda ki: io["KT_b"]
    io["V_bf"] = lambda jj: io["V_b"]
    io["KT_fn"] = lambda ki: io["KT"][ki]
    io["V_fn"] = lambda jj: io["V"][jj * 128:(jj + 1) * 128, :]
    emit_phase_A(kb, lambda j: (x[j * 128:(j + 1) * 128, :], None), None, io)
    kb.finish([io[n + "_b"] for n in ("QT", "KT", "V", "G")])
    kb.close()
    return kb


DIL_PAIRS = ((128, 1), (512, 4), (2048, 16))
WIN_NSA = 512


def alibi_slopes_np():
    i = np.arange(1, 11, dtype=np.float64)
    return np.exp2(-8.0 * i / 10.0)


def dil_combos(g):
    w = DIL_PAIRS[g][0] // 128
    out = []
    for dj in range(0, (w + 3) // 4 + 1):
        for rp in range(4):
            if any(0 <= 4 * dj + r - rp <= w for r in range(4)):
                out.append((dj, rp))
    return out


DIL_COMBOS = [dil_combos(g) for g in range(3)]
N_DIL_TAB = sum(len(c) for c in DIL_COMBOS) * 2


def bf16_pack(a):
    a = np.ascontiguousarray(np.asarray(a, dtype=np.float32).astype(NPBF))
    return a.view(np.float32)


def host_tables(r):
    sl = alibi_slopes_np()
    sl_dil, sl_nsa = sl[:6], sl[6:]
    ki = np.arange(128)[:, None].astype(np.float64)
    qi = np.arange(128)[None, :].astype(np.float64)
    T = {}
    ms = np.zeros((128, 4, 128))
    mc = np.zeros((128, 4, 128))
    for rp in range(4):
        if rp < r:
            ms[:, rp] = 1
            mc[:, rp] = 1
        elif rp == r:
            ms[:, rp] = (ki < qi)
            mc[:, rp] = (ki <= qi)
    T["mstrict"] = bf16_pack(ms.reshape(128, 512))
    T["mcausal"] = bf16_pack(mc.reshape(128, 512))
    tw = np.zeros((128, 4, 2, 4, 128))
    for h in range(4):
        for dj in range(2):
            for rp in range(4):
                d = 128 * (4 * dj + r - rp) + qi - ki
                tw[:, h, dj, rp] = ((d >= 0) & (d < WIN_NSA)) * np.exp(-sl_nsa[h] * np.maximum(d, 0))
    T["twin"] = bf16_pack(tw.reshape(128, -1))
    td = np.zeros((128, N_DIL_TAB, 128))
    idx = 0
    for g, (w, rd) in enumerate(DIL_PAIRS):
        for hh in range(2):
            for (dj, rp) in DIL_COMBOS[g]:
                d = 128 * (4 * dj + r - rp) + qi - ki
                ok = (d >= 0) & (d <= w) & (np.mod(d, rd) == 0)
                td[:, idx] = ok * np.exp(-sl_dil[2 * g + hh] * np.maximum(d, 0))
                idx += 1
    T["tdil"] = bf16_pack(td.reshape(128, -1))
    cm = np.zeros((128, 2, 4, 128))
    ni = ki
    for dl in range(2):
        for a in range(4):
            cm[:, dl, a] = (2048 * dl + 128 * (4 * a + r) + qi - 16 * ni - 31 >= 0)
    T["cmask"] = bf16_pack(cm.reshape(128, -1))
    bs = np.zeros((128, 4, 4, 128), np.float32)
    qcol = np.arange(128)[:, None]
    jj = np.arange(128)[None, :]
    for i in range(4):
        for a in range(4):
            cur = 32 * i + 8 * a + 2 * r + (qcol >= 64)
            b = np.where((jj == cur) | (jj == cur - 1), 1e4, 0.0) + np.where(jj == 0, 1e4, 0.0)
            b = np.where(jj > cur, -1e30, b)
            bs[:, i, a] = b
    T["bsel"] = bs.reshape(128, -1)
    u = np.arange(64)[None, :]
    bsl = np.zeros((128, 4, 64), np.float32)
    bcm = np.zeros((128, 4, 4), np.float32)
    for h in range(4):
        bsl[:, h] = sl_nsa[h] * (128 * (u - 48) + ki)
        for dl in range(4):
            bcm[:, h, dl] = (sl_nsa[h] * (-2048 * dl + 16 * ni + 31))[:, 0]
    T["bias_sel"] = bsl.reshape(128, -1)
    T["bias_cmp"] = bcm.reshape(128, -1)
    ov = np.zeros((128, 4, 128))
    for nt in range(4):
        n = 128 * nt + np.arange(128)[:, None]
        ov[:, nt] = (16 * n <= 64 * jj + 63) & (16 * n + 31 >= 64 * jj)
    T["ov"] = bf16_pack(ov.reshape(128, -1))
    e32 = np.zeros((128, 32, 128))
    kk = np.arange(128)[None, :]
    jj32 = (np.arange(128) % 64)[:, None]
    for uu in range(32):
        e32[:, uu] = 32768.0 * (jj32 == 2 * uu + kk // 64)
    T["e32"] = bf16_pack(e32.reshape(128, -1))
    tri = (np.arange(128)[:, None] >= np.arange(128)[None, :]).astype(np.float64)
    T["tri"] = bf16_pack(tri)
    T["identb"] = bf16_pack(np.eye(128))
    return T


TABLE_SHAPES = {
    "mstrict": (512, True), "mcausal": (512, True), "twin": (4 * 2 * 4 * 128, True),
    "tdil": (N_DIL_TAB * 128, True), "cmask": (2 * 512, True), "bsel": (4 * 4 * 128, False),
    "bias_sel": (256, False), "bias_cmp": (16, False), "ov": (512, True), "e32": (32 * 128, True),
    "tri": (128, True), "identb": (128, True),
}

TABH_ORDER = ("mstrict", "mcausal", "twin", "tdil", "cmask", "ov", "e32", "tri", "identb")
TABF_ORDER = ("bsel", "bias_sel", "bias_cmp")
TABH_OFF = {}
_o = 0
for _n in TABH_ORDER:
    TABH_OFF[_n] = _o
    _o += TABLE_SHAPES[_n][0]
TABH_COLS = _o
TABF_OFF = {}
_o = 0
for _n in TABF_ORDER:
    TABF_OFF[_n] = _o
    _o += TABLE_SHAPES[_n][0]
TABF_COLS = _o


def host_table_arrays(r):
    T = host_tables(r)
    tabh = np.concatenate([T[n] for n in TABH_ORDER], axis=1)
    tabf = np.concatenate([T[n] for n in TABF_ORDER], axis=1).astype(np.float32)
    assert tabh.shape == (128, TABH_COLS // 2) and tabf.shape == (128, TABF_COLS)
    return np.ascontiguousarray(tabh), np.ascontiguousarray(tabf)


def run_pipeline(units, stages):
    n = len(units)
    S = len(stages)
    for it in range(n + S - 1):
        for s in range(S):
            idx = it - s
            if 0 <= idx < n:
                stages[s](units[idx])


def emit_mixers(kb, io, cat, cat_b):
    nc = kb.nc
    kb.push_scope()
    bank, bank_b = kb.bank, kb.bank_b
    QTs, QT_b = kb.sbt("M_QT", [128, NQB, TOK], BF16)
    tabh, tabh_b = kb.sbt("M_tabh", [128, TABH_COLS], BF16)
    tabf, tabf_b = kb.sbt("M_tabf", [128, TABF_COLS], F32)
    gs, gs_b = kb.sbt("M_G", [128, NT * 12], F32)
    ones_bf, cst_b = kb.sbt("M_ones", [128, 128], BF16)
    onec = kb.sb("M_onec", [128, 1], F32)
    ocomb, ocomb_b = kb.sbt("M_ocomb", [128, NT, 4, HD], F32)
    negselT, negselT_b = kb.sbt("M_negselT", [128, 2, TOK], BF16)
    kslot = [kb.sbt("M_k%d" % i, [128, RANKS, TOK], BF16) for i in range(2)]
    vslot = [kb.sbt("M_v%d" % i, [128, RANKS * NT, HD + 1], BF16) for i in range(2)]
    cnt = {"k": 0, "v": 0}

    def tab(name, lo, n):
        o = TABH_OFF[name] + lo
        return tabh[:, o:o + n]

    def tabfp(name, lo, n):
        o = TABF_OFF[name] + lo
        return tabf[:, o:o + n]

    kb.dma("sp", QTs[:], io["QT"].rearrange("b p t -> p b t"), reads=[io["QT_b"]], writes=[QT_b])
    hc = TABH_COLS // 4
    for q in range(4):
        kb.dma("sp", tabh[:, q * hc:(q + 1) * hc], io["tabh"][:, q * hc:(q + 1) * hc], writes=[tabh_b])
    kb.dma("sp", tabf[:], io["tabf"], writes=[tabf_b])
    kb.dma("sp", gs[:], io["G"], reads=[io["G_b"]], writes=[gs_b])
    kb.op("pool", lambda e: e.memset(ones_bf[:], 1.0), writes=[cst_b])
    kb.op("pool", lambda e: e.memset(onec[:], 1.0), writes=[cst_b])
    for i in range(2):
        kb.op("pool", lambda e: e.memset(vslot[i][0][:, :, HD:HD + 1], 1.0), writes=[vslot[i][1]])

    def load_k(kbi):
        s = cnt["k"] % 2
        cnt["k"] += 1
        t, b = kslot[s]
        for rp in range(RANKS):
            kb.dma("sp", t[:, rp, :], io["KTa_fn"](rp, kbi), reads=[io["KTa_bf"](kbi)], writes=[b])
        return t, b

    def load_v(vh):
        s = cnt["v"] % 2
        cnt["v"] += 1
        t, b = vslot[s]
        for rp in range(RANKS):
            for q in range(4):
                src = io["Va_fn"](rp, q, vh)
                kb.dma("sp", t[:, rp * NT + q * 4:rp * NT + q * 4 + 4, 0:HD],
                       src.rearrange("(j p) d -> p j d", p=128), reads=[io["Va_bf"](q)], writes=[b])
        return t, b

    class U:
        pass

    def sb_units(hh, ks, vs):
        units = []
        n = 0
        for i in range(NG):
            first = True
            for jp in range(4 * i + 3, -1, -1):
                for rp in range(3, -1, -1):
                    u = U()
                    u.n = n
                    n += 1
                    u.i, u.jp, u.rp = i, jp, rp
                    u.masked = jp >= 4 * i
                    u.a0 = jp - 4 * i if u.masked else 0
                    u.c0 = 128 * u.a0
                    u.first = first
                    first = False
                    u.last = (jp == 0 and rp == 0)
                    units.append(u)
        return units

    def emit_sb_head(h, ks, vs):
        kt, ktb = ks
        vt, vtb = vs
        hp, hh = divmod(h, 2)
        rows = slice(64 * hh, 64 * hh + 64)
        qb = 5 + hp

        ZB = (0, 1, 6)

        def s1(u):
            zb, zbb = bank[ZB[u.n % 3]], bank_b[ZB[u.n % 3]]
            q0 = 512 * u.i + u.c0
            q1 = 512 * u.i + 512
            kb.op("pe", lambda e: e.matmul(zb[:, u.c0:512], lhsT=kt[rows, u.rp, u.jp * 128:(u.jp + 1) * 128],
                                           rhs=QTs[rows, qb, q0:q1], start=True, stop=True),
                  reads=[ktb, QT_b], writes=[zbb])

        def s1a(u):
            zb, zbb = bank[ZB[u.n % 3]], bank_b[ZB[u.n % 3]]
            et, etb = e_sb[u.n % 6]
            kb.op("act", lambda e: e.activation(out=et[:, u.c0:512], in_=zb[:, u.c0:512], func=AF.Exp,
                                                scale=0.125),
                  reads=[zbb], writes=[etb])

        def s1b(u):
            et, etb = e_sb[u.n % 6]
            st, stb = sp_sb[u.n % 3]
            kb.op("act", lambda e: e.activation(out=st[:, u.c0:512], in_=et[:, u.c0:512], func=AF.Ln,
                                                bias=1.0),
                  reads=[etb], writes=[stb])
            if u.masked:
                kb.op("pool", lambda e: e.tensor_tensor(out=st[:, u.c0:u.c0 + 128], in0=st[:, u.c0:u.c0 + 128],
                                                        in1=tab("mstrict", u.rp * 128, 128), op=ALU.mult),
                      reads=[stb, tabh_b], writes=[stb])

        def s2(u):
            gb, gbb = bank[2 + u.n % 2], bank_b[2 + u.n % 2]
            st, stb = sp_sb[u.n % 3]
            q0 = 512 * u.i + u.c0
            q1 = 512 * u.i + 512
            if u.first:
                kb.op("pool", lambda e: e.memset(spacc[:], 0.0), writes=[spacc_b])
            kb.op("pe", lambda e: e.matmul(gb[:, u.c0:512], lhsT=tab("tri", 0, 128), rhs=st[:, u.c0:512],
                                           start=True, stop=u.first),
                  reads=[tabh_b, stb], writes=[gbb])
            if not u.first:
                kb.op("pe", lambda e: e.matmul(gb[:, u.c0:512], lhsT=ones_bf[:], rhs=spacc[:, u.c0:512],
                                               start=False, stop=True),
                      reads=[cst_b, spacc_b], writes=[gbb])
            if not u.last:
                kb.op("dve", lambda e: e.tensor_tensor(out=spacc[:, u.c0:512], in0=spacc[:, u.c0:512],
                                                       in1=st[:, u.c0:512], op=ALU.add),
                      reads=[spacc_b, stb], writes=[spacc_b])

        def s2b(u):
            gb, gbb = bank[2 + u.n % 2], bank_b[2 + u.n % 2]
            at, atb = aT_sb[u.n % 3]
            xt, xtb = x_sb[u.n % 2]
            et, etb = e_sb[u.n % 6]
            kb.op("act", lambda e: e.activation(out=xt[:, u.c0:512], in_=gb[:, u.c0:512], func=AF.Exp,
                                                scale=-1.0),
                  reads=[gbb], writes=[xtb])
            kb.op("dve", lambda e: e.tensor_tensor(out=at[:, u.c0:512], in0=xt[:, u.c0:512], in1=et[:, u.c0:512],
                                                   op=ALU.mult),
                  reads=[xtb, etb], writes=[atb])
            if u.masked:
                kb.op("pool", lambda e: e.tensor_tensor(out=at[:, u.c0:u.c0 + 128], in0=at[:, u.c0:u.c0 + 128],
                                                        in1=tab("mstrict", u.rp * 128, 128), op=ALU.mult),
                      reads=[atb, tabh_b], writes=[atb])

        def s3(u):
            ab, abb = bank[4 + u.i % 2], bank_b[4 + u.i % 2]
            at, atb = aT_sb[u.n % 3]
            for a in range(u.a0, 4):
                kb.op("pe", lambda e: e.matmul(ab[:, a * HD:(a + 1) * HD], lhsT=at[:, a * 128:(a + 1) * 128],
                                               rhs=vt[:, u.rp * NT + u.jp, 0:HD],
                                               start=(u.first and a == u.a0), stop=(u.last and a == 3)),
                      reads=[atb, vtb], writes=[abb])
            if u.last:
                kb.op("dve", lambda e: e.tensor_copy(
                    out=cat[:, 4 * u.i:4 * u.i + 4, 384 + h * HD:384 + (h + 1) * HD],
                    in_=ab[:, 0:4 * HD].rearrange("p (a d) -> p a d", d=HD)),
                    reads=[abb], writes=[cat_b])

        run_pipeline(sb_units(hh, ks, vs), [s1, s1a, s1b, s2, s2b, s3])


    HD1 = HD + 1
    p_sb = [kb.sbt("M_p%d" % i, [128, 512], BF16) for i in range(4)]
    r4 = [kb.sbt("M_r4%d" % i, [128, 8], F32) for i in range(4)]
    r4c = [0]

    def acc_view(ab, lo, n):
        return ab[:, 0:4 * HD1].rearrange("p (a d) -> p a d", d=HD1)[:, :, lo:lo + n]

    def recip_den(ab, abb, i, gate_col):
        t, tb = r4[r4c[0] % 4]
        r4c[0] += 1
        kb.op("dve", lambda e: e.tensor_scalar(out=t[:, 0:4], in0=acc_view(ab, HD, 1), scalar1=1e-30,
                                               scalar2=None, op0=ALU.max),
              reads=[abb], writes=[tb])
        kb.op("dve", lambda e: e.reciprocal(out=t[:, 0:4], in_=t[:, 0:4]), reads=[tb], writes=[tb])
        if gate_col is not None:
            gv = gs[:, 4 * i * 12:(4 * i + 4) * 12].rearrange("p (a k) -> p a k", k=12)[:, :, gate_col]
            kb.op("dve", lambda e: e.tensor_tensor(out=t[:, 4:8], in0=t[:, 0:4], in1=gv, op=ALU.mult),
                  reads=[tb, gs_b], writes=[tb])
        return t, tb

    def evac_nsa(ab, abb, i, h, br, first):
        t, tb = recip_den(ab, abb, i, 3 * h + br)
        for a in range(4):
            src = ab[:, a * HD1:a * HD1 + HD]
            dst = ocomb[:, 4 * i + a, h, :]
            if first:
                kb.op("dve", lambda e: e.tensor_scalar(out=dst, in0=src, scalar1=t[:, 4 + a:5 + a], scalar2=None,
                                                       op0=ALU.mult),
                      reads=[abb, tb], writes=[ocomb_b])
            else:
                kb.op("dve", lambda e: e.scalar_tensor_tensor(out=dst, in0=src, scalar=t[:, 4 + a:5 + a], in1=dst,
                                                              op0=ALU.mult, op1=ALU.add),
                      reads=[abb, tb, ocomb_b], writes=[ocomb_b])
        return t, tb

    def emit_banded(kt, ktb, vt, vtb, rows, qblk, tabname, tab0, combos, bank_s, bank_acc, evac):
        units = []
        n = 0
        for i in range(NG):
            glist = []
            for a in range(4):
                j = 4 * i + a
                valid = [(ci, dj, rp) for ci, (dj, rp) in enumerate(combos) if j - dj >= 0]
                for c0 in range(0, len(valid), 4):
                    u = U()
                    u.i, u.a, u.j = i, a, j
                    u.sub = valid[c0:c0 + 4]
                    glist.append(u)
            for k, u in enumerate(glist):
                u.n = n
                n += 1
                u.first = (k == 0)
                u.last = (k == len(glist) - 1)
                units.append(u)

        def s1(u):
            sbk, sbb = bank[bank_s[u.n % 2]], bank_b[bank_s[u.n % 2]]
            for q, (ci, dj, rp) in enumerate(u.sub):
                jp = u.j - dj
                kb.op("pe", lambda e: e.matmul(sbk[:, q * 128:(q + 1) * 128],
                                               lhsT=kt[rows, rp, jp * 128:(jp + 1) * 128],
                                               rhs=QTs[rows, qblk, u.j * 128:(u.j + 1) * 128],
                                               start=True, stop=True),
                      reads=[ktb, QT_b], writes=[sbb])

        def s2(u):
            sbk, sbb = bank[bank_s[u.n % 2]], bank_b[bank_s[u.n % 2]]
            pt, ptb = p_sb[u.n % 4]
            w = 128 * len(u.sub)
            kb.op("act", lambda e: e.activation(out=pt[:, 0:w], in_=sbk[:, 0:w], func=AF.Exp, scale=0.125),
                  reads=[sbb], writes=[ptb])

        def s2b(u):
            pt, ptb = p_sb[u.n % 4]
            w = 128 * len(u.sub)
            ci0 = u.sub[0][0]
            kb.op("dve", lambda e: e.tensor_tensor(out=pt[:, 0:w], in0=pt[:, 0:w],
                                                   in1=tab(tabname, (tab0 + ci0) * 128, w), op=ALU.mult),
                  reads=[ptb, tabh_b], writes=[ptb])

        def s3(u):
            ab, abb = bank[bank_acc[u.i % 2]], bank_b[bank_acc[u.i % 2]]
            pt, ptb = p_sb[u.n % 4]
            for q, (ci, dj, rp) in enumerate(u.sub):
                jp = u.j - dj
                kb.op("pe", lambda e: e.matmul(ab[:, u.a * HD1:(u.a + 1) * HD1], lhsT=pt[:, q * 128:(q + 1) * 128],
                                               rhs=vt[:, rp * NT + jp, 0:HD1],
                                               start=(u.first and q == 0),
                                               stop=(u.last and q == len(u.sub) - 1)),
                      reads=[ptb, vtb], writes=[abb])
            if u.last:
                evac(ab, abb, u.i)

        run_pipeline(units, [s1, s2, s2b, s3])

    def emit_dil():
        kb.push_scope()
        dacc, dacc_b = kb.sbt("M_dacc", [128, NT, 2, HD1], F32)
        tab0 = 0
        for g in range(3):
            ks = load_k(3 + g)
            for hh in range(2):
                vs = load_v(2 + 2 * g + hh)

                def evac_d(ab, abb, i, g=g, hh=hh):
                    dst = dacc[:, 4 * i:4 * i + 4, hh, :]
                    src = acc_view(ab, 0, HD1)
                    if g == 0:
                        kb.op("dve", lambda e: e.tensor_copy(out=dst, in_=src), reads=[abb], writes=[dacc_b])
                    else:
                        kb.op("dve", lambda e: e.tensor_tensor(out=dst, in0=src, in1=dst, op=ALU.add),
                              reads=[abb, dacc_b], writes=[dacc_b])

                emit_banded(ks[0], ks[1], vs[0], vs[1], slice(64 * hh, 64 * hh + 64), 2 + g, "tdil", tab0,
                            DIL_COMBOS[g], (0, 1), (2, 3), evac_d)
                tab0 += len(DIL_COMBOS[g])
        dr, dr_b = kb.sbt("M_dr", [128, NT * 2], F32)
        dflat = dacc[:].rearrange("p j h d -> p (j h) d")
        kb.op("dve", lambda e: e.reciprocal(out=dr[:], in_=dflat[:, :, HD]), reads=[dacc_b], writes=[dr_b])
        for j in range(NT):
            for hh in range(2):
                kb.op("dve", lambda e: e.tensor_scalar(out=cat[:, j, 256 + hh * HD:256 + (hh + 1) * HD],
                                                       in0=dacc[:, j, hh, 0:HD],
                                                       scalar1=dr[:, 2 * j + hh:2 * j + hh + 1], scalar2=None,
                                                       op0=ALU.mult),
                      reads=[dacc_b, dr_b], writes=[cat_b])
        kb.pop_scope()


    def emit_nsa():
        kb.push_scope()
        _xs = cnt["k"] % 2
        cnt["k"] += 1
        XT_b = kslot[_xs][1]
        W1s, W1_b = kb.sbt("N_W1", [128, 32, 128], BF16)
        W2s, W2_b = kb.sbt("N_W2", [128, 128], BF16)
        peTf, pe_b = kb.sbt("N_peTf", [128, 32], F32)
        peTb, peb_b = kb.sbt("N_peTb", [128, 32], BF16)
        gkc, gkc_b = kb.sbt("N_gkc", [128, HD], F32)
        hTc = [kb.sbt("N_hT%d" % c, [128, 512], BF16) for c in range(2)]
        u_sb, u_b = kb.sbt("N_u", [128, 512], F32)
        t_sb, t_b = kb.sbt("N_t", [128, 512], F32)
        bias2, bias2_b = kb.sbt("N_bias2", [128, 2], F32)
        KcT2, KcT_b = kb.sbt("N_KcT2", [128, 512], BF16)
        Vc, Vc_b = kb.sbt("N_Vc", [128, 4, HD1], BF16)
        kc2, kc2_b = kb.sbt("N_kc2", [128, 128], BF16)
        stt, stt_b = kb.sbt("N_st", [128, 4], F32)
        epsb, epsb_b = kb.sbt("N_eps", [128, 1], F32)
        imp, imp_b = kb.sbt("N_imp", [128, 512], F32)
        impb, impb_b = kb.sbt("N_impb", [128, 512], F32)
        tmpm, tmpm_b = kb.sbt("N_tmpm", [128, 128], F32)
        m8, m8_b = kb.sbt("N_m8", [128, 16], F32)
        nsel, nsel_b = kb.sbt("N_nsel", [128, 128], BF16)
        nsel2, nsel2_b = kb.sbt("N_nsel2", [128, 128], BF16)

        XTflat = kslot[_xs][0][:].rearrange("p r t -> p (r t)")
        xv = XTflat.rearrange("p (j r q) -> p j r q", r=RANKS, q=128)
        for rp in range(RANKS):
            kb.dma("sp", xv[:, :, rp, :], io["KTa_fn"](rp, 2).rearrange("p (j q) -> p j q", q=128),
                   reads=[io["KTa_bf"](2)], writes=[XT_b])
        for hf in range(2):
            kb.dma("pool", W1s[:, hf * 16:(hf + 1) * 16, :].rearrange("p l h -> p (l h)"),
                   io["w1p"][:, hf * 2048:(hf + 1) * 2048], writes=[W1_b])
        kb.dma("pool", W2s[:], io["w2p"], writes=[W2_b])
        kb.dma("sp", peTf[:], io["peT"], writes=[pe_b])
        kb.dma("sp", gkc[:], io["gkc"], writes=[gkc_b])
        kb.op("dve", lambda e: e.tensor_copy(out=peTb[:], in_=peTf[:]), reads=[pe_b], writes=[peb_b])
        kb.op("pool", lambda e: e.memset(epsb[:], RMS_EPS), writes=[epsb_b])
        kb.op("pool", lambda e: e.memset(Vc[:], 1.0), writes=[Vc_b])
        for c in range(2):
            kb.op("pool", lambda e: e.memset(hTc[c][0][:], 0.0), writes=[hTc[c][1]])
        xs = XTflat.rearrange("p (n s) -> p n s", s=16)
        if "n1" in DBG:
            kb.pop_scope()
            return
        for c in range(2):
            rows = slice(64 * c, 64 * c + 64)
            for l in range(32):
                kb.op("pe", lambda e: e.matmul(bank[2][:, 32 * c + l:32 * c + l + 1], lhsT=W1s[rows, l, :],
                                               rhs=peTb[rows, l:l + 1], start=True, stop=True),
                      reads=[W1_b, peb_b], writes=[bank_b[2]])
        kb.op("dve", lambda e: e.tensor_reduce(out=bias2[:], in_=bank[2][:, 0:64].rearrange("p (c l) -> p c l", l=32),
                                               axis=mybir.AxisListType.X, op=ALU.add),
              reads=[bank_b[2]], writes=[bias2_b])
        for c in range(2):
            rows = slice(64 * c, 64 * c + 64)
            pbk, pbb = bank[c], bank_b[c]
            for l in range(32):
                rhs = xs[rows, 0:511, l] if l < 16 else xs[rows, 1:512, l - 16]
                kb.op("pe", lambda e: e.matmul(pbk[:, 0:511], lhsT=W1s[rows, l, :], rhs=rhs,
                                               start=(l == 0), stop=(l == 31)),
                      reads=[W1_b, XT_b], writes=[pbb])
            kb.op("act", lambda e: e.activation(out=u_sb[:, 0:511], in_=pbk[:, 0:511], func=AF.Identity,
                                                bias=bias2[:, c:c + 1]),
                  reads=[pbb, bias2_b], writes=[u_b])
            kb.op("dve", lambda e: e.tensor_tensor(out=t_sb[:, 0:511], in0=u_sb[:, 0:511], in1=u_sb[:, 0:511],
                                                   op=ALU.mult), reads=[u_b], writes=[t_b])
            kb.op("dve", lambda e: e.tensor_scalar(out=t_sb[:, 0:511], in0=t_sb[:, 0:511], scalar1=0.044715,
                                                   scalar2=1.0, op0=ALU.mult, op1=ALU.add),
                  reads=[t_b], writes=[t_b])
            kb.op("dve", lambda e: e.tensor_tensor(out=t_sb[:, 0:511], in0=t_sb[:, 0:511], in1=u_sb[:, 0:511],
                                                   op=ALU.mult), reads=[t_b, u_b], writes=[t_b])
            kb.op("act", lambda e: e.activation(out=t_sb[:, 0:511], in_=t_sb[:, 0:511], func=AF.Sigmoid,
                                                scale=1.5957691216057308), reads=[t_b], writes=[t_b])
            kb.op("dve", lambda e: e.tensor_tensor(out=hTc[c][0][:, 0:511], in0=t_sb[:, 0:511],
                                                   in1=u_sb[:, 0:511], op=ALU.mult),
                  reads=[t_b, u_b], writes=[hTc[c][1]])
        if "n2" in DBG:
            kb.pop_scope()
            return
        for c in range(2):
            for nt in range(4):
                pk, pkb = bank[3 + nt % 2], bank_b[3 + nt % 2]
                kb.op("pe", lambda e: e.matmul(pk[:, 0:HD], lhsT=hTc[c][0][:, nt * 128:(nt + 1) * 128],
                                               rhs=W2s[:, c * HD:(c + 1) * HD], start=True, stop=True),
                      reads=[hTc[c][1], W2_b], writes=[pkb])
                if c == 1:
                    kb.op("act", lambda e: e.activation(out=Vc[:, nt, 0:HD], in_=pk[:, 0:HD], func=AF.Copy),
                          reads=[pkb], writes=[Vc_b])
                    continue
                kb.op("act", lambda e: e.activation(out=t_sb[:, 0:HD], in_=pk[:, 0:HD], func=AF.Square,
                                                    accum_out=stt[:, 0:1]),
                      reads=[pkb], writes=[t_b, stt_b])
                kb.op("dve", lambda e: e.tensor_scalar(out=stt[:, 1:2], in0=stt[:, 0:1], scalar1=1.0 / HD,
                                                       scalar2=RMS_EPS, op0=ALU.mult, op1=ALU.add),
                      reads=[stt_b], writes=[stt_b])
                kb.op("act", lambda e: e.activation(out=stt[:, 2:3], in_=stt[:, 1:2], func=AF.Sqrt),
                      reads=[stt_b], writes=[stt_b])
                kb.op("dve", lambda e: e.reciprocal(out=stt[:, 3:4], in_=stt[:, 2:3]), reads=[stt_b],
                      writes=[stt_b])
                kb.op("dve", lambda e: e.scalar_tensor_tensor(out=kc2[:, 0:HD], in0=pk[:, 0:HD],
                                                              scalar=stt[:, 3:4], in1=gkc[:], op0=ALU.mult,
                                                              op1=ALU.mult),
                      reads=[pkb, stt_b, gkc_b], writes=[kc2_b])
                kb.op("dve", lambda e: e.tensor_copy(out=kc2[:, HD:2 * HD], in_=kc2[:, 0:HD]), reads=[kc2_b],
                      writes=[kc2_b])
                kb.op("pe", lambda e: e.transpose(out=kb.bankh[:, 0:128], in_=kc2[:], identity=tab("identb", 0, 128)),
                      reads=[kc2_b, tabh_b], writes=[kb.bankh_b])
                kb.op("act", lambda e: e.activation(out=KcT2[:, nt * 128:(nt + 1) * 128], in_=kb.bankh[:, 0:128],
                                                    func=AF.Copy), reads=[kb.bankh_b], writes=[KcT_b])
        if "n3" in DBG:
            kb.op("dve", lambda e: e.tensor_copy(out=cat[:, 0, 0:512], in_=KcT2[:]), reads=[KcT_b], writes=[cat_b])
            kb.op("dve", lambda e: e.tensor_copy(out=cat[:, 1, 0:260], in_=Vc[:].rearrange("p a d -> p (a d)")),
                  reads=[Vc_b], writes=[cat_b])
            kb.op("dve", lambda e: e.tensor_copy(out=cat[:, 2, 0:512], in_=hTc[0][0][:]), reads=[hTc[0][1]], writes=[cat_b])
            kb.op("dve", lambda e: e.tensor_copy(out=cat[:, 3, 0:512], in_=hTc[1][0][:]), reads=[hTc[1][1]], writes=[cat_b])
            kb.op("dve", lambda e: e.tensor_copy(out=cat[:, 4, 0:2], in_=bias2[:]), reads=[bias2_b], writes=[cat_b])
            kb.pop_scope()
            return

        for i in range(NG):
            for h in range(4):
                rows = slice(64 * (h % 2), 64 * (h % 2) + 64)
                accC, accC_b = bank[2 + 2 * (h % 2)], bank_b[2 + 2 * (h % 2)]
                accI, accI_b = bank[3 + 2 * (h % 2)], bank_b[3 + 2 * (h % 2)]
                for nt in range(i + 1):
                    dl = i - nt
                    sbk, sbb = bank[nt % 2], bank_b[nt % 2]
                    pt, ptb = p_sb[nt % 3]
                    kb.op("pe", lambda e: e.matmul(sbk[:, :], lhsT=KcT2[rows, nt * 128:(nt + 1) * 128],
                                                   rhs=QTs[rows, h // 2, 512 * i:512 * i + 512],
                                                   start=True, stop=True),
                          reads=[KcT_b, QT_b], writes=[sbb])
                    kb.op("act", lambda e: e.activation(out=pt[:], in_=sbk[:, :], func=AF.Exp, scale=0.125,
                                                        bias=tabfp("bias_cmp", 4 * h + dl, 1)),
                          reads=[sbb, tabf_b], writes=[ptb])
                    if dl <= 1:
                        kb.op("pool", lambda e: e.tensor_tensor(out=pt[:], in0=pt[:],
                                                                in1=tab("cmask", 512 * dl, 512), op=ALU.mult),
                              reads=[ptb, tabh_b], writes=[ptb])
                    for a in range(4):
                        kb.op("pe", lambda e: e.matmul(accC[:, a * HD1:(a + 1) * HD1],
                                                       lhsT=pt[:, a * 128:(a + 1) * 128], rhs=Vc[:, nt, :],
                                                       start=(nt == 0 and a == 0), stop=(nt == i and a == 3)),
                              reads=[ptb, Vc_b], writes=[accC_b])
                        kb.op("pe", lambda e: e.matmul(accI[:, a * 128:(a + 1) * 128],
                                                       lhsT=pt[:, a * 128:(a + 1) * 128],
                                                       rhs=tab("ov", nt * 128, 128),
                                                       start=(nt == 0 and a == 0), stop=(nt == i and a == 3)),
                              reads=[ptb, tabh_b], writes=[accI_b])
                t, tb = evac_nsa(accC, accC_b, i, h, 0, True)
                for a in range(4):
                    src = accI[:, a * 128:(a + 1) * 128]
                    dst = imp[:, a * 128:(a + 1) * 128]
                    if h == 0:
                        kb.op("dve", lambda e: e.tensor_scalar(out=dst, in0=src, scalar1=t[:, a:a + 1],
                                                               scalar2=None, op0=ALU.mult),
                              reads=[accI_b, tb], writes=[imp_b])
                    else:
                        kb.op("dve", lambda e: e.scalar_tensor_tensor(out=dst, in0=src, scalar=t[:, a:a + 1],
                                                                      in1=dst, op0=ALU.mult, op1=ALU.add),
                              reads=[accI_b, tb, imp_b], writes=[imp_b])
            kb.op("dve", lambda e: e.tensor_tensor(out=impb[:], in0=imp[:], in1=tabfp("bsel", 512 * i, 512),
                                                   op=ALU.add), reads=[imp_b, tabf_b], writes=[impb_b])
            for a in range(4):
                iv = impb[:, a * 128:(a + 1) * 128]
                kb.op("dve", lambda e: e.max(out=m8[:, 0:8], in_=iv), reads=[impb_b], writes=[m8_b])
                kb.op("dve", lambda e: e.match_replace(out=tmpm[:], in_to_replace=m8[:, 0:8], in_values=iv,
                                                       imm_value=-3.0e38),
                      reads=[impb_b, m8_b], writes=[tmpm_b])
                kb.op("dve", lambda e: e.max(out=m8[:, 8:16], in_=tmpm[:]), reads=[tmpm_b], writes=[m8_b])
                kb.op("dve", lambda e: e.tensor_scalar(out=nsel[:], in0=iv, scalar1=m8[:, 15:16], scalar2=-1.0,
                                                       op0=ALU.is_ge, op1=ALU.add),
                      reads=[impb_b, m8_b], writes=[nsel_b])
                for bh in range(2):
                    kb.op("dve", lambda e: e.tensor_copy(out=nsel2[:, 0:64], in_=nsel[:, 64 * bh:64 * bh + 64]),
                          reads=[nsel_b], writes=[nsel2_b])
                    kb.op("dve", lambda e: e.tensor_copy(out=nsel2[:, 64:128], in_=nsel[:, 64 * bh:64 * bh + 64]),
                          reads=[nsel_b], writes=[nsel2_b])
                    kb.op("pe", lambda e: e.transpose(out=kb.bankh[:, 128:256], in_=nsel2[:],
                                                      identity=tab("identb", 0, 128)),
                          reads=[nsel2_b, tabh_b], writes=[kb.bankh_b])
                    kb.op("act", lambda e: e.activation(
                        out=negselT[:, bh, 512 * i + 128 * a:512 * i + 128 * (a + 1)],
                        in_=kb.bankh[:, 128:256], func=AF.Copy),
                        reads=[kb.bankh_b], writes=[negselT_b])
        kb.pop_scope()
        if "n4" in DBG:
            kb.op("act", lambda e: e.activation(out=cat[:, :, 0:256], in_=ocomb[:].rearrange("p j h d -> p j (h d)"),
                                                func=AF.Copy), reads=[ocomb_b], writes=[cat_b])
            return

        ks = load_k(0)
        vs = load_v(0)
        kt, ktb = ks
        vt, vtb = vs
        for h in range(4):
            rows = slice(64 * (h % 2), 64 * (h % 2) + 64)
            units = []
            n = 0
            for i in range(NG):
                tot = 4 * (4 * i + 4)
                k = 0
                for jp in range(4 * i + 4):
                    for rp in range(4):
                        u = U()
                        u.n = n
                        n += 1
                        u.i, u.jp, u.rp = i, jp, rp
                        u.masked = jp >= 4 * i
                        u.a0 = jp - 4 * i if u.masked else 0
                        u.c0 = 128 * u.a0
                        u.first = (k == 0)
                        u.last = (k == tot - 1)
                        k += 1
                        units.append(u)

            def s1(u):
                sbk, sbb = bank[u.n % 2], bank_b[u.n % 2]
                q0, q1 = 512 * u.i + u.c0, 512 * u.i + 512
                mk = 4 * u.jp + u.rp
                q4, uu = divmod(mk, 32)
                kb.op("pe", lambda e: e.matmul(sbk[:, u.c0:512], lhsT=kt[rows, u.rp, u.jp * 128:(u.jp + 1) * 128],
                                               rhs=QTs[rows, h // 2, q0:q1], start=True, stop=False),
                      reads=[ktb, QT_b], writes=[sbb])
                kb.op("pe", lambda e: e.matmul(sbk[:, u.c0:512],
                                               lhsT=tab("e32", uu * 128, 128)[rows, :],
                                               rhs=negselT[rows, q4, q0:q1], start=False, stop=True),
                      reads=[tabh_b, negselT_b], writes=[sbb])

            def s2(u):
                sbk, sbb = bank[u.n % 2], bank_b[u.n % 2]
                pt, ptb = p_sb[u.n % 3]
                col = 64 * h + 4 * (u.jp - 4 * u.i) + u.rp + 48
                kb.op("act", lambda e: e.activation(out=pt[:, u.c0:512], in_=sbk[:, u.c0:512], func=AF.Exp,
                                                    scale=0.125, bias=tabfp("bias_sel", col, 1)),
                      reads=[sbb, tabf_b], writes=[ptb])
                if u.masked:
                    kb.op("pool", lambda e: e.tensor_tensor(out=pt[:, u.c0:u.c0 + 128], in0=pt[:, u.c0:u.c0 + 128],
                                                            in1=tab("mcausal", u.rp * 128, 128), op=ALU.mult),
                          reads=[ptb, tabh_b], writes=[ptb])

            def s3(u):
                ab, abb = bank[2 + u.i % 2], bank_b[2 + u.i % 2]
                pt, ptb = p_sb[u.n % 3]
                for a in range(u.a0, 4):
                    kb.op("pe", lambda e: e.matmul(ab[:, a * HD1:(a + 1) * HD1], lhsT=pt[:, a * 128:(a + 1) * 128],
                                                   rhs=vt[:, u.rp * NT + u.jp, :],
                                                   start=(u.first and a == u.a0), stop=(u.last and a == 3)),
                          reads=[ptb, vtb], writes=[abb])
                if u.last:
                    evac_nsa(ab, abb, u.i, h, 1, False)

            run_pipeline(units, [s1, s2, s3])

        if "n5" in DBG:
            kb.op("act", lambda e: e.activation(out=cat[:, :, 0:256], in_=ocomb[:].rearrange("p j h d -> p j (h d)"),
                                                func=AF.Copy), reads=[ocomb_b], writes=[cat_b])
            return
        ks = load_k(1)
        vs = load_v(1)
        wcombos = [(dj, rp) for dj in range(2) for rp in range(4)]
        for h in range(4):
            emit_banded(ks[0], ks[1], vs[0], vs[1], slice(64 * (h % 2), 64 * (h % 2) + 64), h // 2, "twin", 8 * h,
                        wcombos, (0, 1), (2, 3),
                        lambda ab, abb, i, h=h: evac_nsa(ab, abb, i, h, 2, False))
        kb.op("act", lambda e: e.activation(out=cat[:, :, 0:256], in_=ocomb[:].rearrange("p j h d -> p j (h d)"),
                                            func=AF.Copy), reads=[ocomb_b], writes=[cat_b])


    if "nonsa" not in DBG:
        emit_nsa()
    if "nodil" not in DBG:
        emit_dil()

    kb.push_scope()
    e_sb = [kb.sbt("M_e%d" % i, [128, 512], F32) for i in range(6)]
    x_sb = [kb.sbt("M_x%d" % i, [128, 512], F32) for i in range(2)]
    sp_sb = [kb.sbt("M_sp%d" % i, [128, 512], BF16) for i in range(3)]
    aT_sb = [kb.sbt("M_aT%d" % i, [128, 512], BF16) for i in range(3)]
    spacc, spacc_b = kb.sbt("M_spacc", [128, 512], BF16)

    if "nosb" not in DBG:
        for hp in range(3):
            ks = load_k(6 + hp)
            for hh in range(2):
                if "sb1" in DBG and (hp, hh) != (0, 0):
                    continue
                vs = load_v(8 + 2 * hp + hh)
                emit_sb_head(2 * hp + hh, ks, vs)
    kb.pop_scope()

    kb.pop_scope()


def declare_mixer_io(kb, io):
    io["QT"] = kb.dram_bf16("QT", [NQB, 128, TOK], "ExternalInput")
    io["KTa"] = kb.dram_bf16("KTa", [RANKS, NKB, 128, TOK], "ExternalInput")
    io["Va"] = kb.dram_bf16("Va", [RANKS, TOK, NVH * HD], "ExternalInput")
    io["G"] = kb.dram("G", [128, NT * 12], F32, "ExternalInput")
    io["tabh"] = kb.dram_bf16("tabh", [128, TABH_COLS], "ExternalInput")
    io["tabf"] = kb.dram("tabf", [128, TABF_COLS], F32, "ExternalInput")
    io["w1p"] = kb.dram("w1p", [128, 32 * 128], F32, "ExternalInput")
    io["w2p"] = kb.dram("w2p", [128, 128], F32, "ExternalInput")
    io["peT"] = kb.dram("peT", [128, 32], F32, "ExternalInput")
    io["gkc"] = kb.dram("gkc", [128, HD], F32, "ExternalInput")
    for n in ("QT", "KTa", "Va", "G"):
        io[n + "_b"] = kb.buf(n)
    io["KTa_fn"] = lambda rp, kbi: io["KTa"][rp, kbi]
    io["KTa_bf"] = lambda kbi: io["KTa_b"]
    io["Va_fn"] = lambda rp, q, vh: io["Va"][rp, q * 512:(q + 1) * 512, vh * HD:(vh + 1) * HD]
    io["Va_bf"] = lambda q: io["Va_b"]


def build_B_debug():
    kb = KB()
    io = {}
    declare_mixer_io(kb, io)
    cat_out = kb.dram_bf16("cat_out", [128, NT * D_CAT], "ExternalOutput")
    cat_out_b = kb.buf("cat_out")
    cat, cat_b = kb.sbt("cat", [128, NT, D_CAT], BF16)
    kb.op("pool", lambda e: e.memset(cat[:], 0.0), writes=[cat_b])
    emit_mixers(kb, io, cat, cat_b)
    kb.dma("sp", cat_out, cat[:].rearrange("p j d -> p (j d)"), reads=[cat_b], writes=[cat_out_b])
    kb.finish([cat_out_b])
    kb.close()
    return kb


def emit_phase_C(kb, io, cat, cat_b, x_src, x_dst):
    bank, bank_b = kb.bank, kb.bank_b
    kb.push_scope()
    xres, xres_b = kb.sbt("C_xres", [128, NT, D_MODEL], F32)
    xres_bj = [kb.buf("C_xres%d" % j) for j in range(NT)]
    h2T = kb.sb("C_h2T", [128, DC, TOK], BF16)
    h2T_b = [kb.buf("C_h2T%d" % g) for g in range(NG)]
    gm, cst_b = kb.sbt("C_gm", [128, DC], F32)
    ident, ident_b = kb.sbt("C_ident", [128, 128], F32)
    identb, identb_b = kb.sbt("C_identb", [128, 128], BF16)
    kb.dma("sp", gm[:], io["gmlp"], writes=[cst_b])
    kb.dma("sp", ident[:], io["ident"], writes=[ident_b])
    kb.op("dve", lambda e: e.tensor_copy(out=identb[:], in_=ident[:]), reads=[ident_b], writes=[identb_b])

    kb.push_scope()
    catT, catT_b = kb.sbt("C_catT", [128, 6, 128], BF16), None
    catT = [kb.sbt("C_catT%d" % i, [128, 6, 128], BF16) for i in range(2)]
    Wo, Wo_b = kb.sbt("C_Wo", [128, 6, D_MODEL], BF16)
    xt = [kb.sbt("C_xt%d" % i, [128, D_MODEL], F32) for i in range(2)]
    xn = [kb.sbt("C_xn%d" % i, [128, D_MODEL], F32) for i in range(2)]
    junk, junk_b = kb.sbt("C_junk", [128, D_MODEL], F32)
    st = [kb.sbt("C_st%d" % i, [128, 4], F32) for i in range(2)]
    for c in range(6):
        kb.dma("pool", Wo[:, c, :], io["wout"][c * 128:(c + 1) * 128, :], writes=[Wo_b])
    for j in range(NT):
        s = j % 2
        g = j // 4
        src_ap, src_b = x_src(j)
        kb.dma("sp", xt[s][0][:], src_ap, reads=[src_b] if src_b else [], writes=[xt[s][1]])
        for c in range(6):
            kb.op("pe", lambda e: e.transpose(out=kb.bankh[:, c * 128:(c + 1) * 128],
                                              in_=cat[:, j, c * 128:(c + 1) * 128], identity=identb[:]),
                  reads=[cat_b, identb_b], writes=[kb.bankh_b])
        ct, ctb = catT[s]
        kb.op("act", lambda e: e.activation(out=ct[:], in_=kb.bankh[:, 0:768].rearrange("p (c t) -> p c t", t=128),
                                            func=AF.Copy), reads=[kb.bankh_b], writes=[ctb])
        for hf in range(2):
            p, pb = bank[hf], bank_b[hf]
            for c in range(6):
                kb.op("pe", lambda e: e.matmul(p[:, :], lhsT=ct[:, c, :], rhs=Wo[:, c, hf * 512:(hf + 1) * 512],
                                               start=(c == 0), stop=(c == 5)),
                      reads=[ctb, Wo_b], writes=[pb])
            kb.op("dve", lambda e: e.tensor_tensor(out=xres[:, j, hf * 512:(hf + 1) * 512], in0=p[:, :],
                                                   in1=xt[s][0][:, hf * 512:(hf + 1) * 512], op=ALU.add),
                  reads=[pb, xt[s][1]], writes=[xres_bj[j]])
        kb.op("act", lambda e: e.activation(out=junk[:], in_=xres[:, j, :], func=AF.Square,
                                            accum_out=st[s][0][:, 0:1]),
              reads=[xres_bj[j]], writes=[junk_b, st[s][1]])
        kb.op("dve", lambda e: e.tensor_scalar(out=st[s][0][:, 1:2], in0=st[s][0][:, 0:1], scalar1=1.0 / D_MODEL,
                                               scalar2=RMS_EPS, op0=ALU.mult, op1=ALU.add),
              reads=[st[s][1]], writes=[st[s][1]])
        kb.op("act", lambda e: e.activation(out=st[s][0][:, 2:3], in_=st[s][0][:, 1:2], func=AF.Sqrt),
              reads=[st[s][1]], writes=[st[s][1]])
        kb.op("dve", lambda e: e.reciprocal(out=st[s][0][:, 3:4], in_=st[s][0][:, 2:3]),
              reads=[st[s][1]], writes=[st[s][1]])
        kb.op("dve", lambda e: e.tensor_scalar(out=xn[s][0][:], in0=xres[:, j, :], scalar1=st[s][0][:, 3:4],
                                               scalar2=None, op0=ALU.mult),
              reads=[xres_bj[j], st[s][1]], writes=[xn[s][1]])
        for hf in range(2):
            p, pb = bank[2 + hf], bank_b[2 + hf]
            for cc in range(4):
                c = hf * 4 + cc
                kb.op("pe", lambda e: e.transpose(out=p[:, cc * 128:(cc + 1) * 128],
                                                  in_=xn[s][0][:, c * 128:(c + 1) * 128], identity=ident[:]),
                      reads=[xn[s][1], ident_b], writes=[pb])
            for cc in range(4):
                c = hf * 4 + cc
                if cc % 2 == 0:
                    kb.op("act", lambda e: e.activation(out=h2T[:, c, j * 128:(j + 1) * 128],
                                                        in_=p[:, cc * 128:(cc + 1) * 128], func=AF.Copy,
                                                        scale=gm[:, c:c + 1]),
                          reads=[pb, cst_b], writes=[h2T_b[g]])
                else:
                    kb.op("dve", lambda e: e.tensor_scalar(out=h2T[:, c, j * 128:(j + 1) * 128],
                                                           in0=p[:, cc * 128:(cc + 1) * 128],
                                                           scalar1=gm[:, c:c + 1], scalar2=None, op0=ALU.mult),
                          reads=[pb, cst_b], writes=[h2T_b[g]])
    kb.pop_scope()

    kb.push_scope()
    NE = 8
    FE = D_FF // NE
    Wu = [kb.sbt("C_Wu%d" % i, [128, DC, FE], BF16) for i in range(2)]
    Wd = [kb.sbt("C_Wd%d" % i, [128, FE // 128, D_MODEL], BF16) for i in range(2)]
    actT = [kb.sbt("C_act%d" % i, [128, FE // 128, 512], BF16) for i in range(2)]
    rl = [kb.sbt("C_rl%d" % i, [128, 512], F32) for i in range(2)]
    n = 0
    nctr = [0]

    def c2_loads(e8):
        ws = e8 % 2
        for c in range(DC):
            kb.dma("pool", Wu[ws][0][:, c, :], io["wup"][c * 128:(c + 1) * 128, e8 * FE:(e8 + 1) * FE],
                   writes=[Wu[ws][1]])
        for fc in range(FE // 128):
            r0 = e8 * FE + fc * 128
            kb.dma("pool", Wd[ws][0][:, fc, :], io["wdown"][r0:r0 + 128, :], writes=[Wd[ws][1]])

    def c2_up(e8, g):
        ws = e8 % 2
        at, atb = actT[g % 2]
        for fc in range(FE // 128):
            n = nctr[0]
            p, pb = bank[n % 2], bank_b[n % 2]
            rt, rtb = rl[n % 2]
            nctr[0] += 1
            for c in range(DC):
                kb.op("pe", lambda e: e.matmul(p[:, :], lhsT=Wu[ws][0][:, c, fc * 128:(fc + 1) * 128],
                                               rhs=h2T[:, c, g * 512:(g + 1) * 512],
                                               start=(c == 0), stop=(c == DC - 1)),
                      reads=[Wu[ws][1], h2T_b[g]], writes=[pb])
            kb.op("act", lambda e: e.activation(out=rt[:], in_=p[:, :], func=AF.Relu), reads=[pb], writes=[rtb])
            kb.op("dve", lambda e: e.tensor_tensor(out=at[:, fc, :], in0=rt[:], in1=rt[:], op=ALU.mult),
                  reads=[rtb], writes=[atb])

    def c2_down(e8, g):
        ws = e8 % 2
        at, atb = actT[g % 2]
        for a in range(4):
            j = 4 * g + a
            for hf in range(2):
                p, pb = bank[2 + (2 * a + hf) % 4], bank_b[2 + (2 * a + hf) % 4]
                for fc in range(FE // 128):
                    kb.op("pe", lambda e: e.matmul(p[:, :], lhsT=at[:, fc, a * 128:(a + 1) * 128],
                                                   rhs=Wd[ws][0][:, fc, hf * 512:(hf + 1) * 512],
                                                   start=(fc == 0), stop=(fc == FE // 128 - 1)),
                          reads=[atb, Wd[ws][1]], writes=[pb])
                kb.op("dve", lambda e: e.tensor_tensor(out=xres[:, j, hf * 512:(hf + 1) * 512], in0=p[:, :],
                                                       in1=xres[:, j, hf * 512:(hf + 1) * 512], op=ALU.add),
                      reads=[pb, xres_bj[j]], writes=[xres_bj[j]])
            if e8 == NE - 1:
                dst_ap, dst_b = x_dst(j)
                kb.dma("sp", dst_ap, xres[:, j, :], reads=[xres_bj[j]], writes=[dst_b])

    steps = [(e8, g) for e8 in range(NE) for g in range(NG)]
    prev = None
    for (e8, g) in steps:
        if g == 0:
            c2_loads(e8)
        c2_up(e8, g)
        if prev is not None:
            c2_down(*prev)
        prev = (e8, g)
    c2_down(*prev)
    kb.pop_scope()
    kb.pop_scope()


def build_B():
    kb = KB()
    io = {}
    declare_mixer_io(kb, io)
    x = kb.dram("x_own", [TOK, D_MODEL], F32, "ExternalInput")
    xo = kb.dram("x_out", [TOK, D_MODEL], F32, "ExternalOutput")
    xo_b = kb.buf("x_out")
    io["wout"] = kb.dram("wout", [D_CAT, D_MODEL], F32, "ExternalInput")
    io["wup"] = kb.dram("wup", [D_MODEL, D_FF], F32, "ExternalInput")
    io["wdown"] = kb.dram("wdown", [D_FF, D_MODEL], F32, "ExternalInput")
    io["gmlp"] = kb.dram("gmlp", [128, DC], F32, "ExternalInput")
    io["ident"] = kb.dram("ident", [128, 128], F32, "ExternalInput")
    cat, cat_b = kb.sbt("cat", [128, NT, D_CAT], BF16)
    emit_mixers(kb, io, cat, cat_b)
    emit_phase_C(kb, io, cat, cat_b, lambda j: (x[j * 128:(j + 1) * 128, :], None),
                 lambda j: (xo[j * 128:(j + 1) * 128, :], xo_b))
    kb.finish([xo_b])
    kb.close()
    return kb


PIECE = 256
NPIECE = 9
PIECE_ORDER = (1, 0, 5, 6, 7, 8, 2, 3, 4)
LAYER_W = ("winp", "gmix", "gfm", "w1p", "w2p", "peT", "gkc", "wout", "wup", "wdown", "gmlp")


def build_fused():
    kb = KB()
    nc = kb.nc
    x = kb.dram("x_own", [TOK, D_MODEL], F32, "ExternalInput")
    xo = kb.dram("x_out", [TOK, D_MODEL], F32, "ExternalOutput")
    xo_b = kb.buf("x_out")
    shapes = {"winp": [D_MODEL, WP], "gmix": [128, DC], "gfm": [128, NFM], "w1p": [128, 4096], "w2p": [128, 128],
              "peT": [128, 32], "gkc": [128, HD], "wout": [D_CAT, D_MODEL], "wup": [D_MODEL, D_FF],
              "wdown": [D_FF, D_MODEL], "gmlp": [128, DC]}
    ext = {n: kb.dram(n, [DEPTH] + shapes[n], F32, "ExternalInput") for n in LAYER_W}
    ident = kb.dram("ident", [128, 128], F32, "ExternalInput")
    tabh = kb.dram_bf16("tabh", [128, TABH_COLS], "ExternalInput")
    tabf = kb.dram("tabf", [128, TABF_COLS], F32, "ExternalInput")
    QT = kb.dram("QT_i", [NQB, 128, TOK], BF16, "Internal")
    G = kb.dram("G_i", [128, NT * 12], F32, "Internal")
    xmid = kb.dram("xmid_i", [TOK, D_MODEL], F32, "Internal")
    xmid_b = kb.buf("xmid")
    gbo = [[kb.dram("gbo%d_%d" % (l, i), [PIECE, 2048], BF16, "Internal") for i in range(NPIECE)]
           for l in range(DEPTH)]
    gba = [[kb.dram("gba%d_%d" % (l, i), [RANKS * PIECE, 2048], BF16, "Internal") for i in range(NPIECE)]
           for l in range(DEPTH)]
    cat, cat_b = kb.sbt("cat", [128, NT, D_CAT], BF16)
    QT_b, G_b = kb.buf("QT"), kb.buf("G")
    for l in range(DEPTH):
        io = {n: ext[n][l] for n in LAYER_W}
        io["ident"], io["tabh"], io["tabf"] = ident, tabh, tabf
        io["QT"], io["QT_b"], io["G"], io["G_b"] = QT, QT_b, G, G_b
        gbo_b = [kb.buf("gbo%d_%d" % (l, i)) for i in range(NPIECE)]
        gba_b = [kb.buf("gba%d_%d" % (l, i)) for i in range(NPIECE)]
        go, ga = gbo[l], gba[l]
        io["KT_fn"] = lambda ki, go=go: go[ki // 2][(ki % 2) * 128:(ki % 2) * 128 + 128, :]
        io["KT_bf"] = lambda ki, gbo_b=gbo_b: gbo_b[ki // 2]
        io["V_fn"] = lambda jj, go=go: go[5 + jj // 4][:, :].rearrange("a (u d) -> (a u) d", d=1024)[
            (jj % 4) * 128:(jj % 4) * 128 + 128, :]
        io["V_cols"] = 1024
        io["zero_fill"] = [(go[4][128:256, :], gbo_b[4])]
        io["V_bf"] = lambda jj, gbo_b=gbo_b: gbo_b[5 + jj // 4]
        io["KTa_fn"] = lambda rp, kbi, ga=ga: ga[kbi // 2][rp * PIECE + (kbi % 2) * 128:
                                                         rp * PIECE + (kbi % 2) * 128 + 128, :]
        io["KTa_bf"] = lambda kbi, gba_b=gba_b: gba_b[kbi // 2]
        io["Va_fn"] = lambda rp, q, vh, ga=ga: ga[5 + q][rp * PIECE:(rp + 1) * PIECE, :] \
            .rearrange("a (u d) -> (a u) d", d=1024)[:, vh * HD:(vh + 1) * HD]
        io["Va_bf"] = lambda q, gba_b=gba_b: gba_b[5 + q]
        if l == 0:
            x_src = lambda j: (x[j * 128:(j + 1) * 128, :], None)
        else:
            x_src = lambda j: (xmid[j * 128:(j + 1) * 128, :], xmid_b)
        if l == DEPTH - 1:
            x_dst = lambda j: (xo[j * 128:(j + 1) * 128, :], xo_b)
        else:
            x_dst = lambda j: (xmid[j * 128:(j + 1) * 128, :], xmid_b)
        emit_phase_A(kb, x_src, None, io)
        for i in PIECE_ORDER:
            if "nocc" in DBG:
                for rr in range(RANKS):
                    kb.dma("sp", ga[i][rr * PIECE:(rr + 1) * PIECE, :], go[i][:, :], reads=[gbo_b[i]],
                           writes=[gba_b[i]])
                continue
            kb.dma("pool", None, None, reads=[gbo_b[i]], writes=[gba_b[i]],
                   fn=lambda e, i=i: e.collective_compute(
                       "AllGather", ALU.bypass, replica_groups=[[0, 1, 2, 3], [4, 5, 6, 7]],
                       ins=[go[i][:, :]], outs=[ga[i][:, :]]))
        emit_mixers(kb, io, cat, cat_b)
        emit_phase_C(kb, io, cat, cat_b, x_src, x_dst)
    kb.finish([xo_b])
    kb.close()
    return kb


_PROGS = {}
LAST_EXEC_NS = None


def _prog(name):
    if name not in _PROGS:
        _PROGS[name] = {"A": build_A, "B": build_B, "F": build_fused}[name]()
    return _PROGS[name]


def kernel_unfused(x, norm_mix, norm_mlp, w_in, qk_gain_nsa, qk_gain_dil, cmp_pe, cmp_w1, cmp_w2, w_out, w_up, w_down):
    f32 = lambda a: np.ascontiguousarray(np.asarray(a, dtype=np.float32))
    x = f32(x)
    perm = win_column_perm()
    ident = np.eye(128, dtype=np.float32)
    rows = [core_rows(c) for c in range(NCORE)]
    tabs = [host_table_arrays(r) for r in range(RANKS)]
    xc = [np.ascontiguousarray(x[b, idx]) for (b, r, idx) in rows]
    for l in range(DEPTH):
        gmix, gfm = host_consts_A(f32(norm_mix[l]), f32(qk_gain_nsa[l]), f32(qk_gain_dil[l]))
        winp = np.ascontiguousarray(f32(w_in[l])[:, perm])
        resA = run_bass_kernel_spmd(_prog("A").nc, [
            {"x_own": xc[c], "winp": winp, "gmix": gmix, "gfm": gfm, "ident": ident} for c in range(NCORE)],
            core_ids=list(range(NCORE))).results
        w1p = np.ascontiguousarray(f32(cmp_w1[l]).reshape(2, 32, 64, 128).transpose(0, 2, 1, 3).reshape(128, 4096))
        w2p = np.ascontiguousarray(f32(cmp_w2[l]).transpose(1, 0, 2).reshape(128, 128))
        peT = np.ascontiguousarray(f32(cmp_pe[l]).transpose(0, 2, 1).reshape(128, 32))
        gkc = np.ascontiguousarray(np.broadcast_to(f32(qk_gain_nsa[l])[1], (128, HD)))
        gmlp = np.ascontiguousarray(f32(norm_mlp[l]).reshape(DC, 128).T)
        in_maps = []
        for c in range(NCORE):
            b, r, idx = rows[c]
            KTa = np.stack([resA[4 * b + rr]["KT"] for rr in range(RANKS)])
            Va = np.stack([resA[4 * b + rr]["V"] for rr in range(RANKS)])
            in_maps.append({"QT": resA[c]["QT"], "KTa": KTa, "Va": Va, "G": resA[c]["G"],
                            "tabh": tabs[r][0], "tabf": tabs[r][1], "w1p": w1p, "w2p": w2p, "peT": peT,
                            "gkc": gkc, "x_own": xc[c], "wout": f32(w_out[l]), "wup": f32(w_up[l]),
                            "wdown": f32(w_down[l]), "gmlp": gmlp, "ident": ident})
        resB = run_bass_kernel_spmd(_prog("B").nc, in_maps, core_ids=list(range(NCORE))).results
        xc = [resB[c]["x_out"] for c in range(NCORE)]
    out = np.empty((BATCH, SEQ, D_MODEL), np.float32)
    for c in range(NCORE):
        b, r, idx = rows[c]
        out[b, idx] = xc[c]
    return out


def kernel(x, norm_mix, norm_mlp, w_in, qk_gain_nsa, qk_gain_dil, cmp_pe, cmp_w1, cmp_w2, w_out, w_up, w_down):
    f32 = lambda a: np.ascontiguousarray(np.asarray(a, dtype=np.float32))
    x = f32(x)
    perm = win_column_perm()
    rows = [core_rows(c) for c in range(NCORE)]
    tabs = [host_table_arrays(r) for r in range(RANKS)]
    L = {n: [] for n in LAYER_W}
    for l in range(DEPTH):
        gmix, gfm = host_consts_A(f32(norm_mix[l]), f32(qk_gain_nsa[l]), f32(qk_gain_dil[l]))
        L["gmix"].append(gmix)
        L["gfm"].append(gfm)
        L["winp"].append(f32(w_in[l])[:, perm])
        L["w1p"].append(f32(cmp_w1[l]).reshape(2, 32, 64, 128).transpose(0, 2, 1, 3).reshape(128, 4096))
        L["w2p"].append(f32(cmp_w2[l]).transpose(1, 0, 2).reshape(128, 128))
        L["peT"].append(f32(cmp_pe[l]).transpose(0, 2, 1).reshape(128, 32))
        L["gkc"].append(np.broadcast_to(f32(qk_gain_nsa[l])[1], (128, HD)))
        L["gmlp"].append(f32(norm_mlp[l]).reshape(DC, 128).T)
        L["wout"].append(f32(w_out[l]))
        L["wup"].append(f32(w_up[l]))
        L["wdown"].append(f32(w_down[l]))
    shared = {n: np.ascontiguousarray(np.stack(v)).astype(np.float32) for n, v in L.items()}
    shared["ident"] = np.eye(128, dtype=np.float32)
    in_maps = []
    for c in range(NCORE):
        b, r, idx = rows[c]
        m = dict(shared)
        m["x_own"] = np.ascontiguousarray(x[b, idx])
        m["tabh"], m["tabf"] = tabs[r]
        in_maps.append(m)
    _r = run_bass_kernel_spmd(_prog("F").nc, in_maps, core_ids=list(range(NCORE)))
    global LAST_EXEC_NS
    LAST_EXEC_NS = getattr(_r, "exec_time_ns", None)
    res = _r.results
    out = np.empty((BATCH, SEQ, D_MODEL), np.float32)
    for c in range(NCORE):
        b, r, idx = rows[c]
        out[b, idx] = res[c]["x_out"]
    return out
```

```python
from contextlib import ExitStack
import numpy as np
import ml_dtypes
import concourse.bass as bass
import concourse.mybir as mybir
from concourse.bass_utils import run_bass_kernel_spmd

F32 = mybir.dt.float32
BF16 = mybir.dt.bfloat16
AF = mybir.ActivationFunctionType
ALU = mybir.AluOpType
NPBF = ml_dtypes.bfloat16

SAME_ENGINE_SYNC = True
DBG = set()


class Buf:
    __slots__ = ("name", "w", "r", "semkey", "semv")

    def __init__(self, name):
        self.name = name
        self.w = None
        self.r = {}
        self.semkey = None
        self.semv = 0


class KB:
    def __init__(self):
        self.nc = bass.Bass("TRN2", target_bir_lowering=False)
        nc = self.nc
        self.es = ExitStack()
        self.engs = {"pe": nc.tensor, "act": nc.scalar, "dve": nc.vector,
                     "pool": nc.gpsimd, "sp": nc.sync}
        self.semobj = {}
        self.cnt = {}
        self.seen = {}
        for n in self.engs:
            self.semobj[n] = self.es.enter_context(nc.semaphore("s_" + n))
            self.cnt[n] = 0
            self.seen[n] = {}
        self.nbuf = 0
        self.ninstr = 0
        self.nwait = 0
        self.root_es = self.es
        self.dmasems = {}
        self.bank = [self.es.enter_context(nc.psum_tensor("bank%d" % i, [128, 512], F32)) for i in range(7)]
        self.bank_b = [Buf("bank%d" % i) for i in range(7)]
        self.bankh = self.es.enter_context(nc.psum_tensor("bankh", [128, 1024], BF16))
        self.bankh_b = Buf("bankh")

    def push_scope(self):
        self._outer = getattr(self, "_outer", [])
        self._outer.append(self.es)
        self.es = ExitStack()

    def pop_scope(self):
        self.barrier()
        self.es.close()
        self.es = self._outer.pop()

    def barrier(self):
        deps = {n: self.cnt[n] for n in self.engs if self.cnt[n] > 0}
        deps.update(self.dmasems)
        for n in self.engs:
            self._wait(n, dict(deps))

    def sbt(self, name, shape, dt):
        return self.sb(name, shape, dt), self.buf(name)

    def sb(self, name, shape, dt):
        self.nsb = getattr(self, "nsb", 0) + 1
        return self.es.enter_context(self.nc.sbuf_tensor("%s_%d" % (name, self.nsb), list(shape), dt))

    def ps(self, name, shape, dt=F32):
        return self.es.enter_context(self.nc.psum_tensor(name, list(shape), dt))

    def dram(self, name, shape, dt, kind):
        return self.nc.dram_tensor(name, list(shape), dt, kind=kind).ap()

    def dram_bf16(self, name, shape, kind):
        shp = list(shape)
        assert shp[-1] % 2 == 0
        shp[-1] //= 2
        return self.nc.dram_tensor(name, shp, F32, kind=kind).ap().bitcast(BF16)

    def buf(self, name=None):
        self.nbuf += 1
        return Buf(name or f"b{self.nbuf}")

    def _deps(self, reads, writes):
        deps = {}
        for b in reads:
            if b.w is not None:
                k, v = b.w
                if deps.get(k, 0) < v:
                    deps[k] = v
        for b in writes:
            if b.w is not None:
                k, v = b.w
                if deps.get(k, 0) < v:
                    deps[k] = v
            for k, v in b.r.items():
                if deps.get(k, 0) < v:
                    deps[k] = v
        return deps

    def _wait(self, eng, deps):
        seen = self.seen[eng]
        e = self.engs[eng]
        for k, v in deps.items():
            if k == eng and (eng in ("pe", "sp") or not SAME_ENGINE_SYNC):
                continue
            if seen.get(k, 0) >= v:
                continue
            e.wait_ge(self.semobj[k], v)
            self.nwait += 1
            seen[k] = v

    def op(self, eng, fn, reads=(), writes=()):
        self._wait(eng, self._deps(reads, writes))
        ins = fn(self.engs[eng])
        self.cnt[eng] += 1
        tok = (eng, self.cnt[eng])
        ins.then_inc(self.semobj[eng], 1)
        self.ninstr += 1
        for b in reads:
            if b.r.get(eng, 0) < tok[1]:
                b.r[eng] = tok[1]
        for b in writes:
            b.w = tok
            b.r = {}
        return ins

    def dma(self, q, out, in_, reads=(), writes=(), fn=None, **kw):
        self._wait(q, self._deps(reads, writes))
        wb = writes[0]
        if wb.semkey is None:
            wb.semkey = "d%d_%s" % (self.nbuf, wb.name)
            self.nbuf += 1
            self.semobj[wb.semkey] = self.root_es.enter_context(self.nc.semaphore(wb.semkey))
        if fn is not None:
            ins = fn(self.engs[q])
            ins.then_inc(self.semobj[wb.semkey])
            wb.semv += 1
        else:
            ins = self.engs[q].dma_start(out=out, in_=in_, **kw)
            ins.then_inc(self.semobj[wb.semkey], 16)
            wb.semv += 16
        tok = (wb.semkey, wb.semv)
        self.dmasems[wb.semkey] = wb.semv
        self.ninstr += 1
        for b in reads:
            if b.r.get(tok[0], 0) < tok[1]:
                b.r[tok[0]] = tok[1]
        for b in writes:
            b.w = tok
            b.r = {}
        return ins

    def finish(self, outs):
        deps = {}
        for b in outs:
            if b.w is not None:
                deps[b.w[0]] = max(deps.get(b.w[0], 0), b.w[1])
        self._wait("sp", deps)

    def close(self):
        self.es.close()


D_MODEL = 1024
BATCH = 2
SEQ = 8192
DEPTH = 2
HD = 64
NCORE = 8
RANKS = 4
NT = 16
TOK = NT * 128
NG = 4
DC = D_MODEL // 128
D_PROJ = 2956
NFM = 17
NTM = 908
WP = NFM * 128 + NTM
D_FF = 4096
D_CAT = 768
RMS_EPS = 1e-6
NORM_BLOCKS = (0, 1, 2, 3, 5, 6, 7, 8, 9, 10)
Q_BLOCKS = (0, 1, 5, 6, 7, 11, 12, 13)
K_BLOCKS = (2, 3, 4, 8, 9, 10, 14, 15, 16)
NQB = len(Q_BLOCKS)
NKB = len(K_BLOCKS)
NVH = 14


def win_column_perm():
    cols = []
    A_KV = 256
    B0 = 652
    C0 = 1804
    cols += list(range(0, 128))
    cols += list(range(128, 256))
    ksel = list(range(A_KV + 128, A_KV + 192))
    kwin = list(range(A_KV + 256, A_KV + 320))
    cols += ksel + ksel
    cols += kwin + kwin
    cols += list(range(A_KV, A_KV + 128))
    for g in range(3):
        cols += list(range(B0 + g * 128, B0 + g * 128 + 128))
    for g in range(3):
        cols += list(range(B0 + (3 + g) * 128, B0 + (3 + g) * 128 + 128))
    for p in range(3):
        cols += list(range(C0 + p * 128, C0 + p * 128 + 128))
    for p in range(3):
        cols += list(range(C0 + 384 + p * 128, C0 + 384 + p * 128 + 128))
    assert len(cols) == NFM * 128
    cols += list(range(A_KV + 192, A_KV + 256))
    cols += list(range(A_KV + 320, A_KV + 384))
    cols += list(range(B0 + 768, B0 + 1152))
    cols += list(range(C0 + 768, C0 + 1152))
    cols += list(range(640, 652))
    assert len(cols) == WP
    return np.array(cols, dtype=np.int64)


def emit_phase_A(kb, x_src, xb_fn, io):
    nc = kb.nc
    kb.push_scope()
    W = kb.sb("A_W", [128, DC, WP], BF16)
    W_b = kb.buf("A_W")
    hT = kb.sb("A_hT", [128, DC, TOK], BF16)
    hT_b = [kb.buf("A_hT%d" % g) for g in range(NG)]
    gmix = kb.sb("A_gmix", [128, DC], F32)
    gfm = kb.sb("A_gfm", [128, NFM], F32)
    ident = kb.sb("A_ident", [128, 128], F32)
    blk = kb.sb("A_blk", [128, 128], BF16)
    cst_b = kb.buf("A_cst")
    blk_b = kb.buf("A_blk")
    xt = [kb.sb("A_xt%d" % i, [128, D_MODEL], F32) for i in range(2)]
    xt_b = [kb.buf("A_xt%d" % i) for i in range(2)]
    junk = kb.sb("A_junk", [128, D_MODEL], F32)
    junk_b = kb.buf("A_junk")
    xn = [kb.sb("A_xn%d" % i, [128, D_MODEL], F32) for i in range(2)]
    xn_b = [kb.buf("A_xn%d" % i) for i in range(2)]
    st = [kb.sb("A_st%d" % i, [128, 4], F32) for i in range(2)]
    st_b = [kb.buf("A_st%d" % i) for i in range(2)]
    NPS = 6
    ps = kb.bank[:NPS]
    ps_b = kb.bank_b[:NPS]
    sq = [kb.sb("A_sq%d" % i, [128, 512], BF16) for i in range(2)]
    sq_b = [kb.buf("A_sq%d" % i) for i in range(2)]
    lnb = [kb.sb("A_ln%d" % i, [128, 512], F32) for i in range(2)]
    lnb_b = [kb.buf("A_ln%d" % i) for i in range(2)]
    fo = [kb.sb("A_fo%d" % i, [128, 512], BF16) for i in range(3)]
    fo_b = [kb.buf("A_fo%d" % i) for i in range(3)]
    vo = [kb.sb("A_vo%d" % i, [128, 1024], BF16) for i in range(2)]
    vo_b = [kb.buf("A_vo%d" % i) for i in range(2)]
    go = kb.sb("A_go", [128, NT * 12], F32)
    go_b = kb.buf("A_go")
    epsb = kb.sb("A_eps", [128, 1], F32)
    eps_ap = epsb[:, 0:1]

    kb.op("pool", lambda e: e.memset(epsb[:], RMS_EPS), writes=[cst_b])
    for i in range(2):
        kb.op("pool", lambda e: e.memset(vo[i][:, 896:1024], 0.0), writes=[vo_b[i]])
    if io.get("zero_fill"):
        zt = kb.sb("A_zero", [128, 2048], BF16)
        zt_b = kb.buf("A_zero")
        kb.op("pool", lambda e: e.memset(zt[:], 0.0), writes=[zt_b])
        for (zap, zb) in io["zero_fill"]:
            kb.dma("sp", zap, zt[:], reads=[zt_b], writes=[zb])
    kb.dma("sp", gmix[:], io["gmix"], writes=[cst_b])
    kb.dma("sp", gfm[:], io["gfm"], writes=[cst_b])
    kb.dma("sp", ident[:], io["ident"], writes=[cst_b])
    kb.op("pool", lambda e: e.memset(blk[:], 0.0), writes=[blk_b])
    kb.op("pool", lambda e: e.memset(blk[0:64, 0:64], 1.0), writes=[blk_b])
    kb.op("pool", lambda e: e.memset(blk[64:128, 64:128], 1.0), writes=[blk_b])
    half = WP // 2
    for c in range(DC):
        for h0 in (0, half):
            kb.dma("pool", W[:, c, h0:h0 + half], io["winp"][c * 128:(c + 1) * 128, h0:h0 + half],
                   writes=[W_b])

    psi = [0]

    def next_ps():
        i = psi[0] % NPS
        psi[0] += 1
        return ps[i], ps_b[i]

    cnt = {"sq": 0, "fo": 0, "vo": 0}

    def tile_work(j):
        s = j % 2
        src_ap, src_b = x_src(j)
        kb.dma("pool", xt[s][:], src_ap, reads=[src_b] if src_b else [], writes=[xt_b[s]])
        kb.op("act", lambda e: e.activation(out=junk[:], in_=xt[s][:], func=AF.Square,
                                            accum_out=st[s][:, 0:1]),
              reads=[xt_b[s]], writes=[junk_b, st_b[s]])
        kb.op("dve", lambda e: e.tensor_scalar(out=st[s][:, 1:2], in0=st[s][:, 0:1],
                                               scalar1=1.0 / D_MODEL, scalar2=RMS_EPS,
                                               op0=ALU.mult, op1=ALU.add),
              reads=[st_b[s]], writes=[st_b[s]])
        kb.op("act", lambda e: e.activation(out=st[s][:, 2:3], in_=st[s][:, 1:2], func=AF.Sqrt),
              reads=[st_b[s]], writes=[st_b[s]])
        kb.op("dve", lambda e: e.reciprocal(out=st[s][:, 3:4], in_=st[s][:, 2:3]),
              reads=[st_b[s]], writes=[st_b[s]])
        kb.op("dve", lambda e: e.tensor_scalar(out=xn[s][:], in0=xt[s][:], scalar1=st[s][:, 3:4],
                                               scalar2=None, op0=ALU.mult),
              reads=[xt_b[s], st_b[s]], writes=[xn_b[s]])
        g = j // 4
        for hf in range(2):
            p, pb = next_ps()
            for cc in range(4):
                c = hf * 4 + cc
                kb.op("pe", lambda e: e.transpose(out=p[:, cc * 128:(cc + 1) * 128],
                                                  in_=xn[s][:, c * 128:(c + 1) * 128],
                                                  identity=ident[:]),
                      reads=[xn_b[s], cst_b], writes=[pb])
            for cc in range(4):
                c = hf * 4 + cc
                eng = "act" if cc % 2 == 0 else "dve"
                if eng == "act":
                    kb.op("act", lambda e: e.activation(out=hT[:, c, j * 128:(j + 1) * 128],
                                                        in_=p[:, cc * 128:(cc + 1) * 128],
                                                        func=AF.Copy, scale=gmix[:, c:c + 1]),
                          reads=[pb, cst_b], writes=[hT_b[g]])
                else:
                    kb.op("dve", lambda e: e.tensor_scalar(out=hT[:, c, j * 128:(j + 1) * 128],
                                                           in0=p[:, cc * 128:(cc + 1) * 128],
                                                           scalar1=gmix[:, c:c + 1], scalar2=None,
                                                           op0=ALU.mult),
                          reads=[pb, cst_b], writes=[hT_b[g]])

    def proj_gen(g):
        t0 = g * 512
        for blk_i in range(NFM):
            if "nofm" in DBG:
                break
            p, pb = next_ps()
            for c in range(DC):
                kb.op("pe", lambda e: e.matmul(p[:, :], lhsT=W[:, c, blk_i * 128:(blk_i + 1) * 128],
                                               rhs=hT[:, c, t0:t0 + 512],
                                               start=(c == 0), stop=(c == DC - 1)),
                      reads=[W_b, hT_b[g]], writes=[pb])
            fi = cnt["fo"] % 3
            cnt["fo"] += 1
            if blk_i in NORM_BLOCKS:
                si = cnt["sq"] % 2
                cnt["sq"] += 1
                kb.op("act", lambda e: e.activation(out=sq[si][:], in_=p[:, :], func=AF.Square),
                      reads=[pb], writes=[sq_b[si]])
                p2, p2b = next_ps()
                kb.op("pe", lambda e: e.matmul(p2[:, :], lhsT=blk[:], rhs=sq[si][:],
                                               start=True, stop=True),
                      reads=[blk_b, sq_b[si]], writes=[p2b])
                kb.op("act", lambda e: e.activation(out=lnb[si][:], in_=p2[:, :], func=AF.Ln,
                                                    scale=1.0 / HD, bias=eps_ap),
                      reads=[p2b, cst_b], writes=[lnb_b[si]])
                kb.op("act", lambda e: e.activation(out=lnb[si][:], in_=lnb[si][:], func=AF.Exp,
                                                    scale=-0.5),
                      reads=[lnb_b[si]], writes=[lnb_b[si]])
                kb.op("dve", lambda e: e.scalar_tensor_tensor(out=fo[fi][:], in0=p[:, :],
                                                              scalar=gfm[:, blk_i:blk_i + 1],
                                                              in1=lnb[si][:], op0=ALU.mult,
                                                              op1=ALU.mult),
                      reads=[pb, cst_b, lnb_b[si]], writes=[fo_b[fi]])
            else:
                kb.op("dve", lambda e: e.tensor_copy(out=fo[fi][:], in_=p[:, :]),
                      reads=[pb], writes=[fo_b[fi]])
            if blk_i in Q_BLOCKS:
                qi = Q_BLOCKS.index(blk_i)
                kb.dma("sp", io["QT"][qi, :, t0:t0 + 512], fo[fi][:], reads=[fo_b[fi]],
                       writes=[io["QT_b"]])
            else:
                ki = K_BLOCKS.index(blk_i)
                kb.dma("sp", io["KT_fn"](ki)[:, t0:t0 + 512], fo[fi][:], reads=[fo_b[fi]],
                       writes=[io["KT_bf"](ki)])
            yield
        for a in range(4):
            if "notm" in DBG:
                break
            jj = g * 4 + a
            vi = cnt["vo"] % 2
            cnt["vo"] += 1
            for half_i in range(2):
                c0 = NFM * 128 + half_i * 454
                p, pb = next_ps()
                for c in range(DC):
                    kb.op("pe", lambda e: e.matmul(p[:, 0:454], lhsT=hT[:, c, jj * 128:(jj + 1) * 128],
                                                   rhs=W[:, c, c0:c0 + 454],
                                                   start=(c == 0), stop=(c == DC - 1)),
                          reads=[W_b, hT_b[g]], writes=[pb])
                if half_i == 0:
                    kb.op("act", lambda e: e.activation(out=vo[vi][:, 0:454], in_=p[:, 0:454],
                                                        func=AF.Copy),
                          reads=[pb], writes=[vo_b[vi]])
                else:
                    kb.op("dve", lambda e: e.tensor_copy(out=vo[vi][:, 454:896], in_=p[:, 0:442]),
                          reads=[pb], writes=[vo_b[vi]])
                    kb.op("act", lambda e: e.activation(out=go[:, jj * 12:(jj + 1) * 12],
                                                        in_=p[:, 442:454], func=AF.Sigmoid),
                          reads=[pb], writes=[go_b])
            if "nov" not in DBG:
                kb.dma("sp", io["V_fn"](jj), vo[vi][:, 0:io.get("V_cols", 896)], reads=[vo_b[vi]],
                       writes=[io["V_bf"](jj)])
            if jj == NT - 1:
                kb.dma("sp", io["G"], go[:], reads=[go_b], writes=[io["G_b"]])
            yield

    for j in range(4):
        tile_work(j)
    for g in range(NG):
        if "noproj" in DBG:
            for a in range(4):
                if g + 1 < NG:
                    tile_work(4 * (g + 1) + a)
            continue
        nxt = [4 * (g + 1) + a for a in range(4)] if g + 1 < NG else []
        for k, _ in enumerate(proj_gen(g)):
            if k in (2, 7, 12, 17) and nxt:
                tile_work(nxt.pop(0))
        for j in nxt:
            tile_work(j)
    kb.pop_scope()


def core_rows(c):
    b, r = divmod(c, RANKS)
    idx = np.concatenate([np.arange((4 * j + r) * 128, (4 * j + r + 1) * 128) for j in range(NT)])
    return b, r, idx


def host_consts_A(norm_mix_l, qk_gain_nsa_l, qk_gain_dil_l):
    gmix = np.ascontiguousarray(norm_mix_l.reshape(DC, 128).T)
    gfm = np.ones((128, NFM), np.float32)
    two = lambda v: np.concatenate([v, v])
    gfm[:, 0] = two(qk_gain_nsa_l[0])
    gfm[:, 1] = two(qk_gain_nsa_l[0])
    gfm[:, 2] = two(qk_gain_nsa_l[2])
    gfm[:, 3] = two(qk_gain_nsa_l[3])
    for g in range(3):
        gfm[:, 5 + g] = two(qk_gain_dil_l[0])
        gfm[:, 8 + g] = two(qk_gain_dil_l[1])
    return gmix, gfm


def build_A():
    kb = KB()
    io = {}
    x = kb.dram("x_own", [TOK, D_MODEL], F32, "ExternalInput")
    io["winp"] = kb.dram("winp", [D_MODEL, WP], F32, "ExternalInput")
    io["gmix"] = kb.dram("gmix", [128, DC], F32, "ExternalInput")
    io["gfm"] = kb.dram("gfm", [128, NFM], F32, "ExternalInput")
    io["ident"] = kb.dram("ident", [128, 128], F32, "ExternalInput")
    io["QT"] = kb.dram_bf16("QT", [NQB, 128, TOK], "ExternalOutput")
    io["KT"] = kb.dram_bf16("KT", [NKB, 128, TOK], "ExternalOutput")
    io["V"] = kb.dram_bf16("V", [TOK, NVH * 64], "ExternalOutput")
    io["G"] = kb.dram("G", [128, NT * 12], F32, "ExternalOutput")
    for n in ("QT", "KT", "V", "G"):
        io[n + "_b"] = kb.buf(n)
    io["KT_bf"] = lambda ki: io["KT_b"]
    io["V_bf"] = lambda jj: io["V_b"]
    io["KT_fn"] = lambda ki: io["KT"][ki]
    io["V_fn"] = lambda jj: io["V"][jj * 128:(jj + 1) * 128, :]
    emit_phase_A(kb, lambda j: (x[j * 128:(j + 1) * 128, :], None), None, io)
    kb.finish([io[n + "_b"] for n in ("QT", "KT", "V", "G")])
    kb.close()
    return kb


DIL_PAIRS = ((128, 1), (512, 4), (2048, 16))
WIN_NSA = 512


def alibi_slopes_np():
    i = np.arange(1, 11, dtype=np.float64)
    return np.exp2(-8.0 * i / 10.0)


def dil_combos(g):
    w = DIL_PAIRS[g][0] // 128
    out = []
    for dj in range(0, (w + 3) // 4 + 1):
        for rp in range(4):
            if any(0 <= 4 * dj + r - rp <= w for r in range(4)):
                out.append((dj, rp))
    return out


DIL_COMBOS = [dil_combos(g) for g in range(3)]
N_DIL_TAB = sum(len(c) for c in DIL_COMBOS) * 2


def bf16_pack(a):
    a = np.ascontiguousarray(np.asarray(a, dtype=np.float32).astype(NPBF))
    return a.view(np.float32)


def host_tables(r):
    sl = alibi_slopes_np()
    sl_dil, sl_nsa = sl[:6], sl[6:]
    ki = np.arange(128)[:, None].astype(np.float64)
    qi = np.arange(128)[None, :].astype(np.float64)
    T = {}
    ms = np.zeros((128, 4, 128))
    mc = np.zeros((128, 4, 128))
    for rp in range(4):
        if rp < r:
            ms[:, rp] = 1
            mc[:, rp] = 1
        elif rp == r:
            ms[:, rp] = (ki < qi)
            mc[:, rp] = (ki <= qi)
    T["mstrict"] = bf16_pack(ms.reshape(128, 512))
    T["mcausal"] = bf16_pack(mc.reshape(128, 512))
    tw = np.zeros((128, 4, 2, 4, 128))
    for h in range(4):
        for dj in range(2):
            for rp in range(4):
                d = 128 * (4 * dj + r - rp) + qi - ki
                tw[:, h, dj, rp] = ((d >= 0) & (d < WIN_NSA)) * np.exp(-sl_nsa[h] * np.maximum(d, 0))
    T["twin"] = bf16_pack(tw.reshape(128, -1))
    td = np.zeros((128, N_DIL_TAB, 128))
    idx = 0
    for g, (w, rd) in enumerate(DIL_PAIRS):
        for hh in range(2):
            for (dj, rp) in DIL_COMBOS[g]:
                d = 128 * (4 * dj + r - rp) + qi - ki
                ok = (d >= 0) & (d <= w) & (np.mod(d, rd) == 0)
                td[:, idx] = ok * np.exp(-sl_dil[2 * g + hh] * np.maximum(d, 0))
                idx += 1
    T["tdil"] = bf16_pack(td.reshape(128, -1))
    cm = np.zeros((128, 2, 4, 128))
    ni = ki
    for dl in range(2):
        for a in range(4):
            cm[:, dl, a] = (2048 * dl + 128 * (4 * a + r) + qi - 16 * ni - 31 >= 0)
    T["cmask"] = bf16_pack(cm.reshape(128, -1))
    bs = np.zeros((128, 4, 4, 128), np.float32)
    qcol = np.arange(128)[:, None]
    jj = np.arange(128)[None, :]
    for i in range(4):
        for a in range(4):
            cur = 32 * i + 8 * a + 2 * r + (qcol >= 64)
            b = np.where((jj == cur) | (jj == cur - 1), 1e4, 0.0) + np.where(jj == 0, 1e4, 0.0)
            b = np.where(jj > cur, -1e30, b)
            bs[:, i, a] = b
    T["bsel"] = bs.reshape(128, -1)
    u = np.arange(64)[None, :]
    bsl = np.zeros((128, 4, 64), np.float32)
    bcm = np.zeros((128, 4, 4), np.float32)
    for h in range(4):
        bsl[:, h] = sl_nsa[h] * (128 * (u - 48) + ki)
        for dl in range(4):
            bcm[:, h, dl] = (sl_nsa[h] * (-2048 * dl + 16 * ni + 31))[:, 0]
    T["bias_sel"] = bsl.reshape(128, -1)
    T["bias_cmp"] = bcm.reshape(128, -1)
    ov = np.zeros((128, 4, 128))
    for nt in range(4):
        n = 128 * nt + np.arange(128)[:, None]
        ov[:, nt] = (16 * n <= 64 * jj + 63) & (16 * n + 31 >= 64 * jj)
    T["ov"] = bf16_pack(ov.reshape(128, -1))
    e32 = np.zeros((128, 32, 128))
    kk = np.arange(128)[None, :]
    jj32 = (np.arange(128) % 64)[:, None]
    for uu in range(32):
        e32[:, uu] = 32768.0 * (jj32 == 2 * uu + kk // 64)
    T["e32"] = bf16_pack(e32.reshape(128, -1))
    tri = (np.arange(128)[:, None] >= np.arange(128)[None, :]).astype(np.float64)
    T["tri"] = bf16_pack(tri)
    T["identb"] = bf16_pack(np.eye(128))
    return T


TABLE_SHAPES = {
    "mstrict": (512, True), "mcausal": (512, True), "twin": (4 * 2 * 4 * 128, True),
    "tdil": (N_DIL_TAB * 128, True), "cmask": (2 * 512, True), "bsel": (4 * 4 * 128, False),
    "bias_sel": (256, False), "bias_cmp": (16, False), "ov": (512, True), "e32": (32 * 128, True),
    "tri": (128, True), "identb": (128, True),
}

TABH_ORDER = ("mstrict", "mcausal", "twin", "tdil", "cmask", "ov", "e32", "tri", "identb")
TABF_ORDER = ("bsel", "bias_sel", "bias_cmp")
TABH_OFF = {}
_o = 0
for _n in TABH_ORDER:
    TABH_OFF[_n] = _o
    _o += TABLE_SHAPES[_n][0]
TABH_COLS = _o
TABF_OFF = {}
_o = 0
for _n in TABF_ORDER:
    TABF_OFF[_n] = _o
    _o += TABLE_SHAPES[_n][0]
TABF_COLS = _o


def host_table_arrays(r):
    T = host_tables(r)
    tabh = np.concatenate([T[n] for n in TABH_ORDER], axis=1)
    tabf = np.concatenate([T[n] for n in TABF_ORDER], axis=1).astype(np.float32)
    assert tabh.shape == (128, TABH_COLS // 2) and tabf.shape == (128, TABF_COLS)
    return np.ascontiguousarray(tabh), np.ascontiguousarray(tabf)


def run_pipeline(units, stages):
    n = len(units)
    S = len(stages)
    for it in range(n + S - 1):
        for s in range(S):
            idx = it - s
            if 0 <= idx < n:
                stages[s](units[idx])


def emit_mixers(kb, io, cat, cat_b):
    nc = kb.nc
    kb.push_scope()
    bank, bank_b = kb.bank, kb.bank_b
    QTs, QT_b = kb.sbt("M_QT", [128, NQB, TOK], BF16)
    tabh, tabh_b = kb.sbt("M_tabh", [128, TABH_COLS], BF16)
    tabf, tabf_b = kb.sbt("M_tabf", [128, TABF_COLS], F32)
    gs, gs_b = kb.sbt("M_G", [128, NT * 12], F32)
    ones_bf, cst_b = kb.sbt("M_ones", [128, 128], BF16)
    onec = kb.sb("M_onec", [128, 1], F32)
    ocomb, ocomb_b = kb.sbt("M_ocomb", [128, NT, 4, HD], F32)
    negselT, negselT_b = kb.sbt("M_negselT", [128, 2, TOK], BF16)
    kslot = [kb.sbt("M_k%d" % i, [128, RANKS, TOK], BF16) for i in range(2)]
    vslot = [kb.sbt("M_v%d" % i, [128, RANKS * NT, HD + 1], BF16) for i in range(2)]
    cnt = {"k": 0, "v": 0}

    def tab(name, lo, n):
        o = TABH_OFF[name] + lo
        return tabh[:, o:o + n]

    def tabfp(name, lo, n):
        o = TABF_OFF[name] + lo
        return tabf[:, o:o + n]

    kb.dma("sp", QTs[:], io["QT"].rearrange("b p t -> p b t"), reads=[io["QT_b"]], writes=[QT_b])
    hc = TABH_COLS // 4
    for q in range(4):
        kb.dma("sp", tabh[:, q * hc:(q + 1) * hc], io["tabh"][:, q * hc:(q + 1) * hc], writes=[tabh_b])
    kb.dma("sp", tabf[:], io["tabf"], writes=[tabf_b])
    kb.dma("sp", gs[:], io["G"], reads=[io["G_b"]], writes=[gs_b])
    kb.op("pool", lambda e: e.memset(ones_bf[:], 1.0), writes=[cst_b])
    kb.op("pool", lambda e: e.memset(onec[:], 1.0), writes=[cst_b])
    for i in range(2):
        kb.op("pool", lambda e: e.memset(vslot[i][0][:, :, HD:HD + 1], 1.0), writes=[vslot[i][1]])

    def load_k(kbi):
        s = cnt["k"] % 2
        cnt["k"] += 1
        t, b = kslot[s]
        for rp in range(RANKS):
            kb.dma("sp", t[:, rp, :], io["KTa_fn"](rp, kbi), reads=[io["KTa_bf"](kbi)], writes=[b])
        return t, b

    def load_v(vh):
        s = cnt["v"] % 2
        cnt["v"] += 1
        t, b = vslot[s]
        for rp in range(RANKS):
            for q in range(4):
                src = io["Va_fn"](rp, q, vh)
                kb.dma("sp", t[:, rp * NT + q * 4:rp * NT + q * 4 + 4, 0:HD],
                       src.rearrange("(j p) d -> p j d", p=128), reads=[io["Va_bf"](q)], writes=[b])
        return t, b

    class U:
        pass

    def sb_units(hh, ks, vs):
        units = []
        n = 0
        for i in range(NG):
            first = True
            for jp in range(4 * i + 3, -1, -1):
                for rp in range(3, -1, -1):
                    u = U()
                    u.n = n
                    n += 1
                    u.i, u.jp, u.rp = i, jp, rp
                    u.masked = jp >= 4 * i
                    u.a0 = jp - 4 * i if u.masked else 0
                    u.c0 = 128 * u.a0
                    u.first = first
                    first = False
                    u.last = (jp == 0 and rp == 0)
                    units.append(u)
        return units

    def emit_sb_head(h, ks, vs):
        kt, ktb = ks
        vt, vtb = vs
        hp, hh = divmod(h, 2)
        rows = slice(64 * hh, 64 * hh + 64)
        qb = 5 + hp

        ZB = (0, 1, 6)

        def s1(u):
            zb, zbb = bank[ZB[u.n % 3]], bank_b[ZB[u.n % 3]]
            q0 = 512 * u.i + u.c0
            q1 = 512 * u.i + 512
            kb.op("pe", lambda e: e.matmul(zb[:, u.c0:512], lhsT=kt[rows, u.rp, u.jp * 128:(u.jp + 1) * 128],
                                           rhs=QTs[rows, qb, q0:q1], start=True, stop=True),
                  reads=[ktb, QT_b], writes=[zbb])

        def s1a(u):
            zb, zbb = bank[ZB[u.n % 3]], bank_b[ZB[u.n % 3]]
            et, etb = e_sb[u.n % 6]
            kb.op("act", lambda e: e.activation(out=et[:, u.c0:512], in_=zb[:, u.c0:512], func=AF.Exp,
                                                scale=0.125),
                  reads=[zbb], writes=[etb])

        def s1b(u):
            et, etb = e_sb[u.n % 6]
            st, stb = sp_sb[u.n % 3]
            kb.op("act", lambda e: e.activation(out=st[:, u.c0:512], in_=et[:, u.c0:512], func=AF.Ln,
                                                bias=1.0),
                  reads=[etb], writes=[stb])
            if u.masked:
                kb.op("pool", lambda e: e.tensor_tensor(out=st[:, u.c0:u.c0 + 128], in0=st[:, u.c0:u.c0 + 128],
                                                        in1=tab("mstrict", u.rp * 128, 128), op=ALU.mult),
                      reads=[stb, tabh_b], writes=[stb])

        def s2(u):
            gb, gbb = bank[2 + u.n % 2], bank_b[2 + u.n % 2]
            st, stb = sp_sb[u.n % 3]
            q0 = 512 * u.i + u.c0
            q1 = 512 * u.i + 512
            if u.first:
                kb.op("pool", lambda e: e.memset(spacc[:], 0.0), writes=[spacc_b])
            kb.op("pe", lambda e: e.matmul(gb[:, u.c0:512], lhsT=tab("tri", 0, 128), rhs=st[:, u.c0:512],
                                           start=True, stop=u.first),
                  reads=[tabh_b, stb], writes=[gbb])
            if not u.first:
                kb.op("pe", lambda e: e.matmul(gb[:, u.c0:512], lhsT=ones_bf[:], rhs=spacc[:, u.c0:512],
                                               start=False, stop=True),
                      reads=[cst_b, spacc_b], writes=[gbb])
            if not u.last:
                kb.op("dve", lambda e: e.tensor_tensor(out=spacc[:, u.c0:512], in0=spacc[:, u.c0:512],
                                                       in1=st[:, u.c0:512], op=ALU.add),
                      reads=[spacc_b, stb], writes=[spacc_b])

        def s2b(u):
            gb, gbb = bank[2 + u.n % 2], bank_b[2 + u.n % 2]
            at, atb = aT_sb[u.n % 3]
            xt, xtb = x_sb[u.n % 2]
            et, etb = e_sb[u.n % 6]
            kb.op("act", lambda e: e.activation(out=xt[:, u.c0:512], in_=gb[:, u.c0:512], func=AF.Exp,
                                                scale=-1.0),
                  reads=[gbb], writes=[xtb])
            kb.op("dve", lambda e: e.tensor_tensor(out=at[:, u.c0:512], in0=xt[:, u.c0:512], in1=et[:, u.c0:512],
                                                   op=ALU.mult),
                  reads=[xtb, etb], writes=[atb])
            if u.masked:
                kb.op("pool", lambda e: e.tensor_tensor(out=at[:, u.c0:u.c0 + 128], in0=at[:, u.c0:u.c0 + 128],
                                                        in1=tab("mstrict", u.rp * 128, 128), op=ALU.mult),
                      reads=[atb, tabh_b], writes=[atb])

        def s3(u):
            ab, abb = bank[4 + u.i % 2], bank_b[4 + u.i % 2]
            at, atb = aT_sb[u.n % 3]
            for a in range(u.a0, 4):
                kb.op("pe", lambda e: e.matmul(ab[:, a * HD:(a + 1) * HD], lhsT=at[:, a * 128:(a + 1) * 128],
                                               rhs=vt[:, u.rp * NT + u.jp, 0:HD],
                                               start=(u.first and a == u.a0), stop=(u.last and a == 3)),
                      reads=[atb, vtb], writes=[abb])
            if u.last:
                kb.op("dve", lambda e: e.tensor_copy(
                    out=cat[:, 4 * u.i:4 * u.i + 4, 384 + h * HD:384 + (h + 1) * HD],
                    in_=ab[:, 0:4 * HD].rearrange("p (a d) -> p a d", d=HD)),
                    reads=[abb], writes=[cat_b])

        run_pipeline(sb_units(hh, ks, vs), [s1, s1a, s1b, s2, s2b, s3])


    HD1 = HD + 1
    p_sb = [kb.sbt("M_p%d" % i, [128, 512], BF16) for i in range(4)]
    r4 = [kb.sbt("M_r4%d" % i, [128, 8], F32) for i in range(4)]
    r4c = [0]

    def acc_view(ab, lo, n):
        return ab[:, 0:4 * HD1].rearrange("p (a d) -> p a d", d=HD1)[:, :, lo:lo + n]

    def recip_den(ab, abb, i, gate_col):
        t, tb = r4[r4c[0] % 4]
        r4c[0] += 1
        kb.op("dve", lambda e: e.tensor_scalar(out=t[:, 0:4], in0=acc_view(ab, HD, 1), scalar1=1e-30,
                                               scalar2=None, op0=ALU.max),
              reads=[abb], writes=[tb])
        kb.op("dve", lambda e: e.reciprocal(out=t[:, 0:4], in_=t[:, 0:4]), reads=[tb], writes=[tb])
        if gate_col is not None:
            gv = gs[:, 4 * i * 12:(4 * i + 4) * 12].rearrange("p (a k) -> p a k", k=12)[:, :, gate_col]
            kb.op("dve", lambda e: e.tensor_tensor(out=t[:, 4:8], in0=t[:, 0:4], in1=gv, op=ALU.mult),
                  reads=[tb, gs_b], writes=[tb])
        return t, tb

    def evac_nsa(ab, abb, i, h, br, first):
        t, tb = recip_den(ab, abb, i, 3 * h + br)
        for a in range(4):
            src = ab[:, a * HD1:a * HD1 + HD]
            dst = ocomb[:, 4 * i + a, h, :]
            if first:
                kb.op("dve", lambda e: e.tensor_scalar(out=dst, in0=src, scalar1=t[:, 4 + a:5 + a], scalar2=None,
                                                       op0=ALU.mult),
                      reads=[abb, tb], writes=[ocomb_b])
            else:
                kb.op("dve", lambda e: e.scalar_tensor_tensor(out=dst, in0=src, scalar=t[:, 4 + a:5 + a], in1=dst,
                                                              op0=ALU.mult, op1=ALU.add),
                      reads=[abb, tb, ocomb_b], writes=[ocomb_b])
        return t, tb

    def emit_banded(kt, ktb, vt, vtb, rows, qblk, tabname, tab0, combos, bank_s, bank_acc, evac):
        units = []
        n = 0
        for i in range(NG):
            glist = []
            for a in range(4):
                j = 4 * i + a
                valid = [(ci, dj, rp) for ci, (dj, rp) in enumerate(combos) if j - dj >= 0]
                for c0 in range(0, len(valid), 4):
                    u = U()
                    u.i, u.a, u.j = i, a, j
                    u.sub = valid[c0:c0 + 4]
                    glist.append(u)
            for k, u in enumerate(glist):
                u.n = n
                n += 1
                u.first = (k == 0)
                u.last = (k == len(glist) - 1)
                units.append(u)

        def s1(u):
            sbk, sbb = bank[bank_s[u.n % 2]], bank_b[bank_s[u.n % 2]]
            for q, (ci, dj, rp) in enumerate(u.sub):
                jp = u.j - dj
                kb.op("pe", lambda e: e.matmul(sbk[:, q * 128:(q + 1) * 128],
                                               lhsT=kt[rows, rp, jp * 128:(jp + 1) * 128],
                                               rhs=QTs[rows, qblk, u.j * 128:(u.j + 1) * 128],
                                               start=True, stop=True),
                      reads=[ktb, QT_b], writes=[sbb])

        def s2(u):
            sbk, sbb = bank[bank_s[u.n % 2]], bank_b[bank_s[u.n % 2]]
            pt, ptb = p_sb[u.n % 4]
            w = 128 * len(u.sub)
            kb.op("act", lambda e: e.activation(out=pt[:, 0:w], in_=sbk[:, 0:w], func=AF.Exp, scale=0.125),
                  reads=[sbb], writes=[ptb])

        def s2b(u):
            pt, ptb = p_sb[u.n % 4]
            w = 128 * len(u.sub)
            ci0 = u.sub[0][0]
            kb.op("dve", lambda e: e.tensor_tensor(out=pt[:, 0:w], in0=pt[:, 0:w],
                                                   in1=tab(tabname, (tab0 + ci0) * 128, w), op=ALU.mult),
                  reads=[ptb, tabh_b], writes=[ptb])

        def s3(u):
            ab, abb = bank[bank_acc[u.i % 2]], bank_b[bank_acc[u.i % 2]]
            pt, ptb = p_sb[u.n % 4]
            for q, (ci, dj, rp) in enumerate(u.sub):
                jp = u.j - dj
                kb.op("pe", lambda e: e.matmul(ab[:, u.a * HD1:(u.a + 1) * HD1], lhsT=pt[:, q * 128:(q + 1) * 128],
                                               rhs=vt[:, rp * NT + jp, 0:HD1],
                                               start=(u.first and q == 0),
                                               stop=(u.last and q == len(u.sub) - 1)),
                      reads=[ptb, vtb], writes=[abb])
            if u.last:
                evac(ab, abb, u.i)

        run_pipeline(units, [s1, s2, s2b, s3])

    def emit_dil():
        kb.push_scope()
        dacc, dacc_b = kb.sbt("M_dacc", [128, NT, 2, HD1], F32)
        tab0 = 0
        for g in range(3):
            ks = load_k(3 + g)
            for hh in range(2):
                vs = load_v(2 + 2 * g + hh)

                def evac_d(ab, abb, i, g=g, hh=hh):
                    dst = dacc[:, 4 * i:4 * i + 4, hh, :]
                    src = acc_view(ab, 0, HD1)
                    if g == 0:
                        kb.op("dve", lambda e: e.tensor_copy(out=dst, in_=src), reads=[abb], writes=[dacc_b])
                    else:
                        kb.op("dve", lambda e: e.tensor_tensor(out=dst, in0=src, in1=dst, op=ALU.add),
                              reads=[abb, dacc_b], writes=[dacc_b])

                emit_banded(ks[0], ks[1], vs[0], vs[1], slice(64 * hh, 64 * hh + 64), 2 + g, "tdil", tab0,
                            DIL_COMBOS[g], (0, 1), (2, 3), evac_d)
                tab0 += len(DIL_COMBOS[g])
        dr, dr_b = kb.sbt("M_dr", [128, NT * 2], F32)
        dflat = dacc[:].rearrange("p j h d -> p (j h) d")
        kb.op("dve", lambda e: e.reciprocal(out=dr[:], in_=dflat[:, :, HD]), reads=[dacc_b], writes=[dr_b])
        for j in range(NT):
            for hh in range(2):
                kb.op("dve", lambda e: e.tensor_scalar(out=cat[:, j, 256 + hh * HD:256 + (hh + 1) * HD],
                                                       in0=dacc[:, j, hh, 0:HD],
                                                       scalar1=dr[:, 2 * j + hh:2 * j + hh + 1], scalar2=None,
                                                       op0=ALU.mult),
                      reads=[dacc_b, dr_b], writes=[cat_b])
        kb.pop_scope()


    def emit_nsa():
        kb.push_scope()
        _xs = cnt["k"] % 2
        cnt["k"] += 1
        XT_b = kslot[_xs][1]
        W1s, W1_b = kb.sbt("N_W1", [128, 32, 128], BF16)
        W2s, W2_b = kb.sbt("N_W2", [128, 128], BF16)
        peTf, pe_b = kb.sbt("N_peTf", [128, 32], F32)
        peTb, peb_b = kb.sbt("N_peTb", [128, 32], BF16)
        gkc, gkc_b = kb.sbt("N_gkc", [128, HD], F32)
        hTc = [kb.sbt("N_hT%d" % c, [128, 512], BF16) for c in range(2)]
        u_sb, u_b = kb.sbt("N_u", [128, 512], F32)
        t_sb, t_b = kb.sbt("N_t", [128, 512], F32)
        bias2, bias2_b = kb.sbt("N_bias2", [128, 2], F32)
        KcT2, KcT_b = kb.sbt("N_KcT2", [128, 512], BF16)
        Vc, Vc_b = kb.sbt("N_Vc", [128, 4, HD1], BF16)
        kc2, kc2_b = kb.sbt("N_kc2", [128, 128], BF16)
        stt, stt_b = kb.sbt("N_st", [128, 4], F32)
        epsb, epsb_b = kb.sbt("N_eps", [128, 1], F32)
        imp, imp_b = kb.sbt("N_imp", [128, 512], F32)
        impb, impb_b = kb.sbt("N_impb", [128, 512], F32)
        tmpm, tmpm_b = kb.sbt("N_tmpm", [128, 128], F32)
        m8, m8_b = kb.sbt("N_m8", [128, 16], F32)
        nsel, nsel_b = kb.sbt("N_nsel", [128, 128], BF16)
        nsel2, nsel2_b = kb.sbt("N_nsel2", [128, 128], BF16)

        XTflat = kslot[_xs][0][:].rearrange("p r t -> p (r t)")
        xv = XTflat.rearrange("p (j r q) -> p j r q", r=RANKS, q=128)
        for rp in range(RANKS):
            kb.dma("sp", xv[:, :, rp, :], io["KTa_fn"](rp, 2).rearrange("p (j q) -> p j q", q=128),
                   reads=[io["KTa_bf"](2)], writes=[XT_b])
        for hf in range(2):
            kb.dma("pool", W1s[:, hf * 16:(hf + 1) * 16, :].rearrange("p l h -> p (l h)"),
                   io["w1p"][:, hf * 2048:(hf + 1) * 2048], writes=[W1_b])
        kb.dma("pool", W2s[:], io["w2p"], writes=[W2_b])
        kb.dma("sp", peTf[:], io["peT"], writes=[pe_b])
        kb.dma("sp", gkc[:], io["gkc"], writes=[gkc_b])
        kb.op("dve", lambda e: e.tensor_copy(out=peTb[:], in_=peTf[:]), reads=[pe_b], writes=[peb_b])
        kb.op("pool", lambda e: e.memset(epsb[:], RMS_EPS), writes=[epsb_b])
        kb.op("pool", lambda e: e.memset(Vc[:], 1.0), writes=[Vc_b])
        for c in range(2):
            kb.op("pool", lambda e: e.memset(hTc[c][0][:], 0.0), writes=[hTc[c][1]])
        xs = XTflat.rearrange("p (n s) -> p n s", s=16)
        if "n1" in DBG:
            kb.pop_scope()
            return
        for c in range(2):
            rows = slice(64 * c, 64 * c + 64)
            for l in range(32):
                kb.op("pe", lambda e: e.matmul(bank[2][:, 32 * c + l:32 * c + l + 1], lhsT=W1s[rows, l, :],
                                               rhs=peTb[rows, l:l + 1], start=True, stop=True),
                      reads=[W1_b, peb_b], writes=[bank_b[2]])
        kb.op("dve", lambda e: e.tensor_reduce(out=bias2[:], in_=bank[2][:, 0:64].rearrange("p (c l) -> p c l", l=32),
                                               axis=mybir.AxisListType.X, op=ALU.add),
              reads=[bank_b[2]], writes=[bias2_b])
        for c in range(2):
            rows = slice(64 * c, 64 * c + 64)
            pbk, pbb = bank[c], bank_b[c]
            for l in range(32):
                rhs = xs[rows, 0:511, l] if l < 16 else xs[rows, 1:512, l - 16]
                kb.op("pe", lambda e: e.matmul(pbk[:, 0:511], lhsT=W1s[rows, l, :], rhs=rhs,
                                               start=(l == 0), stop=(l == 31)),
                      reads=[W1_b, XT_b], writes=[pbb])
            kb.op("act", lambda e: e.activation(out=u_sb[:, 0:511], in_=pbk[:, 0:511], func=AF.Identity,
                                                bias=bias2[:, c:c + 1]),
                  reads=[pbb, bias2_b], writes=[u_b])
            kb.op("dve", lambda e: e.tensor_tensor(out=t_sb[:, 0:511], in0=u_sb[:, 0:511], in1=u_sb[:, 0:511],
                                                   op=ALU.mult), reads=[u_b], writes=[t_b])
            kb.op("dve", lambda e: e.tensor_scalar(out=t_sb[:, 0:511], in0=t_sb[:, 0:511], scalar1=0.044715,
                                                   scalar2=1.0, op0=ALU.mult, op1=ALU.add),
                  reads=[t_b], writes=[t_b])
            kb.op("dve", lambda e: e.tensor_tensor(out=t_sb[:, 0:511], in0=t_sb[:, 0:511], in1=u_sb[:, 0:511],
                                                   op=ALU.mult), reads=[t_b, u_b], writes=[t_b])
            kb.op("act", lambda e: e.activation(out=t_sb[:, 0:511], in_=t_sb[:, 0:511], func=AF.Sigmoid,
                                                scale=1.5957691216057308), reads=[t_b], writes=[t_b])
            kb.op("dve", lambda e: e.tensor_tensor(out=hTc[c][0][:, 0:511], in0=t_sb[:, 0:511],
                                                   in1=u_sb[:, 0:511], op=ALU.mult),
                  reads=[t_b, u_b], writes=[hTc[c][1]])
        if "n2" in DBG:
            kb.pop_scope()
            return
        for c in range(2):
            for nt in range(4):
                pk, pkb = bank[3 + nt % 2], bank_b[3 + nt % 2]
                kb.op("pe", lambda e: e.matmul(pk[:, 0:HD], lhsT=hTc[c][0][:, nt * 128:(nt + 1) * 128],
                                               rhs=W2s[:, c * HD:(c + 1) * HD], start=True, stop=True),
                      reads=[hTc[c][1], W2_b], writes=[pkb])
                if c == 1:
                    kb.op("act", lambda e: e.activation(out=Vc[:, nt, 0:HD], in_=pk[:, 0:HD], func=AF.Copy),
                          reads=[pkb], writes=[Vc_b])
                    continue
                kb.op("act", lambda e: e.activation(out=t_sb[:, 0:HD], in_=pk[:, 0:HD], func=AF.Square,
                                                    accum_out=stt[:, 0:1]),
                      reads=[pkb], writes=[t_b, stt_b])
                kb.op("dve", lambda e: e.tensor_scalar(out=stt[:, 1:2], in0=stt[:, 0:1], scalar1=1.0 / HD,
                                                       scalar2=RMS_EPS, op0=ALU.mult, op1=ALU.add),
                      reads=[stt_b], writes=[stt_b])
                kb.op("act", lambda e: e.activation(out=stt[:, 2:3], in_=stt[:, 1:2], func=AF.Sqrt),
                      reads=[stt_b], writes=[stt_b])
                kb.op("dve", lambda e: e.reciprocal(out=stt[:, 3:4], in_=stt[:, 2:3]), reads=[stt_b],
                      writes=[stt_b])
                kb.op("dve", lambda e: e.scalar_tensor_tensor(out=kc2[:, 0:HD], in0=pk[:, 0:HD],
                                                              scalar=stt[:, 3:4], in1=gkc[:], op0=ALU.mult,
                                                              op1=ALU.mult),
                      reads=[pkb, stt_b, gkc_b], writes=[kc2_b])
                kb.op("dve", lambda e: e.tensor_copy(out=kc2[:, HD:2 * HD], in_=kc2[:, 0:HD]), reads=[kc2_b],
                      writes=[kc2_b])
                kb.op("pe", lambda e: e.transpose(out=kb.bankh[:, 0:128], in_=kc2[:], identity=tab("identb", 0, 128)),
                      reads=[kc2_b, tabh_b], writes=[kb.bankh_b])
                kb.op("act", lambda e: e.activation(out=KcT2[:, nt * 128:(nt + 1) * 128], in_=kb.bankh[:, 0:128],
                                                    func=AF.Copy), reads=[kb.bankh_b], writes=[KcT_b])
        if "n3" in DBG:
            kb.op("dve", lambda e: e.tensor_copy(out=cat[:, 0, 0:512], in_=KcT2[:]), reads=[KcT_b], writes=[cat_b])
            kb.op("dve", lambda e: e.tensor_copy(out=cat[:, 1, 0:260], in_=Vc[:].rearrange("p a d -> p (a d)")),
                  reads=[Vc_b], writes=[cat_b])
            kb.op("dve", lambda e: e.tensor_copy(out=cat[:, 2, 0:512], in_=hTc[0][0][:]), reads=[hTc[0][1]], writes=[cat_b])
            kb.op("dve", lambda e: e.tensor_copy(out=cat[:, 3, 0:512], in_=hTc[1][0][:]), reads=[hTc[1][1]], writes=[cat_b])
            kb.op("dve", lambda e: e.tensor_copy(out=cat[:, 4, 0:2], in_=bias2[:]), reads=[bias2_b], writes=[cat_b])
            kb.pop_scope()
            return

        for i in range(NG):
            for h in range(4):
                rows = slice(64 * (h % 2), 64 * (h % 2) + 64)
                accC, accC_b = bank[2 + 2 * (h % 2)], bank_b[2 + 2 * (h % 2)]
                accI, accI_b = bank[3 + 2 * (h % 2)], bank_b[3 + 2 * (h % 2)]
                for nt in range(i + 1):
                    dl = i - nt
                    sbk, sbb = bank[nt % 2], bank_b[nt % 2]
                    pt, ptb = p_sb[nt % 3]
                    kb.op("pe", lambda e: e.matmul(sbk[:, :], lhsT=KcT2[rows, nt * 128:(nt + 1) * 128],
                                                   rhs=QTs[rows, h // 2, 512 * i:512 * i + 512],
                                                   start=True, stop=True),
                          reads=[KcT_b, QT_b], writes=[sbb])
                    kb.op("act", lambda e: e.activation(out=pt[:], in_=sbk[:, :], func=AF.Exp, scale=0.125,
                                                        bias=tabfp("bias_cmp", 4 * h + dl, 1)),
                          reads=[sbb, tabf_b], writes=[ptb])
                    if dl <= 1:
                        kb.op("pool", lambda e: e.tensor_tensor(out=pt[:], in0=pt[:],
                                                                in1=tab("cmask", 512 * dl, 512), op=ALU.mult),
                              reads=[ptb, tabh_b], writes=[ptb])
                    for a in range(4):
                        kb.op("pe", lambda e: e.matmul(accC[:, a * HD1:(a + 1) * HD1],
                                                       lhsT=pt[:, a * 128:(a + 1) * 128], rhs=Vc[:, nt, :],
                                                       start=(nt == 0 and a == 0), stop=(nt == i and a == 3)),
                              reads=[ptb, Vc_b], writes=[accC_b])
                        kb.op("pe", lambda e: e.matmul(accI[:, a * 128:(a + 1) * 128],
                                                       lhsT=pt[:, a * 128:(a + 1) * 128],
                                                       rhs=tab("ov", nt * 128, 128),
                                                       start=(nt == 0 and a == 0), stop=(nt == i and a == 3)),
                              reads=[ptb, tabh_b], writes=[accI_b])
                t, tb = evac_nsa(accC, accC_b, i, h, 0, True)
                for a in range(4):
                    src = accI[:, a * 128:(a + 1) * 128]
                    dst = imp[:, a * 128:(a + 1) * 128]
                    if h == 0:
                        kb.op("dve", lambda e: e.tensor_scalar(out=dst, in0=src, scalar1=t[:, a:a + 1],
                                                               scalar2=None, op0=ALU.mult),
                              reads=[accI_b, tb], writes=[imp_b])
                    else:
                        kb.op("dve", lambda e: e.scalar_tensor_tensor(out=dst, in0=src, scalar=t[:, a:a + 1],
                                                                      in1=dst, op0=ALU.mult, op1=ALU.add),
                              reads=[accI_b, tb, imp_b], writes=[imp_b])
            kb.op("dve", lambda e: e.tensor_tensor(out=impb[:], in0=imp[:], in1=tabfp("bsel", 512 * i, 512),
                                                   op=ALU.add), reads=[imp_b, tabf_b], writes=[impb_b])
            for a in range(4):
                iv = impb[:, a * 128:(a + 1) * 128]
                kb.op("dve", lambda e: e.max(out=m8[:, 0:8], in_=iv), reads=[impb_b], writes=[m8_b])
                kb.op("dve", lambda e: e.match_replace(out=tmpm[:], in_to_replace=m8[:, 0:8], in_values=iv,
                                                       imm_value=-3.0e38),
                      reads=[impb_b, m8_b], writes=[tmpm_b])
                kb.op("dve", lambda e: e.max(out=m8[:, 8:16], in_=tmpm[:]), reads=[tmpm_b], writes=[m8_b])
                kb.op("dve", lambda e: e.tensor_scalar(out=nsel[:], in0=iv, scalar1=m8[:, 15:16], scalar2=-1.0,
                                                       op0=ALU.is_ge, op1=ALU.add),
                      reads=[impb_b, m8_b], writes=[nsel_b])
                for bh in range(2):
                    kb.op("dve", lambda e: e.tensor_copy(out=nsel2[:, 0:64], in_=nsel[:, 64 * bh:64 * bh + 64]),
                          reads=[nsel_b], writes=[nsel2_b])
                    kb.op("dve", lambda e: e.tensor_copy(out=nsel2[:, 64:128], in_=nsel[:, 64 * bh:64 * bh + 64]),
                          reads=[nsel_b], writes=[nsel2_b])
                    kb.op("pe", lambda e: e.transpose(out=kb.bankh[:, 128:256], in_=nsel2[:],
                                                      identity=tab("identb", 0, 128)),
                          reads=[nsel2_b, tabh_b], writes=[kb.bankh_b])
                    kb.op("act", lambda e: e.activation(
                        out=negselT[:, bh, 512 * i + 128 * a:512 * i + 128 * (a + 1)],
                        in_=kb.bankh[:, 128:256], func=AF.Copy),
                        reads=[kb.bankh_b], writes=[negselT_b])
        kb.pop_scope()
        if "n4" in DBG:
            kb.op("act", lambda e: e.activation(out=cat[:, :, 0:256], in_=ocomb[:].rearrange("p j h d -> p j (h d)"),
                                                func=AF.Copy), reads=[ocomb_b], writes=[cat_b])
            return

        ks = load_k(0)
        vs = load_v(0)
        kt, ktb = ks
        vt, vtb = vs
        for h in range(4):
            rows = slice(64 * (h % 2), 64 * (h % 2) + 64)
            units = []
            n = 0
            for i in range(NG):
                tot = 4 * (4 * i + 4)
                k = 0
                for jp in range(4 * i + 4):
                    for rp in range(4):
                        u = U()
                        u.n = n
                        n += 1
                        u.i, u.jp, u.rp = i, jp, rp
                        u.masked = jp >= 4 * i
                        u.a0 = jp - 4 * i if u.masked else 0
                        u.c0 = 128 * u.a0
                        u.first = (k == 0)
                        u.last = (k == tot - 1)
                        k += 1
                        units.append(u)

            def s1(u):
                sbk, sbb = bank[u.n % 2], bank_b[u.n % 2]
                q0, q1 = 512 * u.i + u.c0, 512 * u.i + 512
                mk = 4 * u.jp + u.rp
                q4, uu = divmod(mk, 32)
                kb.op("pe", lambda e: e.matmul(sbk[:, u.c0:512], lhsT=kt[rows, u.rp, u.jp * 128:(u.jp + 1) * 128],
                                               rhs=QTs[rows, h // 2, q0:q1], start=True, stop=False),
                      reads=[ktb, QT_b], writes=[sbb])
                kb.op("pe", lambda e: e.matmul(sbk[:, u.c0:512],
                                               lhsT=tab("e32", uu * 128, 128)[rows, :],
                                               rhs=negselT[rows, q4, q0:q1], start=False, stop=True),
                      reads=[tabh_b, negselT_b], writes=[sbb])

            def s2(u):
                sbk, sbb = bank[u.n % 2], bank_b[u.n % 2]
                pt, ptb = p_sb[u.n % 3]
                col = 64 * h + 4 * (u.jp - 4 * u.i) + u.rp + 48
                kb.op("act", lambda e: e.activation(out=pt[:, u.c0:512], in_=sbk[:, u.c0:512], func=AF.Exp,
                                                    scale=0.125, bias=tabfp("bias_sel", col, 1)),
                      reads=[sbb, tabf_b], writes=[ptb])
                if u.masked:
                    kb.op("pool", lambda e: e.tensor_tensor(out=pt[:, u.c0:u.c0 + 128], in0=pt[:, u.c0:u.c0 + 128],
                                                            in1=tab("mcausal", u.rp * 128, 128), op=ALU.mult),
                          reads=[ptb, tabh_b], writes=[ptb])

            def s3(u):
                ab, abb = bank[2 + u.i % 2], bank_b[2 + u.i % 2]
                pt, ptb = p_sb[u.n % 3]
                for a in range(u.a0, 4):
                    kb.op("pe", lambda e: e.matmul(ab[:, a * HD1:(a + 1) * HD1], lhsT=pt[:, a * 128:(a + 1) * 128],
                                                   rhs=vt[:, u.rp * NT + u.jp, :],
                                                   start=(u.first and a == u.a0), stop=(u.last and a == 3)),
                          reads=[ptb, vtb], writes=[abb])
                if u.last:
                    evac_nsa(ab, abb, u.i, h, 1, False)

            run_pipeline(units, [s1, s2, s3])

        if "n5" in DBG:
            kb.op("act", lambda e: e.activation(out=cat[:, :, 0:256], in_=ocomb[:].rearrange("p j h d -> p j (h d)"),
                                                func=AF.Copy), reads=[ocomb_b], writes=[cat_b])
            return
        ks = load_k(1)
        vs = load_v(1)
        wcombos = [(dj, rp) for dj in range(2) for rp in range(4)]
        for h in range(4):
            emit_banded(ks[0], ks[1], vs[0], vs[1], slice(64 * (h % 2), 64 * (h % 2) + 64), h // 2, "twin", 8 * h,
                        wcombos, (0, 1), (2, 3),
                        lambda ab, abb, i, h=h: evac_nsa(ab, abb, i, h, 2, False))
        kb.op("act", lambda e: e.activation(out=cat[:, :, 0:256], in_=ocomb[:].rearrange("p j h d -> p j (h d)"),
                                            func=AF.Copy), reads=[ocomb_b], writes=[cat_b])


    if "nonsa" not in DBG:
        emit_nsa()
    if "nodil" not in DBG:
        emit_dil()

    kb.push_scope()
    e_sb = [kb.sbt("M_e%d" % i, [128, 512], F32) for i in range(6)]
    x_sb = [kb.sbt("M_x%d" % i, [128, 512], F32) for i in range(2)]
    sp_sb = [kb.sbt("M_sp%d" % i, [128, 512], BF16) for i in range(3)]
    aT_sb = [kb.sbt("M_aT%d" % i, [128, 512], BF16) for i in range(3)]
    spacc, spacc_b = kb.sbt("M_spacc", [128, 512], BF16)

    if "nosb" not in DBG:
        for hp in range(3):
            ks = load_k(6 + hp)
            for hh in range(2):
                if "sb1" in DBG and (hp, hh) != (0, 0):
                    continue
                vs = load_v(8 + 2 * hp + hh)
                emit_sb_head(2 * hp + hh, ks, vs)
    kb.pop_scope()

    kb.pop_scope()


def declare_mixer_io(kb, io):
    io["QT"] = kb.dram_bf16("QT", [NQB, 128, TOK], "ExternalInput")
    io["KTa"] = kb.dram_bf16("KTa", [RANKS, NKB, 128, TOK], "ExternalInput")
    io["Va"] = kb.dram_bf16("Va", [RANKS, TOK, NVH * HD], "ExternalInput")
    io["G"] = kb.dram("G", [128, NT * 12], F32, "ExternalInput")
    io["tabh"] = kb.dram_bf16("tabh", [128, TABH_COLS], "ExternalInput")
    io["tabf"] = kb.dram("tabf", [128, TABF_COLS], F32, "ExternalInput")
    io["w1p"] = kb.dram("w1p", [128, 32 * 128], F32, "ExternalInput")
    io["w2p"] = kb.dram("w2p", [128, 128], F32, "ExternalInput")
    io["peT"] = kb.dram("peT", [128, 32], F32, "ExternalInput")
    io["gkc"] = kb.dram("gkc", [128, HD], F32, "ExternalInput")
    for n in ("QT", "KTa", "Va", "G"):
        io[n + "_b"] = kb.buf(n)
    io["KTa_fn"] = lambda rp, kbi: io["KTa"][rp, kbi]
    io["KTa_bf"] = lambda kbi: io["KTa_b"]
    io["Va_fn"] = lambda rp, q, vh: io["Va"][rp, q * 512:(q + 1) * 512, vh * HD:(vh + 1) * HD]
    io["Va_bf"] = lambda q: io["Va_b"]


def build_B_debug():
    kb = KB()
    io = {}
    declare_mixer_io(kb, io)
    cat_out = kb.dram_bf16("cat_out", [128, NT * D_CAT], "ExternalOutput")
    cat_out_b = kb.buf("cat_out")
    cat, cat_b = kb.sbt("cat", [128, NT, D_CAT], BF16)
    kb.op("pool", lambda e: e.memset(cat[:], 0.0), writes=[cat_b])
    emit_mixers(kb, io, cat, cat_b)
    kb.dma("sp", cat_out, cat[:].rearrange("p j d -> p (j d)"), reads=[cat_b], writes=[cat_out_b])
    kb.finish([cat_out_b])
    kb.close()
    return kb


def emit_phase_C(kb, io, cat, cat_b, x_src, x_dst):
    bank, bank_b = kb.bank, kb.bank_b
    kb.push_scope()
    xres, xres_b = kb.sbt("C_xres", [128, NT, D_MODEL], F32)
    xres_bj = [kb.buf("C_xres%d" % j) for j in range(NT)]
    h2T = kb.sb("C_h2T", [128, DC, TOK], BF16)
    h2T_b = [kb.buf("C_h2T%d" % g) for g in range(NG)]
    gm, cst_b = kb.sbt("C_gm", [128, DC], F32)
    ident, ident_b = kb.sbt("C_ident", [128, 128], F32)
    identb, identb_b = kb.sbt("C_identb", [128, 128], BF16)
    kb.dma("sp", gm[:], io["gmlp"], writes=[cst_b])
    kb.dma("sp", ident[:], io["ident"], writes=[ident_b])
    kb.op("dve", lambda e: e.tensor_copy(out=identb[:], in_=ident[:]), reads=[ident_b], writes=[identb_b])

    kb.push_scope()
    catT, catT_b = kb.sbt("C_catT", [128, 6, 128], BF16), None
    catT = [kb.sbt("C_catT%d" % i, [128, 6, 128], BF16) for i in range(2)]
    Wo, Wo_b = kb.sbt("C_Wo", [128, 6, D_MODEL], BF16)
    xt = [kb.sbt("C_xt%d" % i, [128, D_MODEL], F32) for i in range(2)]
    xn = [kb.sbt("C_xn%d" % i, [128, D_MODEL], F32) for i in range(2)]
    junk, junk_b = kb.sbt("C_junk", [128, D_MODEL], F32)
    st = [kb.sbt("C_st%d" % i, [128, 4], F32) for i in range(2)]
    for c in range(6):
        kb.dma("pool", Wo[:, c, :], io["wout"][c * 128:(c + 1) * 128, :], writes=[Wo_b])
    def c1_p1(j):
        s = j % 2
        g = j // 4
        src_ap, src_b = x_src(j)
        kb.dma("sp", xt[s][0][:], src_ap, reads=[src_b] if src_b else [], writes=[xt[s][1]])
        for c in range(6):
            kb.op("pe", lambda e: e.transpose(out=kb.bankh[:, c * 128:(c + 1) * 128],
                                              in_=cat[:, j, c * 128:(c + 1) * 128], identity=identb[:]),
                  reads=[cat_b, identb_b], writes=[kb.bankh_b])
        ct, ctb = catT[s]
        kb.op("act", lambda e: e.activation(out=ct[:], in_=kb.bankh[:, 0:768].rearrange("p (c t) -> p c t", t=128),
                                            func=AF.Copy), reads=[kb.bankh_b], writes=[ctb])
        for hf in range(2):
            p, pb = bank[hf], bank_b[hf]
            for c in range(6):
                kb.op("pe", lambda e: e.matmul(p[:, :], lhsT=ct[:, c, :], rhs=Wo[:, c, hf * 512:(hf + 1) * 512],
                                               start=(c == 0), stop=(c == 5)),
                      reads=[ctb, Wo_b], writes=[pb])
            kb.op("dve", lambda e: e.tensor_tensor(out=xres[:, j, hf * 512:(hf + 1) * 512], in0=p[:, :],
                                                   in1=xt[s][0][:, hf * 512:(hf + 1) * 512], op=ALU.add),
                  reads=[pb, xt[s][1]], writes=[xres_bj[j]])

    def c1_p2(j):
        s = j % 2
        g = j // 4
        kb.op("act", lambda e: e.activation(out=junk[:], in_=xres[:, j, :], func=AF.Square,
                                            accum_out=st[s][0][:, 0:1]),
              reads=[xres_bj[j]], writes=[junk_b, st[s][1]])
        kb.op("dve", lambda e: e.tensor_scalar(out=st[s][0][:, 1:2], in0=st[s][0][:, 0:1], scalar1=1.0 / D_MODEL,
                                               scalar2=RMS_EPS, op0=ALU.mult, op1=ALU.add),
              reads=[st[s][1]], writes=[st[s][1]])
        kb.op("act", lambda e: e.activation(out=st[s][0][:, 2:3], in_=st[s][0][:, 1:2], func=AF.Sqrt),
              reads=[st[s][1]], writes=[st[s][1]])
        kb.op("dve", lambda e: e.reciprocal(out=st[s][0][:, 3:4], in_=st[s][0][:, 2:3]),
              reads=[st[s][1]], writes=[st[s][1]])
        kb.op("dve", lambda e: e.tensor_scalar(out=xn[s][0][:], in0=xres[:, j, :], scalar1=st[s][0][:, 3:4],
                                               scalar2=None, op0=ALU.mult),
              reads=[xres_bj[j], st[s][1]], writes=[xn[s][1]])

    def c1_p3(j):
        s = j % 2
        g = j // 4
        for hf in range(2):
            p, pb = bank[2 + hf], bank_b[2 + hf]
            for cc in range(4):
                c = hf * 4 + cc
                kb.op("pe", lambda e: e.transpose(out=p[:, cc * 128:(cc + 1) * 128],
                                                  in_=xn[s][0][:, c * 128:(c + 1) * 128], identity=ident[:]),
                      reads=[xn[s][1], ident_b], writes=[pb])
            for cc in range(4):
                c = hf * 4 + cc
                if cc % 2 == 0:
                    kb.op("act", lambda e: e.activation(out=h2T[:, c, j * 128:(j + 1) * 128],
                                                        in_=p[:, cc * 128:(cc + 1) * 128], func=AF.Copy,
                                                        scale=gm[:, c:c + 1]),
                          reads=[pb, cst_b], writes=[h2T_b[g]])
                else:
                    kb.op("dve", lambda e: e.tensor_scalar(out=h2T[:, c, j * 128:(j + 1) * 128],
                                                           in0=p[:, cc * 128:(cc + 1) * 128],
                                                           scalar1=gm[:, c:c + 1], scalar2=None, op0=ALU.mult),
                          reads=[pb, cst_b], writes=[h2T_b[g]])

    run_pipeline(list(range(NT)), [c1_p1, c1_p2, c1_p3])
    kb.pop_scope()

    kb.push_scope()
    NE = 8
    FE = D_FF // NE
    Wu = [kb.sbt("C_Wu%d" % i, [128, DC, FE], BF16) for i in range(2)]
    Wd = [kb.sbt("C_Wd%d" % i, [128, FE // 128, D_MODEL], BF16) for i in range(2)]
    actT = [kb.sbt("C_act%d" % i, [128, FE // 128, 512], BF16) for i in range(2)]
    rl = [kb.sbt("C_rl%d" % i, [128, 512], F32) for i in range(2)]
    n = 0
    nctr = [0]

    def c2_loads(e8):
        ws = e8 % 2
        for c in range(DC):
            kb.dma("pool", Wu[ws][0][:, c, :], io["wup"][c * 128:(c + 1) * 128, e8 * FE:(e8 + 1) * FE],
                   writes=[Wu[ws][1]])
        for fc in range(FE // 128):
            r0 = e8 * FE + fc * 128
            kb.dma("pool", Wd[ws][0][:, fc, :], io["wdown"][r0:r0 + 128, :], writes=[Wd[ws][1]])

    def c2_up(e8, g):
        ws = e8 % 2
        at, atb = actT[g % 2]
        for fc in range(FE // 128):
            n = nctr[0]
            p, pb = bank[n % 2], bank_b[n % 2]
            rt, rtb = rl[n % 2]
            nctr[0] += 1
            for c in range(DC):
                kb.op("pe", lambda e: e.matmul(p[:, :], lhsT=Wu[ws][0][:, c, fc * 128:(fc + 1) * 128],
                                               rhs=h2T[:, c, g * 512:(g + 1) * 512],
                                               start=(c == 0), stop=(c == DC - 1)),
                      reads=[Wu[ws][1], h2T_b[g]], writes=[pb])
            kb.op("act", lambda e: e.activation(out=rt[:], in_=p[:, :], func=AF.Relu), reads=[pb], writes=[rtb])
            kb.op("dve", lambda e: e.tensor_tensor(out=at[:, fc, :], in0=rt[:], in1=rt[:], op=ALU.mult),
                  reads=[rtb], writes=[atb])

    def c2_down(e8, g):
        ws = e8 % 2
        at, atb = actT[g % 2]
        for a in range(4):
            j = 4 * g + a
            for hf in range(2):
                p, pb = bank[2 + (2 * a + hf) % 4], bank_b[2 + (2 * a + hf) % 4]
                for fc in range(FE // 128):
                    kb.op("pe", lambda e: e.matmul(p[:, :], lhsT=at[:, fc, a * 128:(a + 1) * 128],
                                                   rhs=Wd[ws][0][:, fc, hf * 512:(hf + 1) * 512],
                                                   start=(fc == 0), stop=(fc == FE // 128 - 1)),
                          reads=[atb, Wd[ws][1]], writes=[pb])
                kb.op("dve", lambda e: e.tensor_tensor(out=xres[:, j, hf * 512:(hf + 1) * 512], in0=p[:, :],
                                                       in1=xres[:, j, hf * 512:(hf + 1) * 512], op=ALU.add),
                      reads=[pb, xres_bj[j]], writes=[xres_bj[j]])
            if e8 == NE - 1:
                dst_ap, dst_b = x_dst(j)
                kb.dma("sp", dst_ap, xres[:, j, :], reads=[xres_bj[j]], writes=[dst_b])

    steps = [(e8, g) for e8 in range(NE) for g in range(NG)]
    prev = None
    for (e8, g) in steps:
        if g == 0:
            c2_loads(e8)
        c2_up(e8, g)
        if prev is not None:
            c2_down(*prev)
        prev = (e8, g)
    c2_down(*prev)
    kb.pop_scope()
    kb.pop_scope()


def build_B():
    kb = KB()
    io = {}
    declare_mixer_io(kb, io)
    x = kb.dram("x_own", [TOK, D_MODEL], F32, "ExternalInput")
    xo = kb.dram("x_out", [TOK, D_MODEL], F32, "ExternalOutput")
    xo_b = kb.buf("x_out")
    io["wout"] = kb.dram("wout", [D_CAT, D_MODEL], F32, "ExternalInput")
    io["wup"] = kb.dram("wup", [D_MODEL, D_FF], F32, "ExternalInput")
    io["wdown"] = kb.dram("wdown", [D_FF, D_MODEL], F32, "ExternalInput")
    io["gmlp"] = kb.dram("gmlp", [128, DC], F32, "ExternalInput")
    io["ident"] = kb.dram("ident", [128, 128], F32, "ExternalInput")
    cat, cat_b = kb.sbt("cat", [128, NT, D_CAT], BF16)
    emit_mixers(kb, io, cat, cat_b)
    emit_phase_C(kb, io, cat, cat_b, lambda j: (x[j * 128:(j + 1) * 128, :], None),
                 lambda j: (xo[j * 128:(j + 1) * 128, :], xo_b))
    kb.finish([xo_b])
    kb.close()
    return kb


PIECE = 256
NPIECE = 9
PIECE_ORDER = (1, 0, 5, 6, 7, 8, 2, 3, 4)
LAYER_W = ("winp", "gmix", "gfm", "w1p", "w2p", "peT", "gkc", "wout", "wup", "wdown", "gmlp")


def build_fused():
    kb = KB()
    nc = kb.nc
    x = kb.dram("x_own", [TOK, D_MODEL], F32, "ExternalInput")
    xo = kb.dram("x_out", [TOK, D_MODEL], F32, "ExternalOutput")
    xo_b = kb.buf("x_out")
    shapes = {"winp": [D_MODEL, WP], "gmix": [128, DC], "gfm": [128, NFM], "w1p": [128, 4096], "w2p": [128, 128],
              "peT": [128, 32], "gkc": [128, HD], "wout": [D_CAT, D_MODEL], "wup": [D_MODEL, D_FF],
              "wdown": [D_FF, D_MODEL], "gmlp": [128, DC]}
    ext = {n: kb.dram(n, [DEPTH] + shapes[n], F32, "ExternalInput") for n in LAYER_W}
    ident = kb.dram("ident", [128, 128], F32, "ExternalInput")
    tabh = kb.dram_bf16("tabh", [128, TABH_COLS], "ExternalInput")
    tabf = kb.dram("tabf", [128, TABF_COLS], F32, "ExternalInput")
    QT = kb.dram("QT_i", [NQB, 128, TOK], BF16, "Internal")
    G = kb.dram("G_i", [128, NT * 12], F32, "Internal")
    xmid = kb.dram("xmid_i", [TOK, D_MODEL], F32, "Internal")
    xmid_b = kb.buf("xmid")
    gbo = [[kb.dram("gbo%d_%d" % (l, i), [PIECE, 2048], BF16, "Internal") for i in range(NPIECE)]
           for l in range(DEPTH)]
    gba = [[kb.dram("gba%d_%d" % (l, i), [RANKS * PIECE, 2048], BF16, "Internal") for i in range(NPIECE)]
           for l in range(DEPTH)]
    cat, cat_b = kb.sbt("cat", [128, NT, D_CAT], BF16)
    QT_b, G_b = kb.buf("QT"), kb.buf("G")
    for l in range(DEPTH):
        io = {n: ext[n][l] for n in LAYER_W}
        io["ident"], io["tabh"], io["tabf"] = ident, tabh, tabf
        io["QT"], io["QT_b"], io["G"], io["G_b"] = QT, QT_b, G, G_b
        gbo_b = [kb.buf("gbo%d_%d" % (l, i)) for i in range(NPIECE)]
        gba_b = [kb.buf("gba%d_%d" % (l, i)) for i in range(NPIECE)]
        go, ga = gbo[l], gba[l]
        io["KT_fn"] = lambda ki, go=go: go[ki // 2][(ki % 2) * 128:(ki % 2) * 128 + 128, :]
        io["KT_bf"] = lambda ki, gbo_b=gbo_b: gbo_b[ki // 2]
        io["V_fn"] = lambda jj, go=go: go[5 + jj // 4][:, :].rearrange("a (u d) -> (a u) d", d=1024)[
            (jj % 4) * 128:(jj % 4) * 128 + 128, :]
        io["V_cols"] = 1024
        io["zero_fill"] = [(go[4][128:256, :], gbo_b[4])]
        io["V_bf"] = lambda jj, gbo_b=gbo_b: gbo_b[5 + jj // 4]
        io["KTa_fn"] = lambda rp, kbi, ga=ga: ga[kbi // 2][rp * PIECE + (kbi % 2) * 128:
                                                         rp * PIECE + (kbi % 2) * 128 + 128, :]
        io["KTa_bf"] = lambda kbi, gba_b=gba_b: gba_b[kbi // 2]
        io["Va_fn"] = lambda rp, q, vh, ga=ga: ga[5 + q][rp * PIECE:(rp + 1) * PIECE, :] \
            .rearrange("a (u d) -> (a u) d", d=1024)[:, vh * HD:(vh + 1) * HD]
        io["Va_bf"] = lambda q, gba_b=gba_b: gba_b[5 + q]
        if l == 0:
            x_src = lambda j: (x[j * 128:(j + 1) * 128, :], None)
        else:
            x_src = lambda j: (xmid[j * 128:(j + 1) * 128, :], xmid_b)
        if l == DEPTH - 1:
            x_dst = lambda j: (xo[j * 128:(j + 1) * 128, :], xo_b)
        else:
            x_dst = lambda j: (xmid[j * 128:(j + 1) * 128, :], xmid_b)
        emit_phase_A(kb, x_src, None, io)
        for i in PIECE_ORDER:
            if "nocc" in DBG:
                for rr in range(RANKS):
                    kb.dma("sp", ga[i][rr * PIECE:(rr + 1) * PIECE, :], go[i][:, :], reads=[gbo_b[i]],
                           writes=[gba_b[i]])
                continue
            kb.dma("pool", None, None, reads=[gbo_b[i]], writes=[gba_b[i]],
                   fn=lambda e, i=i: e.collective_compute(
                       "AllGather", ALU.bypass, replica_groups=[[0, 1, 2, 3], [4, 5, 6, 7]],
                       ins=[go[i][:, :]], outs=[ga[i][:, :]]))
        emit_mixers(kb, io, cat, cat_b)
        emit_phase_C(kb, io, cat, cat_b, x_src, x_dst)
    kb.finish([xo_b])
    kb.close()
    return kb


_PROGS = {}
LAST_EXEC_NS = None


def _prog(name):
    if name not in _PROGS:
        _PROGS[name] = {"A": build_A, "B": build_B, "F": build_fused}[name]()
    return _PROGS[name]


def kernel_unfused(x, norm_mix, norm_mlp, w_in, qk_gain_nsa, qk_gain_dil, cmp_pe, cmp_w1, cmp_w2, w_out, w_up, w_down):
    f32 = lambda a: np.ascontiguousarray(np.asarray(a, dtype=np.float32))
    x = f32(x)
    perm = win_column_perm()
    ident = np.eye(128, dtype=np.float32)
    rows = [core_rows(c) for c in range(NCORE)]
    tabs = [host_table_arrays(r) for r in range(RANKS)]
    xc = [np.ascontiguousarray(x[b, idx]) for (b, r, idx) in rows]
    for l in range(DEPTH):
        gmix, gfm = host_consts_A(f32(norm_mix[l]), f32(qk_gain_nsa[l]), f32(qk_gain_dil[l]))
        winp = np.ascontiguousarray(f32(w_in[l])[:, perm])
        resA = run_bass_kernel_spmd(_prog("A").nc, [
            {"x_own": xc[c], "winp": winp, "gmix": gmix, "gfm": gfm, "ident": ident} for c in range(NCORE)],
            core_ids=list(range(NCORE))).results
        w1p = np.ascontiguousarray(f32(cmp_w1[l]).reshape(2, 32, 64, 128).transpose(0, 2, 1, 3).reshape(128, 4096))
        w2p = np.ascontiguousarray(f32(cmp_w2[l]).transpose(1, 0, 2).reshape(128, 128))
        peT = np.ascontiguousarray(f32(cmp_pe[l]).transpose(0, 2, 1).reshape(128, 32))
        gkc = np.ascontiguousarray(np.broadcast_to(f32(qk_gain_nsa[l])[1], (128, HD)))
        gmlp = np.ascontiguousarray(f32(norm_mlp[l]).reshape(DC, 128).T)
        in_maps = []
        for c in range(NCORE):
            b, r, idx = rows[c]
            KTa = np.stack([resA[4 * b + rr]["KT"] for rr in range(RANKS)])
            Va = np.stack([resA[4 * b + rr]["V"] for rr in range(RANKS)])
            in_maps.append({"QT": resA[c]["QT"], "KTa": KTa, "Va": Va, "G": resA[c]["G"],
                            "tabh": tabs[r][0], "tabf": tabs[r][1], "w1p": w1p, "w2p": w2p, "peT": peT,
                            "gkc": gkc, "x_own": xc[c], "wout": f32(w_out[l]), "wup": f32(w_up[l]),
                            "wdown": f32(w_down[l]), "gmlp": gmlp, "ident": ident})
        resB = run_bass_kernel_spmd(_prog("B").nc, in_maps, core_ids=list(range(NCORE))).results
        xc = [resB[c]["x_out"] for c in range(NCORE)]
    out = np.empty((BATCH, SEQ, D_MODEL), np.float32)
    for c in range(NCORE):
        b, r, idx = rows[c]
        out[b, idx] = xc[c]
    return out


def kernel(x, norm_mix, norm_mlp, w_in, qk_gain_nsa, qk_gain_dil, cmp_pe, cmp_w1, cmp_w2, w_out, w_up, w_down):
    f32 = lambda a: np.ascontiguousarray(np.asarray(a, dtype=np.float32))
    x = f32(x)
    perm = win_column_perm()
    rows = [core_rows(c) for c in range(NCORE)]
    tabs = [host_table_arrays(r) for r in range(RANKS)]
    L = {n: [] for n in LAYER_W}
    for l in range(DEPTH):
        gmix, gfm = host_consts_A(f32(norm_mix[l]), f32(qk_gain_nsa[l]), f32(qk_gain_dil[l]))
        L["gmix"].append(gmix)
        L["gfm"].append(gfm)
        L["winp"].append(f32(w_in[l])[:, perm])
        L["w1p"].append(f32(cmp_w1[l]).reshape(2, 32, 64, 128).transpose(0, 2, 1, 3).reshape(128, 4096))
        L["w2p"].append(f32(cmp_w2[l]).transpose(1, 0, 2).reshape(128, 128))
        L["peT"].append(f32(cmp_pe[l]).transpose(0, 2, 1).reshape(128, 32))
        L["gkc"].append(np.broadcast_to(f32(qk_gain_nsa[l])[1], (128, HD)))
        L["gmlp"].append(f32(norm_mlp[l]).reshape(DC, 128).T)
        L["wout"].append(f32(w_out[l]))
        L["wup"].append(f32(w_up[l]))
        L["wdown"].append(f32(w_down[l]))
    shared = {n: np.ascontiguousarray(np.stack(v)).astype(np.float32) for n, v in L.items()}
    shared["ident"] = np.eye(128, dtype=np.float32)
    in_maps = []
    for c in range(NCORE):
        b, r, idx = rows[c]
        m = dict(shared)
        m["x_own"] = np.ascontiguousarray(x[b, idx])
        m["tabh"], m["tabf"] = tabs[r]
        in_maps.append(m)
    _r = run_bass_kernel_spmd(_prog("F").nc, in_maps, core_ids=list(range(NCORE)))
    global LAST_EXEC_NS
    LAST_EXEC_NS = getattr(_r, "exec_time_ns", None)
    res = _r.results
    out = np.empty((BATCH, SEQ, D_MODEL), np.float32)
    for c in range(NCORE):
        b, r, idx = rows[c]
        out[b, idx] = res[c]["x_out"]
    return out
```
